# Optimizing a Trainium2 kernel written in Bass

```python
import jax, jax.numpy as jnp
from jax import lax
import numpy as np

D_MODEL = 1024
BATCH = 4
SEQ = 4096
DEPTH = 2
DEC_BATCH = 128
DEC_SEQ = 4
PAST_LEN = 8192
PAGE_SIZE = 128

WINDOW = 128
SWA_BLOCK = WINDOW
HD_A = 64
HQ_A = 8
HKV_A = 2
G_A = HQ_A // HKV_A
H_B = 4
DK_B = 64
DV_B = 128
GATE_RANK = 16
GATE_TAU = 16.0
GLA_CHUNK = 64
CONV_W = 3
D_FF = ((8 * D_MODEL // 3 + 127) // 128) * 128
N_EVEN = (DEPTH + 1) // 2
N_ODD = DEPTH // 2
EPS = 1e-6
SPLIT_EVEN = (HQ_A * HD_A, HKV_A * HD_A, HKV_A * HD_A, H_B * DK_B, H_B * DK_B, H_B * DV_B, H_B * DV_B, GATE_RANK)
D_IN_EVEN = sum(SPLIT_EVEN)
D_MIX_EVEN = HQ_A * HD_A + H_B * DV_B

kernel_name = "hybrid_swa_gla_shortconv_convffn_step"


def rmsnorm(x, g):
    x32 = x.astype(jnp.float32)
    y = x32 * lax.rsqrt(jnp.mean(x32 * x32, axis=-1, keepdims=True) + EPS) * g.astype(jnp.float32)
    return y.astype(x.dtype)


def alibi_slopes(n):
    return jnp.asarray(2.0 ** (-8.0 * np.arange(1, n + 1) / n), dtype=jnp.float32)


def causal_dwconv(x, past, w):
    L = x.shape[1]
    xp = jnp.concatenate([past.astype(x.dtype), x], axis=1)
    y = w[0] * xp[:, 0:L]
    for i in range(1, CONV_W):
        y = y + w[i] * xp[:, i:i + L]
    return y, xp[:, L:]


def sink_attention(q, k, v, dist, valid, slopes, sinks):
    s = jnp.einsum('nbqkgd,nbskd->nbkgqs', q, k).astype(jnp.float32) * (HD_A ** -0.5)
    s = s - slopes.reshape(HKV_A, G_A, 1, 1) * dist.astype(jnp.float32)
    s = jnp.where(valid[None, :, None, None], s, -jnp.inf)
    sink = jnp.broadcast_to(sinks.astype(jnp.float32).reshape(HKV_A, G_A, 1, 1), s.shape[:-1] + (1,))
    p = jax.nn.softmax(jnp.concatenate([s, sink], axis=-1), axis=-1)[..., :-1]
    return jnp.einsum('nbkgqs,nbskd->nbqkgd', p.astype(v.dtype), v)


def swa_prompt(q, k, v, slopes, sinks):
    N, S, _ = q.shape
    nb = S // SWA_BLOCK
    qb = q.reshape(N, nb, SWA_BLOCK, HKV_A, G_A, HD_A)

    def with_prev(t):
        tb = t.reshape(N, nb, SWA_BLOCK, HKV_A, HD_A)
        prev = jnp.pad(tb, ((0, 0), (1, 0), (0, 0), (0, 0), (0, 0)))[:, :-1]
        return jnp.concatenate([prev, tb], axis=2)

    qi = jnp.arange(SWA_BLOCK)[:, None]
    sj = jnp.arange(2 * SWA_BLOCK)[None, :]
    dist = SWA_BLOCK + qi - sj
    blk = jnp.arange(nb)[:, None, None]
    valid = (dist >= 0) & (dist <= WINDOW) & ((blk - 1) * SWA_BLOCK + sj >= 0)
    o = sink_attention(qb, with_prev(k), with_prev(v), dist, valid, slopes, sinks)
    return o.reshape(N, S, HQ_A * HD_A)


def swa_decode(q, k, v, buf_k, buf_v, slopes, sinks):
    N, L, _ = q.shape
    kk = jnp.concatenate([buf_k.astype(k.dtype), k], axis=1)
    vv = jnp.concatenate([buf_v.astype(v.dtype), v], axis=1)
    dist = WINDOW + jnp.arange(L)[:, None] - jnp.arange(WINDOW + L)[None, :]
    valid = ((dist >= 0) & (dist <= WINDOW))[None]
    o = sink_attention(q.reshape(N, 1, L, HKV_A, G_A, HD_A), kk[:, None], vv[:, None], dist, valid, slopes, sinks)
    return o.reshape(N, L, HQ_A * HD_A), kk[:, -WINDOW:], vv[:, -WINDOW:]


def gla_chunked(q, k, v, g, S0):
    N, L, H, dk = q.shape
    dv = v.shape[-1]
    nc = L // GLA_CHUNK
    to_chunks = lambda t: t.reshape(N, nc, GLA_CHUNK, H, t.shape[-1]).transpose(1, 0, 2, 3, 4)
    mask = jnp.tril(jnp.ones((GLA_CHUNK, GLA_CHUNK), dtype=bool))[None, :, :, None, None]

    def step(S, inp):
        qc, kc, vc, gc = inp
        b = jnp.cumsum(gc, axis=1)
        o_inter = jnp.einsum('nchd,nhde->nche', qc * jnp.exp(b), S)
        decay = jnp.exp(jnp.where(mask, b[:, :, None] - b[:, None, :], -jnp.inf))
        A = jnp.einsum('nihd,njhd,nijhd->nhij', qc, kc, decay)
        o_intra = jnp.einsum('nhij,njhe->nihe', A, vc)
        bC = b[:, -1]
        S = jnp.exp(bC)[..., None] * S + jnp.einsum('nchd,nche->nhde', kc * jnp.exp(bC[:, None] - b), vc)
        return S, o_inter + o_intra

    S, o = lax.scan(step, S0, (to_chunks(q), to_chunks(k), to_chunks(v), to_chunks(g)))
    return o.transpose(1, 0, 2, 3, 4).reshape(N, L, H, dv), S


def gla_recurrent(q, k, v, g, S0):
    def step(S, inp):
        qt, kt, vt, gt = inp
        S = jnp.exp(gt)[..., None] * S + kt[..., None] * vt[..., None, :]
        return S, jnp.einsum('nhd,nhde->nhe', qt, S)

    S, o = lax.scan(step, S0, tuple(jnp.swapaxes(t, 0, 1) for t in (q, k, v, g)))
    return jnp.swapaxes(o, 0, 1), S


def even_mixer(h, w_in, w_gate_up, b_gate, sinks, gla_norm, w_out, past):
    f32 = jnp.float32
    N, L, _ = h.shape
    idx = list(np.cumsum(SPLIT_EVEN)[:-1])
    qa, ka, va, qb, kb, vb, rb, glr = jnp.split(h @ w_in, idx, axis=-1)
    slopes = alibi_slopes(HQ_A)
    ka = ka.reshape(N, L, HKV_A, HD_A)
    va = va.reshape(N, L, HKV_A, HD_A)
    qb = qb.reshape(N, L, H_B, DK_B).astype(f32) * (DK_B ** -0.5)
    kb = kb.reshape(N, L, H_B, DK_B).astype(f32)
    vb = vb.reshape(N, L, H_B, DV_B).astype(f32)
    gb = (jax.nn.log_sigmoid((glr @ w_gate_up + b_gate).astype(f32)) / GATE_TAU).reshape(N, L, H_B, DK_B)
    if past is None:
        oa = swa_prompt(qa, ka, va, slopes, sinks)
        new_k, new_v = ka[:, -WINDOW:], va[:, -WINDOW:]
        ob, S = gla_chunked(qb, kb, vb, gb, jnp.zeros((N, H_B, DK_B, DV_B), f32))
    else:
        buf_k, buf_v, S0 = past
        oa, new_k, new_v = swa_decode(qa, ka, va, buf_k, buf_v, slopes, sinks)
        ob, S = gla_recurrent(qb, kb, vb, gb, S0.astype(f32))
    ob = ob * lax.rsqrt(jnp.mean(ob * ob, axis=-1, keepdims=True) + EPS)
    ob = ob.reshape(N, L, H_B * DV_B) * gla_norm.astype(f32) * jax.nn.silu(rb.astype(f32))
    y = jnp.concatenate([oa.astype(h.dtype), ob.astype(h.dtype)], axis=-1) @ w_out
    return y, new_k, new_v, S


def odd_mixer(h, w_in, conv_w, w_out, past):
    bg, cg, u = jnp.split(h @ w_in, 3, axis=-1)
    z, new = causal_dwconv(cg * u, past, conv_w)
    return (bg * z) @ w_out, new


def conv_ffn(h, w_up, conv_w, conv_b, w_down, past):
    u = h @ w_up
    uc, new = causal_dwconv(u, past, conv_w)
    gate, val = jnp.split(uc + conv_b, 2, axis=-1)
    return (jax.nn.gelu(gate, approximate=True) * val) @ w_down, new


def trunk(x, past, weights):
    (norm_mix_pre, norm_mix_post, norm_ffn_pre, norm_ffn_post, w_in_even, w_gate_up, b_gate, attn_sinks,
     gla_norm, w_out_even, w_in_odd, conv_w_odd, w_out_odd, ffn_up, ffn_conv_w, ffn_conv_b, ffn_down) = weights
    N = x.shape[0]
    ks, vs, gs, cs, fs = [], [], [], [], []
    for l in range(DEPTH):
        h = rmsnorm(x, norm_mix_pre[l])
        if l % 2 == 0:
            e = l // 2
            p = None if past is None else (past[0][e], past[1][e], past[2][e])
            m, nk, nv, S = even_mixer(h, w_in_even[e], w_gate_up[e], b_gate[e], attn_sinks[e], gla_norm[e],
                                      w_out_even[e], p)
            ks.append(nk)
            vs.append(nv)
            gs.append(S)
        else:
            o = l // 2
            p = jnp.zeros((N, CONV_W - 1, D_MODEL), x.dtype) if past is None else past[3][o]
            m, nc = odd_mixer(h, w_in_odd[o], conv_w_odd[o], w_out_odd[o], p)
            cs.append(nc)
        x = x + rmsnorm(m, norm_mix_post[l])
        h = rmsnorm(x, norm_ffn_pre[l])
        p = jnp.zeros((N, CONV_W - 1, 2 * D_FF), x.dtype) if past is None else past[4][l]
        f, nf = conv_ffn(h, ffn_up[l], ffn_conv_w[l], ffn_conv_b[l], ffn_down[l], p)
        fs.append(nf)
        x = x + rmsnorm(f, norm_ffn_post[l])
    return x, jnp.stack(ks), jnp.stack(vs), jnp.stack(gs), jnp.stack(cs), jnp.stack(fs)


def setup_inputs(seed: int = 0) -> dict:
    key = jax.random.key(seed)
    ks = jax.random.split(key, 24)
    f32 = jnp.float32
    D = D_MODEL
    F2 = 2 * D_FF

    def nrm(k, shape, scale):
        return jax.random.normal(k, shape, f32) * scale

    return {
        "x_prompt": nrm(ks[0], (BATCH, SEQ, D), 1.0),
        "x_sample": nrm(ks[1], (DEC_BATCH, DEC_SEQ, D), 1.0),
        "cache_swa_k": nrm(ks[2], (N_EVEN, DEC_BATCH, WINDOW, HKV_A, HD_A), 1.0),
        "cache_swa_v": nrm(ks[3], (N_EVEN, DEC_BATCH, WINDOW, HKV_A, HD_A), 1.0),
        "state_gla": nrm(ks[4], (N_EVEN, DEC_BATCH, H_B, DK_B, DV_B), 0.5),
        "state_conv": nrm(ks[5], (N_ODD, DEC_BATCH, CONV_W - 1, D), 1.0),
        "state_ffn": nrm(ks[6], (DEPTH, DEC_BATCH, CONV_W - 1, F2), 1.0),
        "norm_mix_pre": 1.0 + nrm(ks[7], (DEPTH, D), 0.05),
        "norm_mix_post": 1.0 + nrm(ks[8], (DEPTH, D), 0.05),
        "norm_ffn_pre": 1.0 + nrm(ks[9], (DEPTH, D), 0.05),
        "norm_ffn_post": 1.0 + nrm(ks[10], (DEPTH, D), 0.05),
        "w_in_even": nrm(ks[11], (N_EVEN, D, D_IN_EVEN), D ** -0.5),
        "w_gate_up": nrm(ks[12], (N_EVEN, GATE_RANK, H_B * DK_B), GATE_RANK ** -0.5),
        "b_gate": nrm(ks[13], (N_EVEN, H_B * DK_B), 0.1),
        "attn_sinks": nrm(ks[14], (N_EVEN, HQ_A), 0.5),
        "gla_norm": 1.0 + nrm(ks[15], (N_EVEN, H_B * DV_B), 0.05),
        "w_out_even": nrm(ks[16], (N_EVEN, D_MIX_EVEN, D), D_MIX_EVEN ** -0.5),
        "w_in_odd": nrm(ks[17], (N_ODD, D, 3 * D), D ** -0.5),
        "conv_w_odd": nrm(ks[18], (N_ODD, CONV_W, D), CONV_W ** -0.5),
        "w_out_odd": nrm(ks[19], (N_ODD, D, D), D ** -0.5),
        "ffn_up": nrm(ks[20], (DEPTH, D, F2), D ** -0.5),
        "ffn_conv_w": nrm(ks[21], (DEPTH, CONV_W, F2), CONV_W ** -0.5),
        "ffn_conv_b": nrm(ks[22], (DEPTH, F2), 0.02),
        "ffn_down": nrm(ks[23], (DEPTH, D_FF, D), D_FF ** -0.5),
    }


def reference(x_prompt, x_sample, cache_swa_k, cache_swa_v, state_gla, state_conv, state_ffn,
              norm_mix_pre, norm_mix_post, norm_ffn_pre, norm_ffn_post, w_in_even, w_gate_up, b_gate,
              attn_sinks, gla_norm, w_out_even, w_in_odd, conv_w_odd, w_out_odd, ffn_up, ffn_conv_w,
              ffn_conv_b, ffn_down):
    weights = (norm_mix_pre, norm_mix_post, norm_ffn_pre, norm_ffn_post, w_in_even, w_gate_up, b_gate,
               attn_sinks, gla_norm, w_out_even, w_in_odd, conv_w_odd, w_out_odd, ffn_up, ffn_conv_w,
               ffn_conv_b, ffn_down)
    y_prompt, swa_k_p, swa_v_p, gla_p, conv_p, ffn_p = trunk(x_prompt, None, weights)
    y_sample, swa_k_s, swa_v_s, gla_s, conv_s, ffn_s = trunk(
        x_sample, (cache_swa_k, cache_swa_v, state_gla, state_conv, state_ffn), weights)
    return (y_prompt, y_sample, swa_k_p, swa_v_p, gla_p, conv_p, ffn_p, swa_k_s, swa_v_s, gla_s, conv_s, ffn_s)
```

```python
import numpy as np
import ml_dtypes
import concourse.bass as bass
import concourse.mybir as mybir
from concourse.bass_utils import run_bass_kernel_spmd

F32 = mybir.dt.float32
BF16 = mybir.dt.bfloat16
AF = mybir.ActivationFunctionType
ALU = mybir.AluOpType
AX = mybir.AxisListType

D = 1024
DFF = 2816
F2 = 5632
NPRE = 15
NMAIN = 17
NSEQ = 16
EPS = 1e-6
NEG = -240000.0
NSLOT = 5
SLOT_E = 4096


class Buf:
    __slots__ = ("name", "w", "r")

    def __init__(self, name):
        self.name = name
        self.w = None
        self.r = {}


class KB:
    ENG = ("pe", "act", "dve", "pool", "sp")

    def __init__(self, nc):
        self.nc = nc
        self.E = dict(pe=nc.tensor, act=nc.scalar, dve=nc.vector, pool=nc.gpsimd, sp=nc.sync)
        self.sems = {}
        self.cnt = {}
        self.seen = {e: {} for e in self.ENG}
        for e in self.ENG:
            self.new_sem(e)
        self.n_inst = {e: 0 for e in self.ENG}

    def new_sem(self, name):
        self.sems[name] = self.nc.alloc_semaphore(name="s_" + name)
        self.cnt[name] = 0
        return name

    def sb(self, name, shape, dt):
        return self.nc.alloc_sbuf_tensor(name, list(shape), dt)

    def _wait(self, eng, ev, kind):
        if ev is None:
            return
        s, v = ev
        if s == eng and kind == "waw":
            return
        if self.seen[eng].get(s, 0) >= v:
            return
        self.E[eng].wait_ge(self.sems[s], v)
        self.seen[eng][s] = v

    def _deps(self, eng, reads, writes, force=False):
        for b in reads:
            self._wait(eng, b.w, "raw")
        for b in writes:
            self._wait(eng, b.w, "raw" if force else "waw")
            for s, v in b.r.items():
                self._wait(eng, (s, v), "raw" if force else "war")

    def _record(self, ev, reads, writes):
        s, v = ev
        for b in reads:
            if b.r.get(s, 0) < v:
                b.r[s] = v
        for b in writes:
            b.w = ev
            b.r = {}

    @staticmethod
    def _flat(bs):
        out = []
        for b in bs:
            if isinstance(b, (list, tuple)):
                out.extend(KB._flat(b))
            else:
                out.append(b)
        return out

    def op(self, eng, fn, reads=(), writes=(), signal=True):
        reads, writes = self._flat(reads), self._flat(writes)
        self._deps(eng, reads, writes)
        ins = fn(self.E[eng])
        self.n_inst[eng] += 1
        ev = (eng, self.cnt[eng] + 1)
        if signal:
            ins.then_inc(self.sems[eng], 1)
            self.cnt[eng] += 1
        self._record(ev, reads, writes)
        return ins

    def dma(self, out, in_, reads=(), writes=(), sem=None, q="sp", **kw):
        reads, writes = self._flat(reads), self._flat(writes)
        self._deps(q, reads, writes, force=True)
        ins = self.E[q].dma_start(out=out, in_=in_, **kw)
        ins.then_inc(self.sems[sem], 16)
        self.cnt[sem] += 16
        self.n_inst[q] += 1
        self._record((sem, self.cnt[sem]), reads, writes)
        return ins


def weight_blocks():
    blocks = []
    for j in range(5):
        w = min(512, 2320 - 512 * j)
        blocks.append(("in0_%d" % j, 8, w, [("w_in_even", 0, 512 * j, w, 0)]))
    for j in range(2):
        blocks.append(("out0_%d" % j, 8, 512, [("w_out_even", 0, 512 * j, 512, 0)]))
    for l in range(2):
        for j in range(11):
            blocks.append(("up%d_%d" % (l, j), 8, 512, [("ffn_up%d" % l, 0, 256 * j, 256, 0),
                                                        ("ffn_up%d" % l, 0, DFF + 256 * j, 256, 256)]))
        for m in range(8):
            blocks.append(("dn%d_%d" % (l, m), 22, 128, [("ffn_down%d" % l, 0, 128 * m, 128, 0)]))
    chunks = []
    for c in range(8):
        chunks += [1024 + 128 * c, 2048 + 128 * c, 128 * c]
    for j in range(6):
        blocks.append(("in1_%d" % j, 8, 512, [("w_in_odd", 0, chunks[4 * j + q], 128, 128 * q) for q in range(4)]))
    for j in range(2):
        blocks.append(("out1_%d" % j, 8, 512, [("w_out_odd", 0, 512 * j, 512, 0)]))
    return blocks


def unit_block_order(mode):
    if mode == "pre":
        return ["in0_%d" % j for j in range(1, 5)]
    o = ["in0_%d" % j for j in range(5)] + ["out0_%d" % j for j in range(2)]
    o += ["up0_%d" % j for j in range(11)] + ["dn0_%d" % m for m in range(8)]
    o += ["in1_%d" % j for j in range(6)] + ["out1_%d" % j for j in range(2)]
    o += ["up1_%d" % j for j in range(11)] + ["dn1_%d" % m for m in range(8)]
    return o


def pc_norm(q, l, kc):
    return (q * 2 + l) * 8 + kc
PC_FW = 64
PC_FB = 328
PC_CW = 416
PC_GN = 440


def build_program(with_sample=True):
    nc = bass.Bass("TRN2", target_bir_lowering=False)
    k = KB(nc)
    Dm = {}

    def din(name, shape, dt=F32):
        Dm[name] = nc.dram_tensor(name, list(shape), dt, kind="ExternalInput").ap()
        return Dm[name]

    def dout(name, shape):
        Dm[name] = nc.dram_tensor(name, list(shape), F32, kind="ExternalOutput").ap()
        return Dm[name]

    xin = din("xin", [(NPRE + NMAIN) * 128, D])
    flagv = din("flagv", [128, 2])
    pvec = din("pvec", [512, 128])
    c_ident = din("c_ident", [128, 128])
    c_ones = din("c_ones", [128, 128], BF16)
    c_ucum = din("c_ucum", [128, 128])
    c_urev = din("c_urev", [128, 128])
    c_maska = din("c_maska", [128, 128])
    c_bias = din("c_bias", [128, 4, 512])
    din("w_in_even", [D, 2320]); din("w_out_even", [D, D]); din("w_in_odd", [D, 3 * D]); din("w_out_odd", [D, D])
    for l in range(2):
        din("ffn_up%d" % l, [D, F2]); din("ffn_down%d" % l, [DFF, D])
    w_gate_up = din("w_gate_up", [16, 256]); b_gate = din("b_gate", [1, 256]); attn_sinks = din("attn_sinks", [1, 8])
    xs = din("xs", [64, D]); sck = din("sck", [16, 128, 128]); scv = din("scv", [16, 128, 128])
    sgl = din("sgl", [16, 4, 64, 128]); scs = din("scs", [32, D]); sfs = din("sfs", [2, 32, F2])
    c_ucum_s = din("c_ucum_s", [64, 64]); c_urev_s = din("c_urev_s", [64, 64]); c_maska_s = din("c_maska_s", [64, 64])
    c_rowmask = din("c_rowmask", [64, 16]); c_bias_s = din("c_bias_s", [128, 2, 512])
    ys_out = dout("ys_out", [64, D]); ks_out = dout("ks_out", [16, 128, 128]); vs_out = dout("vs_out", [16, 128, 128])
    gs_out = dout("gs_out", [16, 4, 64, 128]); cs_out = dout("cs_out", [32, D]); fs_out = dout("fs_out", [2, 32, F2])
    y_out = dout("y_out", [16 * 128, D])
    k_out = dout("k_out", [128, 128]); v_out = dout("v_out", [128, 128])
    g_out = dout("g_out", [4, 64, 128])
    c_out = dout("c_out", [2, D])
    f_out = dout("f_out", [2, 2, F2])

    for s in ("d_cast", "d_const", "d_pv", "d_sink", "d_stg0", "d_stg1", "d_misc", "d_rows", "d_s1", "d_s2", "d_s3", "d_s4", "d_s5", "d_s6", "d_so", "d_mk", "d_mv", "d_mg", "d_res"):
        k.new_sem(s)
    for i in range(NSLOT):
        k.new_sem("d_w%d" % i)

    blocks = {b[0]: b for b in weight_blocks()}
    scr = {}
    for src in ("w_in_even", "w_out_even", "ffn_up0", "ffn_down0", "w_in_odd", "w_out_odd", "ffn_up1", "ffn_down1"):
        R, C = Dm[src].shape
        t = nc.dram_tensor("scr_" + src, [R, C], BF16, kind="Internal").ap()
        b = Buf("scr_" + src)
        scr[src] = (t, b)
        k.new_sem("d_c_" + src)

    def emit_casts(names):
        for src in names:
            t, b = scr[src]
            R, C = Dm[src].shape
            step = 512
            for r0 in range(0, R, step):
                r1 = min(R, r0 + step)
                k.dma(t[r0:r1, :], Dm[src][r0:r1, :], writes=[b], sem="d_c_" + src, q="pool")

    emit_casts(["w_in_even"])

    ring = [k.sb("ring%d" % i, [128, SLOT_E], BF16) for i in range(NSLOT)]
    ringB = [Buf("ring%d" % i) for i in range(NSLOT)]
    xT = k.sb("xT", [128, 8, 512], F32); BxT = [Buf("xT%d" % i) for i in range(8)]
    hT = k.sb("hT", [128, 8, 512], BF16); BhT = [Buf("hT%d" % i) for i in range(8)]
    mT = k.sb("mT", [128, 8, 512], F32); BmT = [Buf("mT%d" % i) for i in range(8)]
    sq = [k.sb("sq%d" % i, [128, 512], BF16) for i in range(2)]; Bsq = [Buf("sq%d" % i) for i in range(2)]
    rstd = k.sb("rstd", [128, 512], F32); Brstd = Buf("rstd")
    tmpn = [k.sb("tmpn%d" % i, [128, 512], F32) for i in range(2)]; Btmpn = [Buf("tmpn%d" % i) for i in range(2)]
    stg = [k.sb("stg%d" % i, [128, 1024], F32) for i in range(2)]; Bstg = [Buf("stg%d" % i) for i in range(2)]
    qaT = k.sb("qaT", [128, 4, 512], BF16); BqaT = Buf("qaT")
    kaT = k.sb("kaT", [128, 640], BF16); BkaT = Buf("kaT")
    kaT2 = k.sb("kaT2", [128, 640], BF16); BkaT2 = Buf("kaT2")
    vtok = k.sb("vtok", [128, 5, 128], BF16); Bvtok = Buf("vtok")
    qbT = k.sb("qbT", [128, 2, 512], F32); BqbT = Buf("qbT")
    kbT = k.sb("kbT", [128, 2, 512], F32); BkbT = Buf("kbT")
    srbT = k.sb("srbT", [128, 4, 512], F32); BsrbT = Buf("srbT")
    glrT = k.sb("glrT", [32, 512], BF16); BglrT = Buf("glrT")
    mixT = k.sb("mixT", [128, 8, 512], BF16); BmixT = Buf("mixT")
    gated = k.sb("gated", [128, 22, 512], BF16); Bgated = Buf("gated")
    tg = [k.sb("tg%d" % i, [128, 512], F32) for i in range(2)]; Btg = [Buf("tg%d" % i) for i in range(2)]
    tv = [k.sb("tv%d" % i, [128, 512], F32) for i in range(2)]; Btv = [Buf("tv%d" % i) for i in range(2)]
    l_sb = k.sb("l_sb", [128, 256], F32); Bl = Buf("l_sb")
    er_sb = k.sb("er_sb", [128, 256], F32); Ber = Buf("er")
    e1_sb = k.sb("e1_sb", [128, 256], F32); Be1 = Buf("e1")
    e2_sb = k.sb("e2_sb", [128, 256], F32); Be2 = Buf("e2")
    qtl = k.sb("qtl", [128, 2, 128], BF16); Bqtl = Buf("qtl")
    ktl = k.sb("ktl", [128, 2, 128], BF16); Bktl = Buf("ktl")
    khat = k.sb("khat", [128, 256], BF16); Bkhat = Buf("khat")
    vbtok = k.sb("vbtok", [128, 512], BF16); Bvbtok = Buf("vbtok")
    atm = [k.sb("atm%d" % i, [128, 128], BF16) for i in range(2)]; Batm = [Buf("atm%d" % i) for i in range(2)]
    S_f = k.sb("S_f", [128, 2, 128], F32); BS = Buf("S_f")
    S_b = k.sb("S_b", [128, 4, 2, 128], BF16); BSb = [[Buf("Sb%d%d" % (p, i)) for i in range(2)] for p in range(2)]
    osq = k.sb("osq", [128, 512], BF16); Bosq = Buf("osq")
    orstd = k.sb("orstd", [128, 512], F32); Borstd = Buf("orstd")
    otmp = k.sb("otmp", [128, 512], F32); Botmp = Buf("otmp")
    kf32 = k.sb("kf32", [128, 128], F32); Bkf32 = Buf("kf32")
    sc = [k.sb("sc%d" % i, [128, 512], F32) for i in range(2)]; Bsc = [Buf("sc%d" % i) for i in range(2)]
    ptT = k.sb("ptT", [128, 4, 512], BF16); Bpt = [Buf("pt%d" % i) for i in range(4)]
    rec, Brec = sc[0], Bsc[0]
    ident = k.sb("ident", [128, 128], F32); ones = k.sb("ones", [128, 128], BF16)
    ucum = k.sb("ucum", [128, 128], F32); urev = k.sb("urev", [128, 128], F32); maska = k.sb("maska", [128, 128], F32)
    biasT = k.sb("biasT", [128, 4, 512], F32)
    PT = k.sb("PT", [128, 512], F32)
    pv_in = k.sb("pv_in", [128, 4, 128], F32)
    flg = k.sb("flg", [128, 2], F32)
    epsD = k.sb("epsD", [128, 4], F32)
    wg = k.sb("wg", [32, 256], BF16)
    sinkE = k.sb("sinkE", [128, 4], F32)
    Bc = Buf("consts"); BPT = Buf("PT"); Bpv = Buf("pv_in"); Bwg = Buf("wg"); BsinkE = Buf("sinkE")
    fcar = k.sb("fcar", [128, 2, 44, 2], F32); Bfcar = Buf("fcar")
    ccar = k.sb("ccar", [128, 8, 2], F32); Bccar = Buf("ccar")
    ucum_s = k.sb("ucum_s", [64, 64], F32); urev_s = k.sb("urev_s", [64, 64], F32); maska_s = k.sb("maska_s", [64, 64], F32)
    rowmask = k.sb("rowmask", [64, 16], F32)
    fcs = k.sb("fcs", [128, 44, 16, 2], F32); Bfcs = Buf("fcs")
    ccs = k.sb("ccs", [128, 8, 16, 2], F32); Bccs = Buf("ccs")
    cu, Bcu = tg, Btg
    ucp, Bucp = tv, Btv
    zt, Bzt = tmpn, Btmpn
    osm = k.sb("osm", [128, 256], F32); Bosm = Buf("osm")
    rowst = mT[0:2, 0:3, :].rearrange("p a b -> p (a b)")

    PS = [nc.alloc_psum_tensor("ps%d" % i, [128, 512], F32) for i in range(8)]
    BPS = [Buf("ps%d" % i) for i in range(8)]
    pfree = list(range(8))

    def palloc():
        assert pfree, "out of PSUM banks"
        return pfree.pop(0)

    def pfree_(i):
        pfree.append(i)

    k.dma(ident[:, :], c_ident[:, :], writes=[Bc], sem="d_const")
    k.dma(ones[:, :], c_ones[:, :], writes=[Bc], sem="d_const")
    k.dma(ucum[:, :], c_ucum[:, :], writes=[Bc], sem="d_const")
    k.dma(urev[:, :], c_urev[:, :], writes=[Bc], sem="d_const")
    k.dma(maska[:, :], c_maska[:, :], writes=[Bc], sem="d_const")
    k.dma(biasT[:, :, :], c_bias[:, :, :], writes=[Bc], sem="d_const")
    k.dma(flg[:, :], flagv[:, :], writes=[Bc], sem="d_const")
    k.dma(ucum_s[:, :], c_ucum_s[:, :], writes=[Bc], sem="d_const")
    k.dma(urev_s[:, :], c_urev_s[:, :], writes=[Bc], sem="d_const")
    k.dma(maska_s[:, :], c_maska_s[:, :], writes=[Bc], sem="d_const")
    k.dma(rowmask[:, :], c_rowmask[:, :], writes=[Bc], sem="d_const")
    k.dma(pv_in[:, :, :], pvec.rearrange("(a p) c -> p a c", p=128), writes=[Bpv], sem="d_pv")
    k.op("pool", lambda e: e.memset(wg[:, :], 0.0), writes=[Bwg])
    k.dma(wg[0:16, :], w_gate_up[:, :], writes=[Bwg], sem="d_cast", q="pool")
    k.dma(wg[16:17, :], b_gate[:, :], writes=[Bwg], sem="d_cast", q="pool")
    sink1 = k.sb("sink1", [1, 8], F32); onesf = k.sb("onesf", [1, 128], F32); sink8 = k.sb("sink8", [128, 8], F32)
    Bs1 = Buf("sink1"); Bs8 = Buf("sink8")
    k.dma(sink1[:, :], attn_sinks[:, :], writes=[Bs1], sem="d_sink")
    k.op("pool", lambda e: e.memset(onesf[:, :], 1.0), writes=[Bs1])
    pb = palloc()
    k.op("pe", lambda e: e.matmul(PS[pb][:, 0:8], lhsT=onesf[0:1, :], rhs=sink1[0:1, :], start=True, stop=True),
         reads=[Bs1], writes=[BPS[pb]])
    k.op("act", lambda e: e.activation(out=sink8[:, :], in_=PS[pb][:, 0:8], func=AF.Exp), reads=[BPS[pb]], writes=[Bs8])
    pfree_(pb)
    k.op("dve", lambda e: e.tensor_copy(out=sinkE[0:64, :], in_=sink8[0:64, 0:8:2]), reads=[Bs8], writes=[BsinkE])
    k.op("dve", lambda e: e.tensor_copy(out=sinkE[64:128, :], in_=sink8[64:128, 1:8:2]), reads=[Bs8], writes=[BsinkE])
    pb = palloc()
    for a in range(4):
        k.op("pe", lambda e: e.transpose(out=PS[pb][:, a * 128:(a + 1) * 128], in_=pv_in[:, a, :], identity=ident[:, :]),
             reads=[Bpv, Bc], writes=[BPS[pb]])
    k.op("act", lambda e: e.copy(out=PT[:, :], in_=PS[pb][:, :]), reads=[BPS[pb]], writes=[BPT])
    pfree_(pb)
    k.op("act", lambda e: e.mul(out=PT[:, 0:64], in_=PT[:, 0:64], mul=32.0), reads=[BPT], writes=[BPT])
    k.op("act", lambda e: e.mul(out=PT[:, PC_GN:PC_GN + 4], in_=PT[:, PC_GN:PC_GN + 4], mul=float(np.sqrt(128.0))),
         reads=[BPT], writes=[BPT])
    k.op("pool", lambda e: e.memset(glrT[:, :], 1.0), writes=[BglrT])
    k.op("pool", lambda e: e.memset(epsD[:, 0:1], float(D * EPS)), writes=[Bc])
    k.op("pool", lambda e: e.memset(epsD[:, 1:2], float(128 * EPS)), writes=[Bc])
    k.op("pool", lambda e: e.memset(epsD[:, 2:3], 1.0), writes=[Bc])
    k.op("pool", lambda e: e.memset(epsD[:, 3:4], 0.0), writes=[Bc])
    k.op("pool", lambda e: e.memset(S_f[:, :, :], 0.0), writes=[BS])
    k.op("pool", lambda e: e.memset(S_b[:, :, :, :], 0.0), writes=[BSb[0][0], BSb[0][1], BSb[1][0], BSb[1][1]])
    k.op("pool", lambda e: e.memset(fcar[:, :, :, :], 0.0), writes=[Bfcar])
    k.op("pool", lambda e: e.memset(ccar[:, :, :], 0.0), writes=[Bccar])
    k.op("pool", lambda e: e.memset(kaT[:, :], 0.0), writes=[BkaT])
    k.op("pool", lambda e: e.memset(kaT2[:, :], 0.0), writes=[BkaT2])
    k.op("pool", lambda e: e.memset(vtok[:, :, :], 0.0), writes=[Bvtok])
    emit_casts(["w_out_even", "ffn_up0", "ffn_down0", "w_in_odd", "w_out_odd", "ffn_up1", "ffn_down1"])

    units = []
    for i in range(0, NPRE, 4):
        units.append(("pre", list(range(i, min(i + 4, NPRE)))))
    main_split = [[0], [1, 2, 3, 4], [5, 6, 7, 8], [9, 10, 11, 12], [13, 14, 15, 16]]
    for ts in main_split:
        units.append(("main", ts))
    plan = []
    for mode, ts in units:
        if mode != "pre":
            plan += unit_block_order(mode)
    if with_sample:
        plan += unit_block_order("main")
    wstate = dict(next_issue=0, next_use=0)

    def w_issue():
        u = wstate["next_issue"]
        if u >= len(plan):
            return
        name, KC, W, pieces = blocks[plan[u]]
        s = u % NSLOT
        dst = ring[s][:, 0:KC * W].rearrange("p (kc w) -> p kc w", kc=KC)
        for (src, r0, c0, w, off) in pieces:
            t, b = scr[src]
            k.dma(dst[:, :, off:off + w], t[r0:r0 + KC * 128, c0:c0 + w].rearrange("(kc p) w -> p kc w", p=128),
                  reads=[b], writes=[ringB[s]], sem="d_w%d" % s)
        wstate["next_issue"] += 1

    def w_get(name):
        u = wstate["next_use"]
        assert plan[u] == name, (plan[u], name)
        while wstate["next_issue"] <= u:
            w_issue()
        s = u % NSLOT
        _, KC, W, _ = blocks[name]
        return ring[s][:, 0:KC * W].rearrange("p (kc w) -> p kc w", kc=KC), ringB[s]

    gflat = gated[:, :, :].rearrange("p a b -> p (a b)")
    RES = {"in0_1": (gflat[:, 0:2048].rearrange("p (kc w) -> p kc w", kc=8), 512, 256),
           "in0_2": (gflat[:, 2048:6144].rearrange("p (kc w) -> p kc w", kc=8), 1024, 512),
           "in0_3": (gflat[:, 6144:8192].rearrange("p (kc w) -> p kc w", kc=8), 1536, 256),
           "in0_4": (gflat[:, 8192:10368].rearrange("p (kc w) -> p kc w", kc=8), 2048, 272)}

    def load_resident():
        t, b = scr["w_in_even"]
        for name, (view, c0, w) in RES.items():
            k.dma(view, t[0:1024, c0:c0 + w].rearrange("(kc p) w -> p kc w", p=128), reads=[b], writes=[Bgated], sem="d_res")

    def w_get_ahead(name, ahead):
        wstate["next_use"] += ahead
        r = w_get(name)
        wstate["next_use"] -= ahead
        return r

    def w_done():
        wstate["next_use"] += 1
        while wstate["next_issue"] < min(len(plan), wstate["next_use"] + NSLOT):
            w_issue()

    for _ in range(NSLOT):
        w_issue()

    def mm(out, lhsT, rhs, start, stop, reads, writes, signal=None, skip=False):
        if signal is None:
            signal = stop
        if skip:
            k.op("pe", lambda e: e.matmul(out, lhsT=lhsT, rhs=rhs, start=start, stop=stop, skip_group_check=True),
                 reads=reads, writes=writes, signal=signal)
        else:
            k.op("pe", lambda e: e.matmul(out, lhsT=lhsT, rhs=rhs, start=start, stop=stop), reads=reads, writes=writes,
                 signal=signal)

    cnt = dict(sq=0, tmpn=0, stg=0, sc=0, atm=0, tg=0, cu=0)

    def rot(name, n):
        i = cnt[name] % n
        cnt[name] += 1
        return i

    def rms_stats(srcs, N, src_bufs, eps_scaled):
        pb = palloc()
        for c in range(8):
            i = rot("sq", 2)
            k.op("act", lambda e: e.activation(out=sq[i][:, 0:N], in_=srcs[c], func=AF.Square), reads=[src_bufs[c]],
                 writes=[Bsq[i]])
            mm(PS[pb][:, 0:N], ones[:, :], sq[i][:, 0:N], c == 0, c == 7, [Bsq[i], Bc], [BPS[pb]], signal=True)
        k.op("act", lambda e: e.activation(out=rstd[:, 0:N], in_=PS[pb][:, 0:N], func=AF.Ln, bias=epsD[:, 0:1]),
             reads=[BPS[pb], Bc], writes=[Brstd])
        k.op("act", lambda e: e.activation(out=rstd[:, 0:N], in_=rstd[:, 0:N], func=AF.Exp, scale=-0.5), reads=[Brstd], writes=[Brstd])
        pfree_(pb)

    def pre_norm(q, l, N):
        rms_stats([xT[:, c, 0:N] for c in range(8)], N, BxT, D * EPS)
        for c in range(8):
            col = pc_norm(q, l, c)
            k.op("dve", lambda e: e.scalar_tensor_tensor(out=hT[:, c, 0:N], in0=xT[:, c, 0:N], scalar=PT[:, col:col + 1],
                                                         in1=rstd[:, 0:N], op0=ALU.mult, op1=ALU.mult),
                 reads=[BxT[c], BPT, Brstd], writes=[BhT[c]])

    def post_norm_add(q, l, N):
        rms_stats([mT[:, c, 0:N] for c in range(8)], N, BmT, D * EPS)
        ck(142)
        for c in range(8):
            col = pc_norm(q, l, c)
            i = rot("tmpn", 2)
            if c == 1:
                ck(144)
            k.op("dve", lambda e: e.scalar_tensor_tensor(out=tmpn[i][:, 0:N], in0=mT[:, c, 0:N], scalar=PT[:, col:col + 1],
                                                         in1=rstd[:, 0:N], op0=ALU.mult, op1=ALU.mult),
                 reads=[BmT[c], BPT, Brstd], writes=[Btmpn[i]])
            if c == 0:
                ck(143)
            k.op("pool", lambda e: e.tensor_tensor(out=xT[:, c, 0:N], in0=xT[:, c, 0:N], in1=tmpn[i][:, 0:N], op=ALU.add),
                 reads=[Btmpn[i], BxT[c]], writes=[BxT[c]])

    def proj_out(wnames, rhs_tile, rhs_buf, KC, N, per_block):
        m = 0
        for wn in wnames:
            wv, wb = w_get(wn)
            for j in range(per_block):
                pb = palloc()
                for kc in range(KC):
                    mm(PS[pb][:, 0:N], wv[:, kc, j * 128:(j + 1) * 128], rhs_tile[:, kc, 0:N], kc == 0, kc == KC - 1,
                       [wb, rhs_buf], [BPS[pb]])
                k.op("act", lambda e: e.copy(out=mT[:, m, 0:N], in_=PS[pb][:, 0:N]), reads=[BPS[pb]], writes=[BmT[m]])
                pfree_(pb)
                m += 1
            w_done()

    def conv_taps(t_ap, tB, src, srcB, car, carB, w0, w1, w2, bias, N, seg=False, after_act=None):
        if seg:
            V = lambda ap, a, b: ap[:, 0:64].rearrange("p (s i) -> p s i", i=4)[:, :, a:b]
            C = lambda a, b: car[:, :, a:b]
            L = 4
        else:
            V = lambda ap, a, b: ap[:, a:b]
            C = lambda a, b: car[:, a:b]
            L = N
        k.op("act", lambda e: e.activation(out=t_ap[:, 0:N], in_=src[:, 0:N], func=AF.Identity, scale=w2,
                                           bias=(epsD[:, 3:4] if bias is None else bias)), reads=[srcB, BPT, Bc], writes=[tB])
        if after_act is not None:
            after_act()
        k.op("dve", lambda e: e.scalar_tensor_tensor(out=V(t_ap, 1, L), in0=V(src, 0, L - 1), scalar=w1, in1=V(t_ap, 1, L),
                                                     op0=ALU.mult, op1=ALU.add), reads=[srcB, BPT, tB], writes=[tB])
        k.op("dve", lambda e: e.scalar_tensor_tensor(out=V(t_ap, 2, L), in0=V(src, 0, L - 2), scalar=w0, in1=V(t_ap, 2, L),
                                                     op0=ALU.mult, op1=ALU.add), reads=[srcB, BPT, tB], writes=[tB])
        k.op("dve", lambda e: e.scalar_tensor_tensor(out=V(t_ap, 0, 1), in0=C(1, 2), scalar=w1, in1=V(t_ap, 0, 1),
                                                     op0=ALU.mult, op1=ALU.add), reads=[carB, BPT, tB], writes=[tB])
        k.op("dve", lambda e: e.scalar_tensor_tensor(out=V(t_ap, 0, 2), in0=C(0, 2), scalar=w0, in1=V(t_ap, 0, 2),
                                                     op0=ALU.mult, op1=ALU.add), reads=[carB, BPT, tB], writes=[tB])
        k.op("dve", lambda e: e.tensor_copy(out=C(0, 2), in_=V(src, L - 2, L)), reads=[srcB, tB], writes=[carB])

    def load_x(tiles_abs, N):
        for ti, ta in enumerate(tiles_abs):
            i = rot("stg", 2)
            k.dma(stg[i][:, :], xin[ta * 128:(ta + 1) * 128, :], writes=[Bstg[i]], sem="d_stg%d" % i)
            for half in range(2):
                pb = palloc()
                for c4 in range(4):
                    c = half * 4 + c4
                    k.op("pe", lambda e: e.transpose(out=PS[pb][:, c4 * 128:(c4 + 1) * 128], in_=stg[i][:, c * 128:(c + 1) * 128],
                                                     identity=ident[:, :]), reads=[Bstg[i], Bc], writes=[BPS[pb]])
                k.op("act", lambda e: e.copy(out=xT[:, half * 4:(half + 1) * 4, ti * 128:(ti + 1) * 128],
                                             in_=PS[pb][:, :].rearrange("p (a b) -> p a b", a=4)),
                     reads=[BPS[pb]], writes=BxT[half * 4:(half + 1) * 4])
                pfree_(pb)

    def store_y(tiles_real, N):
        for ti, to in tiles_real:
            i = rot("stg", 2)
            for half in range(2):
                pb = palloc()
                for c4 in range(4):
                    c = half * 4 + c4
                    k.op("pe", lambda e: e.transpose(out=PS[pb][:, c4 * 128:(c4 + 1) * 128], in_=xT[:, c, ti * 128:(ti + 1) * 128],
                                                     identity=ident[:, :]), reads=[BxT[c], Bc], writes=[BPS[pb]])
                k.op("act", lambda e: e.copy(out=stg[i][:, half * 512:(half + 1) * 512], in_=PS[pb][:, :]),
                     reads=[BPS[pb]], writes=[Bstg[i]])
                pfree_(pb)
            k.dma(y_out[to * 128:(to + 1) * 128, :], stg[i][:, :], reads=[Bstg[i]], sem="d_stg%d" % i)

    def fm_chunk(wv, wb, c0, M, N, evac):
        pb = palloc()
        for kc in range(8):
            mm(PS[pb][0:M, 0:N], wv[:, kc, c0:c0 + M], hT[:, kc, 0:N], kc == 0, kc == 7, [wb, BhT[kc]], [BPS[pb]])
        evac(pb)
        pfree_(pb)

    def even_mixer(mode, tiles, N, gtiles):
        nt = len(tiles)
        pre = mode == "pre"
        if not pre:
            wv, wb = w_get("in0_0")
            for c in range(4):
                fm_chunk(wv, wb, c * 128, 128, N, lambda pb: k.op(
                    "act", lambda e: e.copy(out=qaT[:, c, 0:N], in_=PS[pb][:, 0:N]), reads=[BPS[pb]], writes=[BqaT]))
            w_done()
        wv, wb = (RES["in0_1"][0], Bgated) if pre else w_get("in0_1")
        def ka_evac(pb):
            k.op("act", lambda e: e.copy(out=kaT[:, 128:128 + N], in_=PS[pb][:, 0:N]), reads=[BPS[pb]], writes=[BkaT])
            if (not pre) and gtiles[-1] == NMAIN - 1:
                k.op("act", lambda e: e.copy(out=kf32[:, :], in_=PS[pb][:, N - 128:N]), reads=[BPS[pb]], writes=[Bkf32])
        fm_chunk(wv, wb, 0, 128, N, ka_evac)
        pb = palloc()
        for kc in range(8):
            mm(PS[pb][64:128, 0:N], wv[:, kc, 0:64], hT[:, kc, 0:N], kc == 0, kc == 7, [wb, BhT[kc]], [BPS[pb]], signal=False)
        for kc in range(8):
            mm(PS[pb][0:64, 0:N], wv[:, kc, 64:128], hT[:, kc, 0:N], kc == 0, kc == 7, [wb, BhT[kc]], [BPS[pb]])
        k.op("act", lambda e: e.copy(out=kaT2[:, 128:128 + N], in_=PS[pb][:, 0:N]), reads=[BPS[pb]], writes=[BkaT2])
        pfree_(pb)
        for ti in range(nt):
            pb = palloc()
            for kc in range(8):
                mm(PS[pb][:, 0:128], hT[:, kc, ti * 128:(ti + 1) * 128], wv[:, kc, 128:256], kc == 0, kc == 7, [wb, BhT[kc]], [BPS[pb]])
            k.op("act", lambda e: e.copy(out=vtok[:, 1 + ti, :], in_=PS[pb][:, 0:128]), reads=[BPS[pb]], writes=[Bvtok])
            if (not pre) and gtiles[ti] == NMAIN - 1:
                k.op("act", lambda e: e.copy(out=osm[:, 128:256], in_=PS[pb][:, 0:128]), reads=[BPS[pb]], writes=[Bosm])
            pfree_(pb)
        if not pre:
            for p in range(2):
                fm_chunk(wv, wb, 256 + p * 128, 128, N, lambda pb: k.op(
                    "act", lambda e: e.copy(out=qbT[:, p, 0:N], in_=PS[pb][:, 0:N]), reads=[BPS[pb]], writes=[BqbT]))
        if not pre:
            w_done()
        ck(3)
        if pre:
            wv2, wb2 = RES["in0_2"][0], Bgated
            wv3, wb3 = RES["in0_3"][0], Bgated
            wv4, wb4 = RES["in0_4"][0], Bgated
        else:
            wv2, wb2 = w_get("in0_2")
            for p in range(2):
                fm_chunk(wv2, wb2, p * 128, 128, N, lambda pb: k.op(
                    "act", lambda e: e.copy(out=kbT[:, p, 0:N], in_=PS[pb][:, 0:N]), reads=[BPS[pb]], writes=[BkbT]))
            wv3, wb3 = w_get_ahead("in0_3", 1)
            wv4, wb4 = w_get_ahead("in0_4", 2)
        if not pre:
            for h in range(4):
                src_v, src_b, c0 = (wv3, wb3, 256 + h * 128) if h < 2 else (wv4, wb4, (h - 2) * 128)
                fm_chunk(src_v, src_b, c0, 128, N, lambda pb: k.op(
                    "act", lambda e: e.activation(out=srbT[:, h, 0:N], in_=PS[pb][:, 0:N], func=AF.Silu),
                    reads=[BPS[pb]], writes=[BsrbT]))
        fm_chunk(wv4, wb4, 256, 16, N, lambda pb: k.op(
            "act", lambda e: e.copy(out=glrT[0:16, 0:N], in_=PS[pb][0:16, 0:N]), reads=[BPS[pb]], writes=[BglrT]))

        ck(4)
        for ti in range(nt):
            c0 = ti * 128
            gt = gtiles[ti]
            if not pre:
                for kv in range(2):
                    pscs = [palloc(), palloc()]
                    for kb_ in range(2):
                        kc0 = c0 + kb_ * 128
                        for par in range(2):
                            ksrc, kB = (kaT, BkaT) if kv == par else (kaT2, BkaT2)
                            mm(PS[pscs[par]][:, kb_ * 256:(kb_ + 1) * 256], ksrc[par * 64:(par + 1) * 64, kc0:kc0 + 128],
                               qaT[par * 64:(par + 1) * 64, 2 * kv:2 * kv + 2, c0:c0 + 128], True, True, [kB, BqaT], [BPS[pscs[par]]])
                    for par in range(2):
                        i = rot("sc", 2)
                        bi = kv * 2 + par
                        psc = pscs[par]
                        if gt == 1:
                            k.op("dve", lambda e: e.scalar_tensor_tensor(out=sc[i][:, 0:256], in0=PS[psc][:, 0:256], scalar=flg[:, 1:2],
                                                                         in1=biasT[:, bi, 0:256], op0=ALU.add, op1=ALU.add),
                                 reads=[BPS[psc], Bc], writes=[Bsc[i]])
                            k.op("dve", lambda e: e.tensor_tensor(out=sc[i][:, 256:512], in0=PS[psc][:, 256:512], in1=biasT[:, bi, 256:512],
                                                                  op=ALU.add), reads=[BPS[psc], Bc], writes=[Bsc[i]])
                        else:
                            k.op("dve", lambda e: e.tensor_tensor(out=sc[i][:, :], in0=PS[psc][:, :], in1=biasT[:, bi, :], op=ALU.add),
                                 reads=[BPS[psc], Bc], writes=[Bsc[i]])
                        pfree_(psc)
                        k.op("act", lambda e: e.activation(out=ptT[:, bi, :], in_=sc[i][:, :], func=AF.Exp, scale=0.125),
                             reads=[Bsc[i]], writes=[Bpt[bi]])
            if ti == 1:
                ck(7)
            pg = palloc()
            mm(PS[pg][:, 0:256], glrT[0:32, c0:c0 + 128], wg[0:32, :], True, True, [BglrT, Bwg], [BPS[pg]])
            ck(51)
            k.op("act", lambda e: e.activation(out=l_sb[:, :], in_=PS[pg][:, 0:256], func=AF.Exp, scale=-1.0),
                 reads=[BPS[pg]], writes=[Bl])
            pfree_(pg)
            ck(52)
            k.op("act", lambda e: e.activation(out=l_sb[:, :], in_=l_sb[:, :], func=AF.Ln, bias=epsD[:, 2:3]), reads=[Bl, Bc], writes=[Bl])
            ck(5)
            pr = palloc()
            mm(PS[pr][:, 0:256], urev[:, :], l_sb[:, :], True, True, [Bc, Bl], [BPS[pr]])
            k.op("act", lambda e: e.activation(out=er_sb[:, :], in_=PS[pr][:, 0:256], func=AF.Exp), reads=[BPS[pr]], writes=[Ber])
            pfree_(pr)
            pk = palloc()
            for kc in range(8):
                mm(PS[pk][:, 0:256], hT[:, kc, c0:c0 + 128], wv2[:, kc, 0:256], kc == 0, kc == 7, [wb2, BhT[kc]], [BPS[pk]])
            k.op("dve", lambda e: e.tensor_tensor(out=khat[:, :], in0=PS[pk][:, 0:256], in1=er_sb[:, :], op=ALU.mult),
                 reads=[BPS[pk], Ber], writes=[Bkhat])
            pfree_(pk)
            pv = palloc()
            for kc in range(8):
                mm(PS[pv][:, 0:256], hT[:, kc, c0:c0 + 128], wv2[:, kc, 256:512], kc == 0, kc == 7, [wb2, BhT[kc]], [BPS[pv]], signal=False)
            for kc in range(8):
                mm(PS[pv][:, 256:512], hT[:, kc, c0:c0 + 128], wv3[:, kc, 0:256], kc == 0, kc == 7, [wb3, BhT[kc]], [BPS[pv]])
            k.op("act", lambda e: e.copy(out=vbtok[:, :], in_=PS[pv][:, :]), reads=[BPS[pv]], writes=[Bvbtok])
            pfree_(pv)
            ck(6)
            pbk = palloc()
            for p in range(2):
                mm(PS[pbk][:, p * 128:(p + 1) * 128], l_sb[:, p * 128:(p + 1) * 128], ucum[:, :], True, True, [Bl, Bc], [BPS[pbk]])
            k.op("act", lambda e: e.activation(out=e1_sb[:, :], in_=PS[pbk][:, 0:256], func=AF.Exp), reads=[BPS[pbk]], writes=[Be1])
            if not pre:
                k.op("act", lambda e: e.activation(out=e2_sb[:, :], in_=PS[pbk][:, 0:256], func=AF.Exp, scale=-1.0),
                     reads=[BPS[pbk]], writes=[Be2])
            pfree_(pbk)
            if not pre:
                k.op("dve", lambda e: e.scalar_tensor_tensor(
                    out=qtl[:, :, :], in0=qbT[:, :, c0:c0 + 128], scalar=0.125,
                    in1=e1_sb[:, :].rearrange("p (a b) -> p a b", a=2), op0=ALU.mult, op1=ALU.mult),
                    reads=[BqbT, Be1], writes=[Bqtl])
                k.op("dve", lambda e: e.tensor_tensor(
                    out=ktl[:, :, :], in0=kbT[:, :, c0:c0 + 128], in1=e2_sb[:, :].rearrange("p (a b) -> p a b", a=2),
                    op=ALU.mult), reads=[BkbT, Be2], writes=[Bktl])
                po = palloc()
            pdc = [palloc(), palloc()]
            for h in range(4):
                p, hp = h // 2, (h % 2) * 64
                if not pre:
                    pa = palloc()
                    mm(PS[pa][:, 0:128], ktl[hp:hp + 64, p, :], qtl[hp:hp + 64, p, :], True, True, [Bktl, Bqtl], [BPS[pa]])
                    ia = rot("atm", 2)
                    k.op("dve", lambda e: e.tensor_tensor(out=atm[ia][:, :], in0=PS[pa][:, 0:128], in1=maska[:, :], op=ALU.mult),
                         reads=[BPS[pa], Bc], writes=[Batm[ia]])
                    pfree_(pa)
                    mm(PS[po][:, h * 128:(h + 1) * 128], vbtok[:, h * 128:(h + 1) * 128], atm[ia][:, :], h == 0, False,
                       [Bvbtok, Batm[ia]], [BPS[po]], signal=False, skip=True)
                for c in range(2):
                    mm(PS[pdc[c]][hp:hp + 64, p * 128:(p + 1) * 128], khat[c * 64:(c + 1) * 64, p * 128 + hp:p * 128 + hp + 64],
                       vbtok[c * 64:(c + 1) * 64, h * 128:(h + 1) * 128], True, True, [Bkhat, Bvbtok], [BPS[pdc[c]]])
            for p in range(2):
                for c in range(2):
                    if not pre:
                        for half in range(2):
                            h = p * 2 + half
                            hp = half * 64
                            mm(PS[po][:, h * 128 + c * 64:h * 128 + (c + 1) * 64], S_b[:, h, c, :],
                               qtl[:, p, c * 64:(c + 1) * 64], False, c == 1, [BSb[p][c], Bqtl], [BPS[po]], skip=True)
                    dcol = p * 128 + c * 64 + 63
                    k.op("dve", lambda e: e.scalar_tensor_tensor(
                        out=S_f[:, p, :], in0=S_f[:, p, :], scalar=e1_sb[:, dcol:dcol + 1],
                        in1=PS[pdc[c]][:, p * 128:(p + 1) * 128], op0=ALU.mult, op1=ALU.add),
                        reads=[BS, Be1, BPS[pdc[c]]], writes=[BS])
                    nslot = (c + 1) % 2
                    for half in range(2):
                        hp = half * 64
                        k.op("act", lambda e: e.copy(out=S_b[hp:hp + 64, p * 2 + half, nslot, :], in_=S_f[hp:hp + 64, p, :]),
                             reads=[BS], writes=[BSb[p][nslot]])
            pfree_(pdc[0]); pfree_(pdc[1])
            if pre:
                continue
            ck(10)
            k.op("act", lambda e: e.activation(out=osq[:, :], in_=PS[po][:, :], func=AF.Square), reads=[BPS[po]], writes=[Bosq])
            ps2 = palloc()
            mm(PS[ps2][:, :], ones[:, :], osq[:, :], True, True, [Bc, Bosq], [BPS[ps2]])
            k.op("act", lambda e: e.activation(out=orstd[:, :], in_=PS[ps2][:, :], func=AF.Ln, bias=epsD[:, 1:2]),
                 reads=[BPS[ps2], Bc], writes=[Borstd])
            k.op("act", lambda e: e.activation(out=orstd[:, :], in_=orstd[:, :], func=AF.Exp, scale=-0.5), reads=[Borstd], writes=[Borstd])
            pfree_(ps2)
            for h in range(4):
                col = PC_GN + h
                k.op("dve", lambda e: e.scalar_tensor_tensor(
                    out=otmp[:, h * 128:(h + 1) * 128], in0=PS[po][:, h * 128:(h + 1) * 128], scalar=PT[:, col:col + 1],
                    in1=orstd[:, h * 128:(h + 1) * 128], op0=ALU.mult, op1=ALU.mult), reads=[BPS[po], BPT, Borstd], writes=[Botmp])
            pfree_(po)
            k.op("pool", lambda e: e.tensor_tensor(out=mixT[:, 4:8, c0:c0 + 128], in0=otmp[:, :].rearrange("p (a b) -> p a b", a=4),
                                                   in1=srbT[:, :, c0:c0 + 128], op=ALU.mult), reads=[Botmp, BsrbT], writes=[BmixT])
            ck(11)
            ck(12)
            poa = palloc()
            pdn = palloc()
            for kv in range(2):
                for par in range(2):
                    for kb_ in range(2):
                        bi = kv * 2 + par
                        rhs = ptT[:, bi, kb_ * 256:(kb_ + 1) * 256]
                        outo = PS[poa][par * 64:(par + 1) * 64, 2 * kv * 128:(2 * kv + 2) * 128]
                        outd = PS[pdn][par * 64:(par + 1) * 64, 2 * kv * 128:(2 * kv + 2) * 128]
                        mm(outo, vtok[:, ti + kb_, kv * 64:(kv + 1) * 64], rhs, kb_ == 0, kb_ == 1, [Bvtok, Bpt[bi]], [BPS[poa]], signal=False)
                        mm(outd, ones[:, 0:64], rhs, kb_ == 0, kb_ == 1, [Bc, Bpt[bi]], [BPS[pdn]], signal=(kb_ == 1))
            k.op("dve", lambda e: e.tensor_tensor(out=rec[:, :].rearrange("p (c q) -> p c q", c=4),
                                                  in0=PS[pdn][:, :].rearrange("p (c q) -> p c q", c=4),
                                                  in1=sinkE[:, :].unsqueeze(2).to_broadcast([128, 4, 128]), op=ALU.add),
                 reads=[BPS[pdn], BsinkE], writes=[Brec])
            pfree_(pdn)
            k.op("dve", lambda e: e.reciprocal(out=rec[:, :], in_=rec[:, :]), reads=[Brec], writes=[Brec])
            k.op("dve", lambda e: e.tensor_tensor(out=mixT[:, 0:4, c0:c0 + 128], in0=PS[poa][:, :].rearrange("p (c q) -> p c q", c=4),
                                                  in1=rec[:, :].rearrange("p (c q) -> p c q", c=4), op=ALU.mult),
                 reads=[BPS[poa], Brec], writes=[BmixT])
            pfree_(poa)
        if not pre:
            w_done(); w_done(); w_done()
        if (not pre) and gtiles[-1] == NMAIN - 1:
            pb = palloc()
            k.op("pe", lambda e: e.transpose(out=PS[pb][:, 0:128], in_=kf32[:, :], identity=ident[:, :]),
                 reads=[Bkf32, Bc], writes=[BPS[pb]])
            k.op("act", lambda e: e.copy(out=osm[:, 0:128], in_=PS[pb][:, 0:128]), reads=[BPS[pb]], writes=[Bosm])
            pfree_(pb)
            k.dma(k_out[:, :], osm[:, 0:128], reads=[Bosm], sem="d_mk")
            k.dma(v_out[:, :], osm[:, 128:256], reads=[Bosm], sem="d_mv")
            k.dma(g_out.rearrange("(p h) d v -> (h d) p v", h=2), S_f[:, :, :], reads=[BS], sem="d_mg")
        k.op("dve", lambda e: e.tensor_copy(out=kaT[:, 0:128], in_=kaT[:, nt * 128:(nt + 1) * 128]), reads=[BkaT], writes=[BkaT])
        k.op("dve", lambda e: e.tensor_copy(out=kaT2[:, 0:128], in_=kaT2[:, nt * 128:(nt + 1) * 128]), reads=[BkaT2], writes=[BkaT2])
        k.op("dve", lambda e: e.tensor_copy(out=vtok[:, 0, :], in_=vtok[:, nt, :]), reads=[Bvtok], writes=[Bvtok])

    def odd_mixer(N, last, seg=False):
        wvs = {}
        for c in range(8):
            chunk_aps = []
            for q in range(3):
                idx = c * 3 + q
                bj, bq = idx // 4, idx % 4
                if bj not in wvs:
                    wvs[bj] = w_get_ahead("in1_%d" % bj, len(wvs))
                chunk_aps.append((wvs[bj][0], wvs[bj][1], bq * 128, bj))
            i = rot("cu", 2)
            wv, wb, c0, bj = chunk_aps[0]
            pcg = palloc()
            for kc in range(8):
                mm(PS[pcg][:, 0:N], wv[:, kc, c0:c0 + 128], hT[:, kc, 0:N], kc == 0, kc == 7, [wb, BhT[kc]], [BPS[pcg]])
            wv, wb, c0, bj = chunk_aps[1]
            pu = palloc()
            for kc in range(8):
                mm(PS[pu][:, 0:N], wv[:, kc, c0:c0 + 128], hT[:, kc, 0:N], kc == 0, kc == 7, [wb, BhT[kc]], [BPS[pu]])
            k.op("act", lambda e: e.copy(out=ucp[i][:, 0:N], in_=PS[pu][:, 0:N]), reads=[BPS[pu]], writes=[Bucp[i]])
            pfree_(pu)
            k.op("dve", lambda e: e.tensor_tensor(out=cu[i][:, 0:N], in0=PS[pcg][:, 0:N], in1=ucp[i][:, 0:N], op=ALU.mult),
                 reads=[BPS[pcg], Bucp[i]], writes=[Bcu[i]])
            pfree_(pcg)
            w0 = PT[:, PC_CW + 0 * 8 + c:PC_CW + 0 * 8 + c + 1]
            w1 = PT[:, PC_CW + 1 * 8 + c:PC_CW + 1 * 8 + c + 1]
            w2 = PT[:, PC_CW + 2 * 8 + c:PC_CW + 2 * 8 + c + 1]
            conv_taps(zt[i], Bzt[i], cu[i], Bcu[i], (ccs[:, c, :, :] if seg else ccar[:, c, :]), (Bccs if seg else Bccar), w0, w1, w2, None, N, seg)
            wv, wb, c0, bj = chunk_aps[2]
            pbg = palloc()
            for kc in range(8):
                mm(PS[pbg][:, 0:N], wv[:, kc, c0:c0 + 128], hT[:, kc, 0:N], kc == 0, kc == 7, [wb, BhT[kc]], [BPS[pbg]])
            k.op("dve", lambda e: e.tensor_tensor(out=mixT[:, c, 0:N], in0=PS[pbg][:, 0:N], in1=zt[i][:, 0:N], op=ALU.mult),
                 reads=[BPS[pbg], Bzt[i]], writes=[BmixT])
            pfree_(pbg)
            done_upto = (c * 3 + 3) // 4
            for bj in sorted(list(wvs.keys())):
                if bj < done_upto:
                    del wvs[bj]
                    w_done()
        assert not wvs

    def ffn(l, N, seg=False):
        pend = []

        def flush():
            while pend:
                i_, c_ = pend.pop(0)
                k.op("act", lambda e: e.activation(out=tg[i_][:, 0:N], in_=tg[i_][:, 0:N], func=AF.Gelu_apprx_tanh),
                     reads=[Btg[i_]], writes=[Btg[i_]])
                k.op("pool", lambda e: e.tensor_tensor(out=gated[:, c_, 0:N], in0=tg[i_][:, 0:N], in1=tv[i_][:, 0:N], op=ALU.mult),
                     reads=[Btg[i_], Btv[i_]], writes=[Bgated])

        for j in range(11):
            wv, wb = w_get("up%d_%d" % (l, j))
            for jj in range(2):
                c = 2 * j + jj
                i = rot("tg", 2)
                pgt = palloc()
                for kc in range(8):
                    mm(PS[pgt][:, 0:N], wv[:, kc, jj * 128:(jj + 1) * 128], hT[:, kc, 0:N], kc == 0, kc == 7, [wb, BhT[kc]], [BPS[pgt]])
                pvl = palloc()
                for kc in range(8):
                    mm(PS[pvl][:, 0:N], wv[:, kc, 256 + jj * 128:256 + (jj + 1) * 128], hT[:, kc, 0:N], kc == 0, kc == 7, [wb, BhT[kc]], [BPS[pvl]])
                first = True
                for (pp, tt, tB, cc) in ((pgt, tg[i], Btg[i], c), (pvl, tv[i], Btv[i], 22 + c)):
                    w0 = PT[:, PC_FW + (l * 3 + 0) * 44 + cc:PC_FW + (l * 3 + 0) * 44 + cc + 1]
                    w1 = PT[:, PC_FW + (l * 3 + 1) * 44 + cc:PC_FW + (l * 3 + 1) * 44 + cc + 1]
                    w2 = PT[:, PC_FW + (l * 3 + 2) * 44 + cc:PC_FW + (l * 3 + 2) * 44 + cc + 1]
                    bb = PT[:, PC_FB + l * 44 + cc:PC_FB + l * 44 + cc + 1]
                    conv_taps(tt, tB, PS[pp], BPS[pp], (fcs[:, cc, :, :] if seg else fcar[:, l, cc, :]), (Bfcs if seg else Bfcar), w0, w1, w2, bb, N, seg,
                              after_act=(flush if first else None))
                    first = False
                    pfree_(pp)
                pend.append((i, c))
            w_done()
        flush()
        proj_out(["dn%d_%d" % (l, m) for m in range(8)], gated, Bgated, 22, N, 1)

    def state_rows_out(src_cols, ncol, dst_rows_ap, stage_off):
        done = 0
        while done < ncol:
            n = min(4, ncol - done)
            pb = palloc()
            for q in range(n):
                ap_, bf_ = src_cols[done + q]
                k.op("pe", lambda e: e.transpose(out=PS[pb][0:2, q * 128:(q + 1) * 128], in_=ap_, identity=ident[:, :]),
                     reads=[bf_, Bc], writes=[BPS[pb]])
            k.op("act", lambda e: e.copy(out=rowst[:, stage_off + done * 128:stage_off + (done + n) * 128], in_=PS[pb][0:2, 0:n * 128]),
                 reads=[BPS[pb]], writes=[BmT])
            pfree_(pb)
            done += n


    def sample_unit():
        N = 64
        KcF = biasT[:, 0:2, :].rearrange("p a (s c) -> p (a s) c", c=128)
        biasS = biasT[:, 2:4, :]
        KcT = ptT[:, :, :].rearrange("p a (s c) -> p (a s) c", c=128)
        KcT2 = stg[1].bitcast(BF16)[:, :].rearrange("p (s c) -> p s c", c=128)
        Vc = gated[:, 16:20, :].rearrange("p a (s c) -> p (a s) c", c=128)
        S0bz = gated[:, 0:16, :].rearrange("p s (h v) -> p s h v", h=4)
        S0f = mT[:, :, :].rearrange("p a (s q v) -> p (a s) q v", s=2, q=2)
        khm = hT[0:64, :, :].rearrange("p a (s c) -> p (a s) c", c=256)
        Pc, BPc = osq, Bosq
        Pn, BPn = sq[0], Bsq[0]
        scC, BscC = sc[1], Bsc[1]
        scN, BscN = tmpn[0], Btmpn[0]
        BK2 = Bstg[1]
        k.dma(stg[0][0:64, :], xs[:, :], writes=[Bstg[0]], sem="d_stg0")
        k.dma(stg[0][64:96, :], scs[:, :], writes=[Bstg[0]], sem="d_stg0")
        k.dma(biasS, c_bias_s[:, :, :], writes=[Bc], sem="d_s1")
        for q in range(4):
            k.dma(Vc[:, q * 4:(q + 1) * 4, :], scv[q * 4:(q + 1) * 4].rearrange("s t c -> t s c"), writes=[Bgated], sem="d_s2", q="pool")
        k.op("pool", lambda e: e.memset(gated[:, 0:16, :], 0.0), writes=[Bgated])
        for sq_ in range(16):
            k.dma(S0f[:, sq_, :, :], sgl[sq_].rearrange("(p h) d v -> (h d) p v", h=2), writes=[BmT], sem="d_s3")
        for half in range(2):
            pb = palloc()
            for c4 in range(4):
                c = half * 4 + c4
                k.op("pe", lambda e: e.transpose(out=PS[pb][:, c4 * 64:(c4 + 1) * 64], in_=stg[0][0:64, c * 128:(c + 1) * 128],
                                                 identity=ident[0:64, 0:64]), reads=[Bstg[0], Bc], writes=[BPS[pb]])
            k.op("act", lambda e: e.copy(out=xT[:, half * 4:(half + 1) * 4, 0:64], in_=PS[pb][:, 0:256].rearrange("p (a b) -> p a b", a=4)),
                 reads=[BPS[pb]], writes=[BxT])
            pfree_(pb)
        pb = palloc()
        for c in range(8):
            k.op("pe", lambda e: e.transpose(out=PS[pb][:, c * 32:(c + 1) * 32], in_=stg[0][64:96, c * 128:(c + 1) * 128],
                                             identity=ident[64:96, 64:96]), reads=[Bstg[0], Bc], writes=[BPS[pb]])
        k.op("act", lambda e: e.copy(out=ccs[:, :, :, :].rearrange("p c s r -> p c (s r)"), in_=PS[pb][:, 0:256].rearrange("p (c x) -> p c x", c=8)),
             reads=[BPS[pb]], writes=[Bccs])
        pfree_(pb)
        for hf in range(2):
            k.dma(KcF, sck[hf * 8:(hf + 1) * 8].rearrange("s t c -> t s c"), writes=[Bc], sem="d_s4")
            for s8 in range(8):
                sq_ = hf * 8 + s8
                pb = palloc()
                k.op("pe", lambda e: e.transpose(out=PS[pb][:, 0:128], in_=KcF[:, s8, :], identity=ident[:, :]), reads=[Bc], writes=[BPS[pb]])
                mm(PS[pb][64:128, 128:256], KcF[:, s8, 0:64], ident[:, :], True, True, [Bc], [BPS[pb]])
                mm(PS[pb][0:64, 128:256], KcF[:, s8, 64:128], ident[:, :], True, True, [Bc], [BPS[pb]])
                k.op("act", lambda e: e.copy(out=KcT[:, sq_, :], in_=PS[pb][:, 0:128]), reads=[BPS[pb]], writes=[Bpt[0]])
                k.op("act", lambda e: e.copy(out=KcT2[:, sq_, :], in_=PS[pb][:, 128:256]), reads=[BPS[pb]], writes=[BK2])
                pfree_(pb)
        k.dma(ks_out[:, 0:124, :], sck[:, 4:128, :], sem="d_so")
        k.dma(vs_out[:, 0:124, :], scv[:, 4:128, :], sem="d_so")

        pre_norm(0, 0, N)
        wv, wb = w_get("in0_0")
        for c in range(4):
            fm_chunk(wv, wb, c * 128, 128, N, lambda pb: k.op(
                "act", lambda e: e.copy(out=qaT[:, c, 0:N], in_=PS[pb][:, 0:N]), reads=[BPS[pb]], writes=[BqaT]))
        w_done()
        wv, wb = w_get("in0_1")

        def ka_evac(pb):
            k.op("act", lambda e: e.copy(out=kaT[:, 0:N], in_=PS[pb][:, 0:N]), reads=[BPS[pb]], writes=[BkaT])
            k.op("act", lambda e: e.copy(out=kf32[:, 0:N], in_=PS[pb][:, 0:N]), reads=[BPS[pb]], writes=[Bkf32])
        fm_chunk(wv, wb, 0, 128, N, ka_evac)
        pb = palloc()
        for kc in range(8):
            mm(PS[pb][64:128, 0:N], wv[:, kc, 0:64], hT[:, kc, 0:N], kc == 0, kc == 7, [wb, BhT[kc]], [BPS[pb]], signal=False)
        for kc in range(8):
            mm(PS[pb][0:64, 0:N], wv[:, kc, 64:128], hT[:, kc, 0:N], kc == 0, kc == 7, [wb, BhT[kc]], [BPS[pb]])
        k.op("act", lambda e: e.copy(out=kaT2[:, 0:N], in_=PS[pb][:, 0:N]), reads=[BPS[pb]], writes=[BkaT2])
        pfree_(pb)
        pb = palloc()
        for kc in range(8):
            mm(PS[pb][0:64, 0:128], hT[:, kc, 0:64], wv[:, kc, 128:256], kc == 0, kc == 7, [wb, BhT[kc]], [BPS[pb]])
        k.op("act", lambda e: e.copy(out=vtok[0:64, 1, :], in_=PS[pb][0:64, 0:128]), reads=[BPS[pb]], writes=[Bvtok])
        k.op("act", lambda e: e.copy(out=osm[0:64, 128:256], in_=PS[pb][0:64, 0:128]), reads=[BPS[pb]], writes=[Bosm])
        pfree_(pb)
        for p in range(2):
            fm_chunk(wv, wb, 256 + p * 128, 128, N, lambda pb: k.op(
                "act", lambda e: e.copy(out=qbT[:, p, 0:N], in_=PS[pb][:, 0:N]), reads=[BPS[pb]], writes=[BqbT]))
        w_done()
        wv2, wb2 = w_get("in0_2")
        for p in range(2):
            fm_chunk(wv2, wb2, p * 128, 128, N, lambda pb: k.op(
                "act", lambda e: e.copy(out=kbT[:, p, 0:N], in_=PS[pb][:, 0:N]), reads=[BPS[pb]], writes=[BkbT]))
        wv3, wb3 = w_get_ahead("in0_3", 1)
        wv4, wb4 = w_get_ahead("in0_4", 2)
        for h in range(4):
            src_v, src_b, c0 = (wv3, wb3, 256 + h * 128) if h < 2 else (wv4, wb4, (h - 2) * 128)
            fm_chunk(src_v, src_b, c0, 128, N, lambda pb: k.op(
                "act", lambda e: e.activation(out=srbT[:, h, 0:N], in_=PS[pb][:, 0:N], func=AF.Silu), reads=[BPS[pb]], writes=[BsrbT]))
        fm_chunk(wv4, wb4, 256, 16, N, lambda pb: k.op(
            "act", lambda e: e.copy(out=glrT[0:16, 0:N], in_=PS[pb][0:16, 0:N]), reads=[BPS[pb]], writes=[BglrT]))
        pb = palloc()
        k.op("pe", lambda e: e.transpose(out=PS[pb][0:64, 0:128], in_=kf32[:, 0:64], identity=ident[:, :]), reads=[Bkf32, Bc], writes=[BPS[pb]])
        k.op("act", lambda e: e.copy(out=osm[0:64, 0:128], in_=PS[pb][0:64, 0:128]), reads=[BPS[pb]], writes=[Bosm])
        pfree_(pb)
        for i4 in range(4):
            k.dma(ks_out[:, 124 + i4, :], osm[i4:64:4, 0:128], reads=[Bosm], sem="d_so")
            k.dma(vs_out[:, 124 + i4, :], osm[i4:64:4, 128:256], reads=[Bosm], sem="d_so")

        pg = palloc()
        mm(PS[pg][0:64, 0:256], glrT[0:32, 0:64], wg[0:32, :], True, True, [BglrT, Bwg], [BPS[pg]])
        k.op("act", lambda e: e.activation(out=l_sb[0:64, :], in_=PS[pg][0:64, 0:256], func=AF.Exp, scale=-1.0), reads=[BPS[pg]], writes=[Bl])
        pfree_(pg)
        k.op("act", lambda e: e.activation(out=l_sb[0:64, :], in_=l_sb[0:64, :], func=AF.Ln, bias=epsD[0:64, 2:3]), reads=[Bl, Bc], writes=[Bl])
        pr = palloc()
        mm(PS[pr][0:64, 0:256], urev_s[:, :], l_sb[0:64, :], True, True, [Bc, Bl], [BPS[pr]])
        k.op("act", lambda e: e.activation(out=er_sb[0:64, :], in_=PS[pr][0:64, 0:256], func=AF.Exp), reads=[BPS[pr]], writes=[Ber])
        pfree_(pr)
        pk = palloc()
        for kc in range(8):
            mm(PS[pk][0:64, 0:256], hT[:, kc, 0:64], wv2[:, kc, 0:256], kc == 0, kc == 7, [wb2, BhT[kc]], [BPS[pk]])
        k.op("dve", lambda e: e.tensor_tensor(out=khat[0:64, :], in0=PS[pk][0:64, 0:256], in1=er_sb[0:64, :], op=ALU.mult),
             reads=[BPS[pk], Ber], writes=[Bkhat])
        pfree_(pk)
        pv = palloc()
        for kc in range(8):
            mm(PS[pv][0:64, 0:256], hT[:, kc, 0:64], wv2[:, kc, 256:512], kc == 0, kc == 7, [wb2, BhT[kc]], [BPS[pv]], signal=False)
        for kc in range(8):
            mm(PS[pv][0:64, 256:512], hT[:, kc, 0:64], wv3[:, kc, 0:256], kc == 0, kc == 7, [wb3, BhT[kc]], [BPS[pv]])
        k.op("act", lambda e: e.copy(out=vbtok[0:64, :], in_=PS[pv][0:64, :]), reads=[BPS[pv]], writes=[Bvbtok])
        pfree_(pv)
        w_done(); w_done(); w_done()
        pbk = palloc()
        for p in range(2):
            mm(PS[pbk][:, p * 64:(p + 1) * 64], l_sb[0:64, p * 128:(p + 1) * 128], ucum_s[:, :], True, True, [Bl, Bc], [BPS[pbk]])
        k.op("act", lambda e: e.activation(out=e1_sb[:, 0:128], in_=PS[pbk][:, 0:128], func=AF.Exp), reads=[BPS[pbk]], writes=[Be1])
        k.op("act", lambda e: e.activation(out=e2_sb[:, 0:128], in_=PS[pbk][:, 0:128], func=AF.Exp, scale=-1.0), reads=[BPS[pbk]], writes=[Be2])
        pfree_(pbk)
        k.op("dve", lambda e: e.scalar_tensor_tensor(out=qtl[:, :, 0:64], in0=qbT[:, :, 0:64], scalar=0.125,
                                                     in1=e1_sb[:, 0:128].rearrange("p (a b) -> p a b", a=2), op0=ALU.mult, op1=ALU.mult),
             reads=[BqbT, Be1], writes=[Bqtl])
        k.op("dve", lambda e: e.tensor_tensor(out=ktl[:, :, 0:64], in0=kbT[:, :, 0:64], in1=e2_sb[:, 0:128].rearrange("p (a b) -> p a b", a=2),
                                              op=ALU.mult), reads=[BkbT, Be2], writes=[Bktl])
        for p in range(2):
            for half in range(2):
                hp = half * 64
                k.op("act", lambda e: e.copy(out=S0bz[hp:hp + 64, :, p * 2 + half, :], in_=S0f[hp:hp + 64, :, p, :]), reads=[BmT], writes=[Bgated])
        k.op("dve", lambda e: e.tensor_tensor(out=khm, in0=khat[0:64, :].unsqueeze(1).to_broadcast([64, 16, 256]),
                                              in1=rowmask[:, :].unsqueeze(2).to_broadcast([64, 16, 256]), op=ALU.mult),
             reads=[Bkhat, Bc], writes=[BhT])
        po = palloc()
        for h in range(4):
            p, hp = h // 2, (h % 2) * 64
            pa = palloc()
            mm(PS[pa][0:64, 0:64], ktl[hp:hp + 64, p, 0:64], qtl[hp:hp + 64, p, 0:64], True, True, [Bktl, Bqtl], [BPS[pa]])
            ia = rot("atm", 2)
            k.op("dve", lambda e: e.tensor_tensor(out=atm[ia][0:64, 0:64], in0=PS[pa][0:64, 0:64], in1=maska_s[:, :], op=ALU.mult),
                 reads=[BPS[pa], Bc], writes=[Batm[ia]])
            pfree_(pa)
            mm(PS[po][:, h * 64:(h + 1) * 64], vbtok[0:64, h * 128:(h + 1) * 128], atm[ia][0:64, 0:64], h == 0, False,
               [Bvbtok, Batm[ia]], [BPS[po]], signal=False, skip=True)
        for h in range(4):
            p = h // 2
            for sq_ in range(16):
                mm(PS[po][:, h * 64 + sq_ * 4:h * 64 + sq_ * 4 + 4], S0bz[:, sq_, h, :], qtl[:, p, sq_ * 4:sq_ * 4 + 4], False,
                   (h == 3 and sq_ == 15), [Bgated, Bqtl], [BPS[po]], signal=(sq_ == 15), skip=True)
        for r in range(8):
            pd = palloc()
            for s2 in range(2):
                sq_ = r * 2 + s2
                for h in range(4):
                    p, hp = h // 2, (h % 2) * 64
                    mm(PS[pd][hp:hp + 64, (s2 * 2 + p) * 128:(s2 * 2 + p + 1) * 128], khm[:, sq_, p * 128 + hp:p * 128 + hp + 64],
                       vbtok[0:64, h * 128:(h + 1) * 128], True, True, [BhT, Bvbtok], [BPS[pd]])
            for s2 in range(2):
                sq_ = r * 2 + s2
                for p in range(2):
                    dcol = p * 64 + sq_ * 4 + 3
                    k.op("dve", lambda e: e.scalar_tensor_tensor(
                        out=S0f[:, sq_, p, :], in0=S0f[:, sq_, p, :], scalar=e1_sb[:, dcol:dcol + 1],
                        in1=PS[pd][:, (s2 * 2 + p) * 128:(s2 * 2 + p + 1) * 128], op0=ALU.mult, op1=ALU.add),
                        reads=[BmT, Be1, BPS[pd]], writes=[BmT])
            pfree_(pd)
        for sq_ in range(16):
            k.dma(gs_out[sq_].rearrange("(p h) d v -> (h d) p v", h=2), S0f[:, sq_, :, :], reads=[BmT], sem="d_s6")
        k.op("act", lambda e: e.activation(out=osq[:, 0:256], in_=PS[po][:, 0:256], func=AF.Square), reads=[BPS[po]], writes=[Bosq])
        ps2 = palloc()
        mm(PS[ps2][:, 0:256], ones[:, :], osq[:, 0:256], True, True, [Bc, Bosq], [BPS[ps2]])
        k.op("act", lambda e: e.activation(out=orstd[:, 0:256], in_=PS[ps2][:, 0:256], func=AF.Ln, bias=epsD[:, 1:2]), reads=[BPS[ps2], Bc], writes=[Borstd])
        k.op("act", lambda e: e.activation(out=orstd[:, 0:256], in_=orstd[:, 0:256], func=AF.Exp, scale=-0.5), reads=[Borstd], writes=[Borstd])
        pfree_(ps2)
        for h in range(4):
            col = PC_GN + h
            k.op("dve", lambda e: e.scalar_tensor_tensor(out=otmp[:, h * 64:(h + 1) * 64], in0=PS[po][:, h * 64:(h + 1) * 64], scalar=PT[:, col:col + 1],
                                                         in1=orstd[:, h * 64:(h + 1) * 64], op0=ALU.mult, op1=ALU.mult),
                 reads=[BPS[po], BPT, Borstd], writes=[Botmp])
        pfree_(po)
        k.op("pool", lambda e: e.tensor_tensor(out=mixT[:, 4:8, 0:64], in0=otmp[:, 0:256].rearrange("p (a b) -> p a b", a=4),
                                               in1=srbT[:, :, 0:64], op=ALU.mult), reads=[Botmp, BsrbT], writes=[BmixT])

        scb = [[palloc(), palloc()], [palloc(), palloc()]]
        for par in range(2):
            for kv in range(2):
                ksrc, kB = (KcT, Bpt[0]) if kv == par else (KcT2, BK2)
                nsrc, nB = (kaT, BkaT) if kv == par else (kaT2, BkaT2)
                for gi in range(2):
                    for sq_ in range(16):
                        cc0 = kv * 128 + gi * 64 + sq_ * 4
                        mm(PS[scb[0][par]][:, cc0:cc0 + 4], ksrc[par * 64:(par + 1) * 64, sq_, :],
                           qaT[par * 64:(par + 1) * 64, 2 * kv + gi, sq_ * 4:sq_ * 4 + 4], True, True, [kB, BqaT], [BPS[scb[0][par]]],
                           signal=(sq_ == 15))
                mm(PS[scb[1][par]][0:64, kv * 128:(kv + 1) * 128], nsrc[par * 64:(par + 1) * 64, 0:64],
                   qaT[par * 64:(par + 1) * 64, 2 * kv:2 * kv + 2, 0:64], True, True, [nB, BqaT], [BPS[scb[1][par]]])
        for par in range(2):
            k.op("dve", lambda e: e.tensor_tensor(out=scC[:, par * 256:(par + 1) * 256], in0=PS[scb[0][par]][:, 0:256],
                                                  in1=biasS[:, 0, par * 256:(par + 1) * 256], op=ALU.add), reads=[BPS[scb[0][par]], Bc], writes=[BscC])
            k.op("dve", lambda e: e.tensor_tensor(out=scN[0:64, par * 256:(par + 1) * 256], in0=PS[scb[1][par]][0:64, 0:256],
                                                  in1=biasS[0:64, 1, par * 256:(par + 1) * 256], op=ALU.add), reads=[BPS[scb[1][par]], Bc], writes=[BscN])
            pfree_(scb[0][par]); pfree_(scb[1][par])
        k.op("act", lambda e: e.activation(out=Pc[:, :], in_=scC[:, :], func=AF.Exp, scale=0.125), reads=[BscC], writes=[BPc])
        k.op("act", lambda e: e.activation(out=Pn[0:64, :], in_=scN[0:64, :], func=AF.Exp, scale=0.125), reads=[BscN], writes=[BPn])
        poa = palloc()
        pdn = palloc()
        for par in range(2):
            first = True
            for kv in range(2):
                for gi in range(2):
                    for sq_ in range(16):
                        cc0 = kv * 128 + gi * 64 + sq_ * 4
                        oc0 = (2 * kv + gi) * 64 + sq_ * 4
                        rhs = Pc[:, par * 256 + cc0:par * 256 + cc0 + 4]
                        mm(PS[poa][par * 64:(par + 1) * 64, oc0:oc0 + 4], Vc[:, sq_, kv * 64:(kv + 1) * 64], rhs, first, False,
                           [Bgated, BPc], [BPS[poa]], signal=False, skip=True)
                        mm(PS[pdn][par * 64:(par + 1) * 64, oc0:oc0 + 4], ones[:, 0:64], rhs, first, False,
                           [Bc, BPc], [BPS[pdn]], signal=False, skip=True)
                        first = False
                rhs = Pn[0:64, par * 256 + kv * 128:par * 256 + (kv + 1) * 128]
                mm(PS[poa][par * 64:(par + 1) * 64, 2 * kv * 64:(2 * kv + 2) * 64], vtok[0:64, 1, kv * 64:(kv + 1) * 64], rhs, False, True,
                   [Bvtok, BPn], [BPS[poa]], signal=True, skip=True)
                mm(PS[pdn][par * 64:(par + 1) * 64, 2 * kv * 64:(2 * kv + 2) * 64], ones[0:64, 0:64], rhs, False, True,
                   [Bc, BPn], [BPS[pdn]], signal=True, skip=True)
        k.op("dve", lambda e: e.tensor_tensor(out=rec[:, 0:256].rearrange("p (c q) -> p c q", c=4),
                                              in0=PS[pdn][:, 0:256].rearrange("p (c q) -> p c q", c=4),
                                              in1=sinkE[:, :].unsqueeze(2).to_broadcast([128, 4, 64]), op=ALU.add),
             reads=[BPS[pdn], BsinkE], writes=[Brec])
        pfree_(pdn)
        k.op("dve", lambda e: e.reciprocal(out=rec[:, 0:256], in_=rec[:, 0:256]), reads=[Brec], writes=[Brec])
        k.op("dve", lambda e: e.tensor_tensor(out=mixT[:, 0:4, 0:64], in0=PS[poa][:, 0:256].rearrange("p (c q) -> p c q", c=4),
                                              in1=rec[:, 0:256].rearrange("p (c q) -> p c q", c=4), op=ALU.mult),
             reads=[BPS[poa], Brec], writes=[BmixT])
        pfree_(poa)

        proj_out(["out0_0", "out0_1"], mixT, BmixT, 8, N, 4)
        post_norm_add(1, 0, N)

        def ffn_state_in(l):
            for (c0_, w_) in ((0, 4096), (4096, 1536)):
                stage = mT[0:32, :, :].rearrange("p a b -> p (a b)")
                k.dma(stage[:, 0:w_], sfs[l, :, c0_:c0_ + w_], writes=[BmT], sem="d_s5")
                for q0 in range(0, w_ // 128, 16):
                    n = min(16, w_ // 128 - q0)
                    pb = palloc()
                    for q in range(n):
                        k.op("pe", lambda e: e.transpose(out=PS[pb][:, q * 32:(q + 1) * 32], in_=stage[:, (q0 + q) * 128:(q0 + q + 1) * 128],
                                                         identity=ident[0:32, 0:32]), reads=[BmT, Bc], writes=[BPS[pb]])
                    cb = c0_ // 128 + q0
                    k.op("act", lambda e: e.copy(out=fcs[:, cb:cb + n, :, :].rearrange("p c s r -> p c (s r)"),
                                                 in_=PS[pb][:, 0:n * 32].rearrange("p (c x) -> p c x", c=n)), reads=[BPS[pb]], writes=[Bfcs])
                    pfree_(pb)

        def rows_out(src, srcB, nchunk, dst):
            stage = mT[0:32, :, :].rearrange("p a b -> p (a b)")
            done = 0
            while done < nchunk:
                nn = min(32, nchunk - done)
                for q0 in range(0, nn, 4):
                    n = min(4, nn - q0)
                    pb = palloc()
                    for q in range(n):
                        cidx = done + q0 + q
                        k.op("pe", lambda e: e.transpose(out=PS[pb][0:32, q * 128:(q + 1) * 128],
                                                         in_=src[:, cidx, :, :].rearrange("p s r -> p (s r)"), identity=ident[:, :]),
                             reads=[srcB, Bc], writes=[BPS[pb]])
                    k.op("act", lambda e: e.copy(out=stage[:, q0 * 128:(q0 + n) * 128], in_=PS[pb][0:32, 0:n * 128]), reads=[BPS[pb]], writes=[BmT])
                    pfree_(pb)
                k.dma(dst[:, done * 128:(done + nn) * 128], stage[:, 0:nn * 128], reads=[BmT], sem="d_rows")
                done += nn

        pre_norm(2, 0, N)
        ffn_state_in(0)
        ffn(0, N, seg=True)
        post_norm_add(3, 0, N)
        rows_out(fcs, Bfcs, 44, fs_out[0])
        pre_norm(0, 1, N)
        odd_mixer(N, False, seg=True)
        proj_out(["out1_0", "out1_1"], mixT, BmixT, 8, N, 4)
        post_norm_add(1, 1, N)
        rows_out(ccs, Bccs, 8, cs_out)
        pre_norm(2, 1, N)
        ffn_state_in(1)
        ffn(1, N, seg=True)
        post_norm_add(3, 1, N)
        rows_out(fcs, Bfcs, 44, fs_out[1])
        for half in range(2):
            pb = palloc()
            for c4 in range(4):
                c = half * 4 + c4
                k.op("pe", lambda e: e.transpose(out=PS[pb][0:64, c4 * 128:(c4 + 1) * 128], in_=xT[:, c, 0:64], identity=ident[:, :]),
                     reads=[BxT, Bc], writes=[BPS[pb]])
            k.op("act", lambda e: e.copy(out=stg[0][0:64, half * 512:(half + 1) * 512], in_=PS[pb][0:64, :]), reads=[BPS[pb]], writes=[Bstg[0]])
            pfree_(pb)
        k.dma(ys_out[:, :], stg[0][0:64, :], reads=[Bstg[0]], sem="d_stg0")

    def finish():
        for s_, v in k.cnt.items():
            if v and s_ != "sp":
                k.E["sp"].wait_ge(k.sems[s_], v)

    import os
    CUT = int(os.environ.get("KCUT", "99"))
    KSUB = int(os.environ.get("KSUB", "0"))

    class _Stop(Exception):
        pass

    cur = dict(ui=-1)

    def ck(n):
        if KSUB == n and cur["ui"] == CUT - 1:
            raise _Stop()
    load_resident()
    for ui, (mode, ts) in enumerate(units):
        if ui >= CUT:
            break
        cur["ui"] = ui
        nt = len(ts)
        N = nt * 128
        if mode == "pre":
            try:
                load_x(ts, N)
                ck(1)
                pre_norm(0, 0, N)
                ck(2)
                even_mixer("pre", ts, N, [-1] * nt)
            except _Stop:
                break
            continue
        try:
            load_x([NPRE + t for t in ts], N)
            pre_norm(0, 0, N)
            even_mixer("main", ts, N, ts)
            ck(13)
            proj_out(["out0_0", "out0_1"], mixT, BmixT, 8, N, 4)
            ck(141)
            post_norm_add(1, 0, N)
            ck(14)
            pre_norm(2, 0, N)
            ffn(0, N)
            ck(15)
            post_norm_add(3, 0, N)
            pre_norm(0, 1, N)
            ck(16)
            odd_mixer(N, ts[-1] == NMAIN - 1)
            ck(17)
            proj_out(["out1_0", "out1_1"], mixT, BmixT, 8, N, 4)
            post_norm_add(1, 1, N)
            pre_norm(2, 1, N)
            ffn(1, N)
            post_norm_add(3, 1, N)
            ck(18)
            store_y([(ti, t - 1) for ti, t in enumerate(ts) if t >= 1], N)
        except _Stop:
            break
        if ts == [0]:
            k.op("dve", lambda e: e.tensor_scalar(out=fcar[:, :, :, :], in0=fcar[:, :, :, :], scalar1=flg[:, 0:1], scalar2=None,
                                                  op0=ALU.mult), reads=[Bfcar, Bc], writes=[Bfcar])
            k.op("dve", lambda e: e.tensor_scalar(out=ccar[:, :, :], in0=ccar[:, :, :], scalar1=flg[:, 0:1], scalar2=None,
                                                  op0=ALU.mult), reads=[Bccar, Bc], writes=[Bccar])
        if ts[-1] == NMAIN - 1:
            state_rows_out([(ccar[:, c, :], Bccar) for c in range(8)], 8, None, 0)
            k.dma(c_out[:, :], rowst[:, 0:1024], reads=[BmT], sem="d_rows")
            for l in range(2):
                for part in range(4):
                    state_rows_out([(fcar[:, l, part * 11 + c, :], Bfcar) for c in range(11)], 11, None, 0)
                    k.dma(f_out[l, :, part * 1408:(part + 1) * 1408], rowst[:, 0:1408], reads=[BmT], sem="d_rows")

    if with_sample and CUT > len(units):
        sample_unit()
    finish()
    return nc, k


def host_consts():
    c = {}
    c["c_ident"] = np.eye(128, dtype=np.float32)
    c["c_ones"] = np.ones((128, 128), dtype=ml_dtypes.bfloat16)
    j = np.arange(128)[:, None]
    i = np.arange(128)[None, :]
    same = (j // 64) == (i // 64)
    c["c_ucum"] = np.where(same & (j <= i), -1.0 / 16.0, 0.0).astype(np.float32)
    c["c_urev"] = np.where(same & (j > i), -1.0 / 16.0, 0.0).astype(np.float32)
    c["c_maska"] = np.where(same & (j <= i), 1.0, 0.0).astype(np.float32)
    slopes = 2.0 ** (-8.0 * np.arange(1, 9) / 8.0)
    bias = np.zeros((128, 4, 4, 128), np.float32)
    s = np.arange(128)[:, None]
    q = np.arange(128)[None, :]
    for kb_ in range(2):
        dist = (128 + q - s) if kb_ == 0 else (q - s)
        valid = (dist >= 0) & (dist <= 128)
        for kv in range(2):
            for par in range(2):
                for gi in range(2):
                    b = -slopes[kv * 4 + par + 2 * gi] * dist * 8.0
                    bias[:, kv * 2 + par, kb_ * 2 + gi, :] = np.where(valid, b, NEG)
    c["c_bias"] = bias.reshape(128, 4, 512)
    j = np.arange(64)[:, None]
    i = np.arange(64)[None, :]
    same = (j // 4) == (i // 4)
    c["c_ucum_s"] = np.where(same & (j <= i), -1.0 / 16.0, 0.0).astype(np.float32)
    c["c_urev_s"] = np.where(same & (j > i), -1.0 / 16.0, 0.0).astype(np.float32)
    c["c_maska_s"] = np.where(same & (j <= i), 1.0, 0.0).astype(np.float32)
    c["c_rowmask"] = ((np.arange(64)[:, None] // 4) == np.arange(16)[None, :]).astype(np.float32)
    bs = np.full((128, 2, 2, 2, 2, 64), NEG, np.float32)
    tok = np.arange(64)
    ti = tok % 4
    srow = np.arange(128)[:, None]
    for par in range(2):
        for kv in range(2):
            for gi in range(2):
                sl = slopes[kv * 4 + par + 2 * gi]
                dist = 128 + ti[None, :] - srow
                bs[:, 0, par, kv, gi, :] = np.where(srow >= ti[None, :], -sl * dist * 8.0, NEG)
                jj = np.arange(64)[:, None]
                d2 = ti[None, :] - (jj % 4)
                ok = ((jj // 4) == (tok[None, :] // 4)) & (d2 >= 0)
                bs[0:64, 1, par, kv, gi, :] = np.where(ok, -sl * d2 * 8.0, NEG)
    c["c_bias_s"] = bs.reshape(128, 2, 512)
    return c


def host_pvec(inp):
    rows = np.zeros((512, 128), np.float32)
    for q, name in enumerate(("norm_mix_pre", "norm_mix_post", "norm_ffn_pre", "norm_ffn_post")):
        a = np.asarray(inp[name], np.float32)
        for l in range(2):
            rows[(q * 2 + l) * 8:(q * 2 + l) * 8 + 8] = a[l].reshape(8, 128)
    fw = np.asarray(inp["ffn_conv_w"], np.float32)
    for l in range(2):
        for i in range(3):
            rows[PC_FW + (l * 3 + i) * 44:PC_FW + (l * 3 + i) * 44 + 44] = fw[l, i].reshape(44, 128)
    fb = np.asarray(inp["ffn_conv_b"], np.float32)
    for l in range(2):
        rows[PC_FB + l * 44:PC_FB + l * 44 + 44] = fb[l].reshape(44, 128)
    cw = np.asarray(inp["conv_w_odd"], np.float32)
    for i in range(3):
        rows[PC_CW + i * 8:PC_CW + i * 8 + 8] = cw[0, i].reshape(8, 128)
    rows[PC_GN:PC_GN + 4] = np.asarray(inp["gla_norm"], np.float32)[0].reshape(4, 128)
    return rows


_CACHE = {}


def make_in_maps(inp):
    consts = host_consts()
    pvec = host_pvec(inp)
    xp = np.asarray(inp["x_prompt"], np.float32)
    shared = dict(consts)
    shared["pvec"] = pvec
    shared["w_in_even"] = np.ascontiguousarray(np.asarray(inp["w_in_even"], np.float32)[0])
    shared["w_out_even"] = np.ascontiguousarray(np.asarray(inp["w_out_even"], np.float32)[0])
    shared["w_in_odd"] = np.ascontiguousarray(np.asarray(inp["w_in_odd"], np.float32)[0])
    shared["w_out_odd"] = np.ascontiguousarray(np.asarray(inp["w_out_odd"], np.float32)[0])
    for l in range(2):
        shared["ffn_up%d" % l] = np.ascontiguousarray(np.asarray(inp["ffn_up"], np.float32)[l])
        shared["ffn_down%d" % l] = np.ascontiguousarray(np.asarray(inp["ffn_down"], np.float32)[l])
    shared["w_gate_up"] = np.ascontiguousarray(np.asarray(inp["w_gate_up"], np.float32)[0])
    shared["b_gate"] = np.ascontiguousarray(np.asarray(inp["b_gate"], np.float32))
    shared["attn_sinks"] = np.ascontiguousarray(np.asarray(inp["attn_sinks"], np.float32))
    in_maps = []
    for c in range(8):
        b, half = c // 2, c % 2
        xin = np.zeros(((NPRE + NMAIN) * 128, D), np.float32)
        if half == 1:
            xin[:] = xp[b]
        else:
            xin[(NPRE + 1) * 128:] = xp[b, 0:2048]
        m = dict(shared)
        m["xin"] = xin
        fl = np.zeros((128, 2), np.float32)
        fl[:, 0] = float(half)
        fl[:, 1] = (float(half) - 1.0) * (-NEG)
        m["flagv"] = fl
        sl = slice(c * NSEQ, (c + 1) * NSEQ)
        m["xs"] = np.ascontiguousarray(np.asarray(inp["x_sample"], np.float32)[sl].reshape(64, D))
        m["sck"] = np.ascontiguousarray(np.asarray(inp["cache_swa_k"], np.float32)[0, sl].reshape(16, 128, 128))
        m["scv"] = np.ascontiguousarray(np.asarray(inp["cache_swa_v"], np.float32)[0, sl].reshape(16, 128, 128))
        m["sgl"] = np.ascontiguousarray(np.asarray(inp["state_gla"], np.float32)[0, sl])
        m["scs"] = np.ascontiguousarray(np.asarray(inp["state_conv"], np.float32)[0, sl].reshape(32, D))
        m["sfs"] = np.ascontiguousarray(np.asarray(inp["state_ffn"], np.float32)[:, sl].reshape(2, 32, F2))
        in_maps.append(m)
    return in_maps


def kernel(**inp):
    if "nc" not in _CACHE:
        _CACHE["nc"] = build_program()[0]
    nc = _CACHE["nc"]
    in_maps = make_in_maps(inp)
    import os
    if os.environ.get("KCORES"):
        lc = [int(x) for x in os.environ["KCORES"].split(",")]
        res = run_bass_kernel_spmd(nc, [in_maps[c] for c in lc], core_ids=list(range(len(lc))))
        R = [None] * 8
        for i, c in enumerate(lc):
            R[c] = res.results[i]
        for c in range(8):
            if R[c] is None:
                R[c] = {kk: np.zeros_like(vv) for kk, vv in res.results[0].items()}
    else:
        res = run_bass_kernel_spmd(nc, in_maps, core_ids=list(range(8)))
        R = res.results
    y_prompt = np.zeros((4, 4096, D), np.float32)
    swa_k = np.zeros((1, 4, 128, 2, 64), np.float32)
    swa_v = np.zeros((1, 4, 128, 2, 64), np.float32)
    gla = np.zeros((1, 4, 4, 64, 128), np.float32)
    conv = np.zeros((1, 4, 2, D), np.float32)
    ffn_s = np.zeros((2, 4, 2, F2), np.float32)
    for c in range(8):
        b, half = c // 2, c % 2
        y_prompt[b, half * 2048:(half + 1) * 2048] = R[c]["y_out"]
        if half == 1:
            swa_k[0, b] = R[c]["k_out"].reshape(128, 2, 64)
            swa_v[0, b] = R[c]["v_out"].reshape(128, 2, 64)
            gla[0, b] = R[c]["g_out"]
            conv[0, b] = R[c]["c_out"]
            ffn_s[:, b] = R[c]["f_out"]
    y_s = np.zeros((128, 4, D), np.float32)
    ks_s = np.zeros((1, 128, 128, 2, 64), np.float32)
    vs_s = np.zeros((1, 128, 128, 2, 64), np.float32)
    gs_s = np.zeros((1, 128, 4, 64, 128), np.float32)
    cs_s = np.zeros((1, 128, 2, D), np.float32)
    fs_s = np.zeros((2, 128, 2, F2), np.float32)
    for c in range(8):
        sl = slice(c * NSEQ, (c + 1) * NSEQ)
        y_s[sl] = R[c]["ys_out"].reshape(16, 4, D)
        ks_s[0, sl] = R[c]["ks_out"].reshape(16, 128, 2, 64)
        vs_s[0, sl] = R[c]["vs_out"].reshape(16, 128, 2, 64)
        gs_s[0, sl] = R[c]["gs_out"]
        cs_s[0, sl] = R[c]["cs_out"].reshape(16, 2, D)
        fs_s[:, sl] = R[c]["fs_out"].reshape(2, 16, 2, F2)
    return (y_prompt, y_s, swa_k, swa_v, gla, conv, ffn_s, ks_s, vs_s, gs_s, cs_s, fs_s)
```

```python
import numpy as np
import ml_dtypes
import concourse.bass as bass
import concourse.mybir as mybir
from concourse.bass_utils import run_bass_kernel_spmd

F32 = mybir.dt.float32
BF16 = mybir.dt.bfloat16
AF = mybir.ActivationFunctionType
ALU = mybir.AluOpType
AX = mybir.AxisListType

D = 1024
DFF = 2816
F2 = 5632
NPRE = 15
NMAIN = 17
NSEQ = 16
EPS = 1e-6
NEG = -240000.0
NSLOT = 5
SLOT_E = 4096


class Buf:
    __slots__ = ("name", "w", "r")

    def __init__(self, name):
        self.name = name
        self.w = None
        self.r = {}


class KB:
    ENG = ("pe", "act", "dve", "pool", "sp")

    def __init__(self, nc):
        self.nc = nc
        self.E = dict(pe=nc.tensor, act=nc.scalar, dve=nc.vector, pool=nc.gpsimd, sp=nc.sync)
        self.sems = {}
        self.cnt = {}
        self.seen = {e: {} for e in self.ENG}
        for e in self.ENG:
            self.new_sem(e)
        self.n_inst = {e: 0 for e in self.ENG}

    def new_sem(self, name):
        self.sems[name] = self.nc.alloc_semaphore(name="s_" + name)
        self.cnt[name] = 0
        return name

    def sb(self, name, shape, dt):
        return self.nc.alloc_sbuf_tensor(name, list(shape), dt)

    def _wait(self, eng, ev, kind):
        if ev is None:
            return
        s, v = ev
        if s == eng and kind != "raw":
            return
        if self.seen[eng].get(s, 0) >= v:
            return
        self.E[eng].wait_ge(self.sems[s], v)
        self.seen[eng][s] = v

    def _deps(self, eng, reads, writes, force=False):
        for b in reads:
            self._wait(eng, b.w, "raw")
        for b in writes:
            self._wait(eng, b.w, "raw" if force else "waw")
            for s, v in b.r.items():
                self._wait(eng, (s, v), "raw" if force else "war")

    def _record(self, ev, reads, writes):
        s, v = ev
        for b in reads:
            if b.r.get(s, 0) < v:
                b.r[s] = v
        for b in writes:
            b.w = ev
            b.r = {}

    @staticmethod
    def _flat(bs):
        out = []
        for b in bs:
            if isinstance(b, (list, tuple)):
                out.extend(KB._flat(b))
            else:
                out.append(b)
        return out

    def op(self, eng, fn, reads=(), writes=(), signal=True):
        reads, writes = self._flat(reads), self._flat(writes)
        self._deps(eng, reads, writes)
        ins = fn(self.E[eng])
        self.n_inst[eng] += 1
        ev = (eng, self.cnt[eng] + 1)
        if signal:
            ins.then_inc(self.sems[eng], 1)
            self.cnt[eng] += 1
        self._record(ev, reads, writes)
        return ins

    def dma(self, out, in_, reads=(), writes=(), sem=None, q="sp", **kw):
        reads, writes = self._flat(reads), self._flat(writes)
        self._deps(q, reads, writes, force=True)
        ins = self.E[q].dma_start(out=out, in_=in_, **kw)
        ins.then_inc(self.sems[sem], 16)
        self.cnt[sem] += 16
        self.n_inst[q] += 1
        self._record((sem, self.cnt[sem]), reads, writes)
        return ins


def weight_blocks():
    blocks = []
    for j in range(5):
        w = min(512, 2320 - 512 * j)
        blocks.append(("in0_%d" % j, 8, w, [("w_in_even", 0, 512 * j, w, 0)]))
    for j in range(2):
        blocks.append(("out0_%d" % j, 8, 512, [("w_out_even", 0, 512 * j, 512, 0)]))
    for l in range(2):
        for j in range(11):
            blocks.append(("up%d_%d" % (l, j), 8, 512, [("ffn_up%d" % l, 0, 256 * j, 256, 0),
                                                        ("ffn_up%d" % l, 0, DFF + 256 * j, 256, 256)]))
        for m in range(8):
            blocks.append(("dn%d_%d" % (l, m), 22, 128, [("ffn_down%d" % l, 0, 128 * m, 128, 0)]))
    chunks = []
    for c in range(8):
        chunks += [1024 + 128 * c, 2048 + 128 * c, 128 * c]
    for j in range(6):
        blocks.append(("in1_%d" % j, 8, 512, [("w_in_odd", 0, chunks[4 * j + q], 128, 128 * q) for q in range(4)]))
    for j in range(2):
        blocks.append(("out1_%d" % j, 8, 512, [("w_out_odd", 0, 512 * j, 512, 0)]))
    return blocks


def unit_block_order(mode):
    if mode == "pre":
        return ["in0_%d" % j for j in range(1, 5)]
    o = ["in0_%d" % j for j in range(5)] + ["out0_%d" % j for j in range(2)]
    o += ["up0_%d" % j for j in range(11)] + ["dn0_%d" % m for m in range(8)]
    o += ["in1_%d" % j for j in range(6)] + ["out1_%d" % j for j in range(2)]
    o += ["up1_%d" % j for j in range(11)] + ["dn1_%d" % m for m in range(8)]
    return o


def pc_norm(q, l, kc):
    return (q * 2 + l) * 8 + kc
PC_FW = 64
PC_FB = 328
PC_CW = 416
PC_GN = 440


def build_program(with_sample=True):
    nc = bass.Bass("TRN2", target_bir_lowering=False)
    k = KB(nc)
    Dm = {}

    def din(name, shape, dt=F32):
        Dm[name] = nc.dram_tensor(name, list(shape), dt, kind="ExternalInput").ap()
        return Dm[name]

    def dout(name, shape):
        Dm[name] = nc.dram_tensor(name, list(shape), F32, kind="ExternalOutput").ap()
        return Dm[name]

    xin = din("xin", [(NPRE + NMAIN) * 128, D])
    flagv = din("flagv", [128, 2])
    pvec = din("pvec", [512, 128])
    c_ident = din("c_ident", [128, 128])
    c_ones = din("c_ones", [128, 128], BF16)
    c_ucum = din("c_ucum", [128, 128])
    c_urev = din("c_urev", [128, 128])
    c_maska = din("c_maska", [128, 128])
    c_bias = din("c_bias", [128, 4, 512])
    din("w_in_even", [D, 2320]); din("w_out_even", [D, D]); din("w_in_odd", [D, 3 * D]); din("w_out_odd", [D, D])
    for l in range(2):
        din("ffn_up%d" % l, [D, F2]); din("ffn_down%d" % l, [DFF, D])
    w_gate_up = din("w_gate_up", [16, 256]); b_gate = din("b_gate", [1, 256]); attn_sinks = din("attn_sinks", [1, 8])
    xs = din("xs", [64, D]); sck = din("sck", [16, 128, 128]); scv = din("scv", [16, 128, 128])
    sgl = din("sgl", [16, 4, 64, 128]); scs = din("scs", [32, D]); sfs = din("sfs", [2, 32, F2])
    c_ucum_s = din("c_ucum_s", [64, 64]); c_urev_s = din("c_urev_s", [64, 64]); c_maska_s = din("c_maska_s", [64, 64])
    c_rowmask = din("c_rowmask", [64, 16]); c_bias_s = din("c_bias_s", [128, 2, 512])
    ys_out = dout("ys_out", [64, D]); ks_out = dout("ks_out", [16, 128, 128]); vs_out = dout("vs_out", [16, 128, 128])
    gs_out = dout("gs_out", [16, 4, 64, 128]); cs_out = dout("cs_out", [32, D]); fs_out = dout("fs_out", [2, 32, F2])
    y_out = dout("y_out", [16 * 128, D])
    k_out = dout("k_out", [128, 128]); v_out = dout("v_out", [128, 128])
    g_out = dout("g_out", [4, 64, 128])
    c_out = dout("c_out", [2, D])
    f_out = dout("f_out", [2, 2, F2])

    for s in ("d_cast", "d_const", "d_pv", "d_sink", "d_stg0", "d_stg1", "d_misc", "d_rows", "d_s1", "d_s2", "d_s3", "d_s4", "d_s5", "d_s6", "d_so", "d_mk", "d_mv", "d_mg", "d_res"):
        k.new_sem(s)
    for i in range(NSLOT):
        k.new_sem("d_w%d" % i)

    blocks = {b[0]: b for b in weight_blocks()}
    scr = {}
    for src in ("w_in_even", "w_out_even", "ffn_up0", "ffn_down0", "w_in_odd", "w_out_odd", "ffn_up1", "ffn_down1"):
        R, C = Dm[src].shape
        t = nc.dram_tensor("scr_" + src, [R, C], BF16, kind="Internal").ap()
        b = Buf("scr_" + src)
        scr[src] = (t, b)
        k.new_sem("d_c_" + src)

    def emit_casts(names):
        for src in names:
            t, b = scr[src]
            R, C = Dm[src].shape
            step = 512
            for r0 in range(0, R, step):
                r1 = min(R, r0 + step)
                k.dma(t[r0:r1, :], Dm[src][r0:r1, :], writes=[b], sem="d_c_" + src, q="pool")

    emit_casts(["w_in_even"])

    ring = [k.sb("ring%d" % i, [128, SLOT_E], BF16) for i in range(NSLOT)]
    ringB = [Buf("ring%d" % i) for i in range(NSLOT)]
    xT = k.sb("xT", [128, 8, 512], F32); BxT = [Buf("xT%d" % i) for i in range(8)]
    hT = k.sb("hT", [128, 8, 512], BF16); BhT = [Buf("hT%d" % i) for i in range(8)]
    mT = k.sb("mT", [128, 8, 512], F32); BmT = [Buf("mT%d" % i) for i in range(8)]
    sq = [k.sb("sq%d" % i, [128, 512], BF16) for i in range(2)]; Bsq = [Buf("sq%d" % i) for i in range(2)]
    rstd = k.sb("rstd", [128, 512], F32); Brstd = Buf("rstd")
    tmpn = [k.sb("tmpn%d" % i, [128, 512], F32) for i in range(2)]; Btmpn = [Buf("tmpn%d" % i) for i in range(2)]
    stg = [k.sb("stg%d" % i, [128, 1024], F32) for i in range(2)]; Bstg = [Buf("stg%d" % i) for i in range(2)]
    qaT = k.sb("qaT", [128, 4, 512], BF16); BqaT = Buf("qaT")
    kaT = k.sb("kaT", [128, 640], BF16); BkaT = Buf("kaT")
    kaT2 = k.sb("kaT2", [128, 640], BF16); BkaT2 = Buf("kaT2")
    vtok = k.sb("vtok", [128, 5, 128], BF16); Bvtok = Buf("vtok")
    qbT = k.sb("qbT", [128, 2, 512], F32); BqbT = Buf("qbT")
    kbT = k.sb("kbT", [128, 2, 512], F32); BkbT = Buf("kbT")
    srbT = k.sb("srbT", [128, 4, 512], F32); BsrbT = Buf("srbT")
    glrT = k.sb("glrT", [32, 512], BF16); BglrT = Buf("glrT")
    mixT = k.sb("mixT", [128, 8, 512], BF16); BmixT = Buf("mixT")
    gated = k.sb("gated", [128, 22, 512], BF16); Bgated = Buf("gated")
    tg = [k.sb("tg%d" % i, [128, 512], F32) for i in range(2)]; Btg = [Buf("tg%d" % i) for i in range(2)]
    tv = [k.sb("tv%d" % i, [128, 512], F32) for i in range(2)]; Btv = [Buf("tv%d" % i) for i in range(2)]
    l_sb = k.sb("l_sb", [128, 256], F32); Bl = Buf("l_sb")
    er_sb = k.sb("er_sb", [128, 256], F32); Ber = Buf("er")
    e1_sb = k.sb("e1_sb", [128, 256], F32); Be1 = Buf("e1")
    e2_sb = k.sb("e2_sb", [128, 256], F32); Be2 = Buf("e2")
    qtl = k.sb("qtl", [128, 2, 128], BF16); Bqtl = Buf("qtl")
    ktl = k.sb("ktl", [128, 2, 128], BF16); Bktl = Buf("ktl")
    khat = k.sb("khat", [128, 256], BF16); Bkhat = Buf("khat")
    vbtok = k.sb("vbtok", [128, 512], BF16); Bvbtok = Buf("vbtok")
    atm = [k.sb("atm%d" % i, [128, 128], BF16) for i in range(2)]; Batm = [Buf("atm%d" % i) for i in range(2)]
    S_f = k.sb("S_f", [128, 2, 128], F32); BS = Buf("S_f")
    S_b = k.sb("S_b", [128, 4, 2, 128], BF16); BSb = [[Buf("Sb%d%d" % (p, i)) for i in range(2)] for p in range(2)]
    osq = k.sb("osq", [128, 512], BF16); Bosq = Buf("osq")
    orstd = k.sb("orstd", [128, 512], F32); Borstd = Buf("orstd")
    otmp = k.sb("otmp", [128, 512], F32); Botmp = Buf("otmp")
    rec, Brec = otmp, Botmp
    kf32 = k.sb("kf32", [128, 128], F32); Bkf32 = Buf("kf32")
    sc = [k.sb("sc%d" % i, [128, 512], F32) for i in range(2)]; Bsc = [Buf("sc%d" % i) for i in range(2)]
    ptT = k.sb("ptT", [128, 4, 512], BF16); Bpt = [Buf("pt%d" % i) for i in range(4)]
    ident = k.sb("ident", [128, 128], F32); ones = k.sb("ones", [128, 128], BF16)
    ucum = k.sb("ucum", [128, 128], F32); urev = k.sb("urev", [128, 128], F32); maska = k.sb("maska", [128, 128], F32)
    biasT = k.sb("biasT", [128, 4, 512], F32)
    PT = k.sb("PT", [128, 512], F32)
    pv_in = k.sb("pv_in", [128, 4, 128], F32)
    flg = k.sb("flg", [128, 2], F32)
    epsD = k.sb("epsD", [128, 4], F32)
    wg = k.sb("wg", [32, 256], BF16)
    sinkE = k.sb("sinkE", [128, 4], F32)
    Bc = Buf("consts"); BPT = Buf("PT"); Bpv = Buf("pv_in"); Bwg = Buf("wg"); BsinkE = Buf("sinkE")
    fcar = k.sb("fcar", [128, 2, 44, 2], F32); Bfcar = Buf("fcar")
    ccar = k.sb("ccar", [128, 8, 2], F32); Bccar = Buf("ccar")
    ucum_s = k.sb("ucum_s", [64, 64], F32); urev_s = k.sb("urev_s", [64, 64], F32); maska_s = k.sb("maska_s", [64, 64], F32)
    rowmask = k.sb("rowmask", [64, 16], F32)
    fcs = k.sb("fcs", [128, 44, 16, 2], F32); Bfcs = Buf("fcs")
    ccs = k.sb("ccs", [128, 8, 16, 2], F32); Bccs = Buf("ccs")
    cu, Bcu = tg, Btg
    ucp, Bucp = tv, Btv
    zt, Bzt = tmpn, Btmpn
    osm = k.sb("osm", [128, 256], F32); Bosm = Buf("osm")
    rowst = mT[0:2, 0:3, :].rearrange("p a b -> p (a b)")

    PS = [nc.alloc_psum_tensor("ps%d" % i, [128, 512], F32) for i in range(8)]
    BPS = [Buf("ps%d" % i) for i in range(8)]
    pfree = list(range(8))

    def palloc():
        assert pfree, "out of PSUM banks"
        return pfree.pop(0)

    def pfree_(i):
        pfree.append(i)

    k.dma(ident[:, :], c_ident[:, :], writes=[Bc], sem="d_const")
    k.dma(ones[:, :], c_ones[:, :], writes=[Bc], sem="d_const")
    k.dma(ucum[:, :], c_ucum[:, :], writes=[Bc], sem="d_const")
    k.dma(urev[:, :], c_urev[:, :], writes=[Bc], sem="d_const")
    k.dma(maska[:, :], c_maska[:, :], writes=[Bc], sem="d_const")
    k.dma(biasT[:, :, :], c_bias[:, :, :], writes=[Bc], sem="d_const")
    k.dma(flg[:, :], flagv[:, :], writes=[Bc], sem="d_const")
    k.dma(ucum_s[:, :], c_ucum_s[:, :], writes=[Bc], sem="d_const")
    k.dma(urev_s[:, :], c_urev_s[:, :], writes=[Bc], sem="d_const")
    k.dma(maska_s[:, :], c_maska_s[:, :], writes=[Bc], sem="d_const")
    k.dma(rowmask[:, :], c_rowmask[:, :], writes=[Bc], sem="d_const")
    k.dma(pv_in[:, :, :], pvec.rearrange("(a p) c -> p a c", p=128), writes=[Bpv], sem="d_pv")
    k.op("pool", lambda e: e.memset(wg[:, :], 0.0), writes=[Bwg])
    k.dma(wg[0:16, :], w_gate_up[:, :], writes=[Bwg], sem="d_cast", q="pool")
    k.dma(wg[16:17, :], b_gate[:, :], writes=[Bwg], sem="d_cast", q="pool")
    sink1 = k.sb("sink1", [1, 8], F32); onesf = k.sb("onesf", [1, 128], F32); sink8 = k.sb("sink8", [128, 8], F32)
    Bs1 = Buf("sink1"); Bs8 = Buf("sink8")
    k.dma(sink1[:, :], attn_sinks[:, :], writes=[Bs1], sem="d_sink")
    k.op("pool", lambda e: e.memset(onesf[:, :], 1.0), writes=[Bs1])
    pb = palloc()
    k.op("pe", lambda e: e.matmul(PS[pb][:, 0:8], lhsT=onesf[0:1, :], rhs=sink1[0:1, :], start=True, stop=True),
         reads=[Bs1], writes=[BPS[pb]])
    k.op("act", lambda e: e.activation(out=sink8[:, :], in_=PS[pb][:, 0:8], func=AF.Exp), reads=[BPS[pb]], writes=[Bs8])
    pfree_(pb)
    k.op("dve", lambda e: e.tensor_copy(out=sinkE[0:64, :], in_=sink8[0:64, 0:8:2]), reads=[Bs8], writes=[BsinkE])
    k.op("dve", lambda e: e.tensor_copy(out=sinkE[64:128, :], in_=sink8[64:128, 1:8:2]), reads=[Bs8], writes=[BsinkE])
    pb = palloc()
    for a in range(4):
        k.op("pe", lambda e: e.transpose(out=PS[pb][:, a * 128:(a + 1) * 128], in_=pv_in[:, a, :], identity=ident[:, :]),
             reads=[Bpv, Bc], writes=[BPS[pb]])
    k.op("act", lambda e: e.copy(out=PT[:, :], in_=PS[pb][:, :]), reads=[BPS[pb]], writes=[BPT])
    pfree_(pb)
    k.op("act", lambda e: e.mul(out=PT[:, 0:64], in_=PT[:, 0:64], mul=32.0), reads=[BPT], writes=[BPT])
    k.op("act", lambda e: e.mul(out=PT[:, PC_GN:PC_GN + 4], in_=PT[:, PC_GN:PC_GN + 4], mul=float(np.sqrt(128.0))),
         reads=[BPT], writes=[BPT])
    k.op("pool", lambda e: e.memset(glrT[:, :], 1.0), writes=[BglrT])
    k.op("pool", lambda e: e.memset(epsD[:, 0:1], float(D * EPS)), writes=[Bc])
    k.op("pool", lambda e: e.memset(epsD[:, 1:2], float(128 * EPS)), writes=[Bc])
    k.op("pool", lambda e: e.memset(epsD[:, 2:3], 1.0), writes=[Bc])
    k.op("pool", lambda e: e.memset(epsD[:, 3:4], 0.0), writes=[Bc])
    k.op("pool", lambda e: e.memset(S_f[:, :, :], 0.0), writes=[BS])
    k.op("pool", lambda e: e.memset(S_b[:, :, :, :], 0.0), writes=[BSb[0][0], BSb[0][1], BSb[1][0], BSb[1][1]])
    k.op("pool", lambda e: e.memset(fcar[:, :, :, :], 0.0), writes=[Bfcar])
    k.op("pool", lambda e: e.memset(ccar[:, :, :], 0.0), writes=[Bccar])
    k.op("pool", lambda e: e.memset(kaT[:, :], 0.0), writes=[BkaT])
    k.op("pool", lambda e: e.memset(kaT2[:, :], 0.0), writes=[BkaT2])
    k.op("pool", lambda e: e.memset(vtok[:, :, :], 0.0), writes=[Bvtok])
    emit_casts(["w_out_even", "ffn_up0", "ffn_down0", "w_in_odd", "w_out_odd", "ffn_up1", "ffn_down1"])

    units = []
    for i in range(0, NPRE, 4):
        units.append(("pre", list(range(i, min(i + 4, NPRE)))))
    main_split = [[0], [1, 2, 3, 4], [5, 6, 7, 8], [9, 10, 11, 12], [13, 14, 15, 16]]
    for ts in main_split:
        units.append(("main", ts))
    plan = []
    for mode, ts in units:
        if mode != "pre":
            plan += unit_block_order(mode)
    if with_sample:
        plan += unit_block_order("main")
    wstate = dict(next_issue=0, next_use=0)

    def w_issue():
        u = wstate["next_issue"]
        if u >= len(plan):
            return
        name, KC, W, pieces = blocks[plan[u]]
        s = u % NSLOT
        dst = ring[s][:, 0:KC * W].rearrange("p (kc w) -> p kc w", kc=KC)
        for (src, r0, c0, w, off) in pieces:
            t, b = scr[src]
            k.dma(dst[:, :, off:off + w], t[r0:r0 + KC * 128, c0:c0 + w].rearrange("(kc p) w -> p kc w", p=128),
                  reads=[b], writes=[ringB[s]], sem="d_w%d" % s)
        wstate["next_issue"] += 1

    def w_get(name):
        u = wstate["next_use"]
        assert plan[u] == name, (plan[u], name)
        while wstate["next_issue"] <= u:
            w_issue()
        s = u % NSLOT
        _, KC, W, _ = blocks[name]
        return ring[s][:, 0:KC * W].rearrange("p (kc w) -> p kc w", kc=KC), ringB[s]

    gflat = gated[:, :, :].rearrange("p a b -> p (a b)")
    RES = {"in0_1": (gflat[:, 0:2048].rearrange("p (kc w) -> p kc w", kc=8), 512, 256),
           "in0_2": (gflat[:, 2048:6144].rearrange("p (kc w) -> p kc w", kc=8), 1024, 512),
           "in0_3": (gflat[:, 6144:8192].rearrange("p (kc w) -> p kc w", kc=8), 1536, 256),
           "in0_4": (gflat[:, 8192:10368].rearrange("p (kc w) -> p kc w", kc=8), 2048, 272)}

    def load_resident():
        t, b = scr["w_in_even"]
        for name, (view, c0, w) in RES.items():
            k.dma(view, t[0:1024, c0:c0 + w].rearrange("(kc p) w -> p kc w", p=128), reads=[b], writes=[Bgated], sem="d_res")

    def w_get_ahead(name, ahead):
        wstate["next_use"] += ahead
        r = w_get(name)
        wstate["next_use"] -= ahead
        return r

    def w_done():
        wstate["next_use"] += 1
        while wstate["next_issue"] < min(len(plan), wstate["next_use"] + NSLOT):
            w_issue()

    for _ in range(NSLOT):
        w_issue()

    def mm(out, lhsT, rhs, start, stop, reads, writes, signal=None, skip=False):
        if signal is None:
            signal = stop
        if skip:
            k.op("pe", lambda e: e.matmul(out, lhsT=lhsT, rhs=rhs, start=start, stop=stop, skip_group_check=True),
                 reads=reads, writes=writes, signal=signal)
        else:
            k.op("pe", lambda e: e.matmul(out, lhsT=lhsT, rhs=rhs, start=start, stop=stop), reads=reads, writes=writes,
                 signal=signal)

    cnt = dict(sq=0, tmpn=0, stg=0, sc=0, atm=0, tg=0, cu=0)

    def rot(name, n):
        i = cnt[name] % n
        cnt[name] += 1
        return i

    def rms_stats(srcs, N, src_bufs, eps_scaled):
        pb = palloc()
        for c in range(8):
            i = rot("sq", 2)
            k.op("act", lambda e: e.activation(out=sq[i][:, 0:N], in_=srcs[c], func=AF.Square), reads=[src_bufs[c]],
                 writes=[Bsq[i]])
            mm(PS[pb][:, 0:N], ones[:, :], sq[i][:, 0:N], c == 0, c == 7, [Bsq[i], Bc], [BPS[pb]], signal=True)
        k.op("act", lambda e: e.activation(out=rstd[:, 0:N], in_=PS[pb][:, 0:N], func=AF.Ln, bias=epsD[:, 0:1]),
             reads=[BPS[pb], Bc], writes=[Brstd])
        k.op("act", lambda e: e.activation(out=rstd[:, 0:N], in_=rstd[:, 0:N], func=AF.Exp, scale=-0.5), reads=[Brstd], writes=[Brstd])
        pfree_(pb)

    def pre_norm(q, l, N):
        rms_stats([xT[:, c, 0:N] for c in range(8)], N, BxT, D * EPS)
        for c in range(8):
            col = pc_norm(q, l, c)
            k.op("dve", lambda e: e.scalar_tensor_tensor(out=hT[:, c, 0:N], in0=xT[:, c, 0:N], scalar=PT[:, col:col + 1],
                                                         in1=rstd[:, 0:N], op0=ALU.mult, op1=ALU.mult),
                 reads=[BxT[c], BPT, Brstd], writes=[BhT[c]])

    def post_norm_add(q, l, N):
        rms_stats([mT[:, c, 0:N] for c in range(8)], N, BmT, D * EPS)
        ck(142)
        for c in range(8):
            col = pc_norm(q, l, c)
            i = rot("tmpn", 2)
            if c == 1:
                ck(144)
            k.op("dve", lambda e: e.scalar_tensor_tensor(out=tmpn[i][:, 0:N], in0=mT[:, c, 0:N], scalar=PT[:, col:col + 1],
                                                         in1=rstd[:, 0:N], op0=ALU.mult, op1=ALU.mult),
                 reads=[BmT[c], BPT, Brstd], writes=[Btmpn[i]])
            if c == 0:
                ck(143)
            k.op("pool", lambda e: e.tensor_tensor(out=xT[:, c, 0:N], in0=xT[:, c, 0:N], in1=tmpn[i][:, 0:N], op=ALU.add),
                 reads=[Btmpn[i], BxT[c]], writes=[BxT[c]])

    def proj_out(wnames, rhs_tile, rhs_buf, KC, N, per_block):
        m = 0
        for wn in wnames:
            wv, wb = w_get(wn)
            for j in range(per_block):
                pb = palloc()
                for kc in range(KC):
                    mm(PS[pb][:, 0:N], wv[:, kc, j * 128:(j + 1) * 128], rhs_tile[:, kc, 0:N], kc == 0, kc == KC - 1,
                       [wb, rhs_buf], [BPS[pb]])
                k.op("act", lambda e: e.copy(out=mT[:, m, 0:N], in_=PS[pb][:, 0:N]), reads=[BPS[pb]], writes=[BmT[m]])
                pfree_(pb)
                m += 1
            w_done()

    def conv_taps(t_ap, tB, src, srcB, car, carB, w0, w1, w2, bias, N, seg=False, after_act=None):
        if seg:
            V = lambda ap, a, b: ap[:, 0:64].rearrange("p (s i) -> p s i", i=4)[:, :, a:b]
            C = lambda a, b: car[:, :, a:b]
            L = 4
        else:
            V = lambda ap, a, b: ap[:, a:b]
            C = lambda a, b: car[:, a:b]
            L = N
        k.op("act", lambda e: e.activation(out=t_ap[:, 0:N], in_=src[:, 0:N], func=AF.Identity, scale=w2,
                                           bias=(epsD[:, 3:4] if bias is None else bias)), reads=[srcB, BPT, Bc], writes=[tB])
        if after_act is not None:
            after_act()
        k.op("dve", lambda e: e.scalar_tensor_tensor(out=V(t_ap, 1, L), in0=V(src, 0, L - 1), scalar=w1, in1=V(t_ap, 1, L),
                                                     op0=ALU.mult, op1=ALU.add), reads=[srcB, BPT, tB], writes=[tB])
        k.op("dve", lambda e: e.scalar_tensor_tensor(out=V(t_ap, 2, L), in0=V(src, 0, L - 2), scalar=w0, in1=V(t_ap, 2, L),
                                                     op0=ALU.mult, op1=ALU.add), reads=[srcB, BPT, tB], writes=[tB])
        k.op("dve", lambda e: e.scalar_tensor_tensor(out=V(t_ap, 0, 1), in0=C(1, 2), scalar=w1, in1=V(t_ap, 0, 1),
                                                     op0=ALU.mult, op1=ALU.add), reads=[carB, BPT, tB], writes=[tB])
        k.op("dve", lambda e: e.scalar_tensor_tensor(out=V(t_ap, 0, 2), in0=C(0, 2), scalar=w0, in1=V(t_ap, 0, 2),
                                                     op0=ALU.mult, op1=ALU.add), reads=[carB, BPT, tB], writes=[tB])
        k.op("dve", lambda e: e.tensor_copy(out=C(0, 2), in_=V(src, L - 2, L)), reads=[srcB, tB], writes=[carB])

    def load_x(tiles_abs, N):
        for ti, ta in enumerate(tiles_abs):
            i = rot("stg", 2)
            k.dma(stg[i][:, :], xin[ta * 128:(ta + 1) * 128, :], writes=[Bstg[i]], sem="d_stg%d" % i)
            for half in range(2):
                pb = palloc()
                for c4 in range(4):
                    c = half * 4 + c4
                    k.op("pe", lambda e: e.transpose(out=PS[pb][:, c4 * 128:(c4 + 1) * 128], in_=stg[i][:, c * 128:(c + 1) * 128],
                                                     identity=ident[:, :]), reads=[Bstg[i], Bc], writes=[BPS[pb]])
                k.op("act", lambda e: e.copy(out=xT[:, half * 4:(half + 1) * 4, ti * 128:(ti + 1) * 128],
                                             in_=PS[pb][:, :].rearrange("p (a b) -> p a b", a=4)),
                     reads=[BPS[pb]], writes=BxT[half * 4:(half + 1) * 4])
                pfree_(pb)

    def store_y(tiles_real, N):
        for ti, to in tiles_real:
            i = rot("stg", 2)
            for half in range(2):
                pb = palloc()
                for c4 in range(4):
                    c = half * 4 + c4
                    k.op("pe", lambda e: e.transpose(out=PS[pb][:, c4 * 128:(c4 + 1) * 128], in_=xT[:, c, ti * 128:(ti + 1) * 128],
                                                     identity=ident[:, :]), reads=[BxT[c], Bc], writes=[BPS[pb]])
                k.op("act", lambda e: e.copy(out=stg[i][:, half * 512:(half + 1) * 512], in_=PS[pb][:, :]),
                     reads=[BPS[pb]], writes=[Bstg[i]])
                pfree_(pb)
            k.dma(y_out[to * 128:(to + 1) * 128, :], stg[i][:, :], reads=[Bstg[i]], sem="d_stg%d" % i)

    def fm_chunk(wv, wb, c0, M, N, evac):
        pb = palloc()
        for kc in range(8):
            mm(PS[pb][0:M, 0:N], wv[:, kc, c0:c0 + M], hT[:, kc, 0:N], kc == 0, kc == 7, [wb, BhT[kc]], [BPS[pb]])
        evac(pb)
        pfree_(pb)

    def even_mixer(mode, tiles, N, gtiles):
        nt = len(tiles)
        pre = mode == "pre"
        if not pre:
            wv, wb = w_get("in0_0")
            for c in range(4):
                fm_chunk(wv, wb, c * 128, 128, N, lambda pb: k.op(
                    "act", lambda e: e.copy(out=qaT[:, c, 0:N], in_=PS[pb][:, 0:N]), reads=[BPS[pb]], writes=[BqaT]))
            w_done()
        wv, wb = (RES["in0_1"][0], Bgated) if pre else w_get("in0_1")
        def ka_evac(pb):
            k.op("act", lambda e: e.copy(out=kaT[:, 128:128 + N], in_=PS[pb][:, 0:N]), reads=[BPS[pb]], writes=[BkaT])
            if (not pre) and gtiles[-1] == NMAIN - 1:
                k.op("act", lambda e: e.copy(out=kf32[:, :], in_=PS[pb][:, N - 128:N]), reads=[BPS[pb]], writes=[Bkf32])
        fm_chunk(wv, wb, 0, 128, N, ka_evac)
        pb = palloc()
        for kc in range(8):
            mm(PS[pb][64:128, 0:N], wv[:, kc, 0:64], hT[:, kc, 0:N], kc == 0, kc == 7, [wb, BhT[kc]], [BPS[pb]], signal=False)
        for kc in range(8):
            mm(PS[pb][0:64, 0:N], wv[:, kc, 64:128], hT[:, kc, 0:N], kc == 0, kc == 7, [wb, BhT[kc]], [BPS[pb]])
        k.op("act", lambda e: e.copy(out=kaT2[:, 128:128 + N], in_=PS[pb][:, 0:N]), reads=[BPS[pb]], writes=[BkaT2])
        pfree_(pb)
        for ti in range(nt):
            pb = palloc()
            for kc in range(8):
                mm(PS[pb][:, 0:128], hT[:, kc, ti * 128:(ti + 1) * 128], wv[:, kc, 128:256], kc == 0, kc == 7, [wb, BhT[kc]], [BPS[pb]])
            k.op("act", lambda e: e.copy(out=vtok[:, 1 + ti, :], in_=PS[pb][:, 0:128]), reads=[BPS[pb]], writes=[Bvtok])
            if (not pre) and gtiles[ti] == NMAIN - 1:
                k.op("act", lambda e: e.copy(out=osm[:, 128:256], in_=PS[pb][:, 0:128]), reads=[BPS[pb]], writes=[Bosm])
            pfree_(pb)
        if not pre:
            for p in range(2):
                fm_chunk(wv, wb, 256 + p * 128, 128, N, lambda pb: k.op(
                    "act", lambda e: e.copy(out=qbT[:, p, 0:N], in_=PS[pb][:, 0:N]), reads=[BPS[pb]], writes=[BqbT]))
        if not pre:
            w_done()
        ck(3)
        if pre:
            wv2, wb2 = RES["in0_2"][0], Bgated
            wv3, wb3 = RES["in0_3"][0], Bgated
            wv4, wb4 = RES["in0_4"][0], Bgated
        else:
            wv2, wb2 = w_get("in0_2")
            for p in range(2):
                fm_chunk(wv2, wb2, p * 128, 128, N, lambda pb: k.op(
                    "act", lambda e: e.copy(out=kbT[:, p, 0:N], in_=PS[pb][:, 0:N]), reads=[BPS[pb]], writes=[BkbT]))
            wv3, wb3 = w_get_ahead("in0_3", 1)
            wv4, wb4 = w_get_ahead("in0_4", 2)
        if not pre:
            for h in range(4):
                src_v, src_b, c0 = (wv3, wb3, 256 + h * 128) if h < 2 else (wv4, wb4, (h - 2) * 128)
                fm_chunk(src_v, src_b, c0, 128, N, lambda pb: k.op(
                    "act", lambda e: e.activation(out=srbT[:, h, 0:N], in_=PS[pb][:, 0:N], func=AF.Silu),
                    reads=[BPS[pb]], writes=[BsrbT]))
        fm_chunk(wv4, wb4, 256, 16, N, lambda pb: k.op(
            "act", lambda e: e.copy(out=glrT[0:16, 0:N], in_=PS[pb][0:16, 0:N]), reads=[BPS[pb]], writes=[BglrT]))

        ck(4)
        for ti in range(nt):
            c0 = ti * 128
            gt = gtiles[ti]
            if not pre:
                for kv in range(2):
                    pscs = [palloc(), palloc()]
                    for kb_ in range(2):
                        kc0 = c0 + kb_ * 128
                        for par in range(2):
                            ksrc, kB = (kaT, BkaT) if kv == par else (kaT2, BkaT2)
                            mm(PS[pscs[par]][:, kb_ * 256:(kb_ + 1) * 256], ksrc[par * 64:(par + 1) * 64, kc0:kc0 + 128],
                               qaT[par * 64:(par + 1) * 64, 2 * kv:2 * kv + 2, c0:c0 + 128], True, True, [kB, BqaT], [BPS[pscs[par]]])
                    for par in range(2):
                        i = rot("sc", 2)
                        bi = kv * 2 + par
                        psc = pscs[par]
                        if gt == 1:
                            k.op("dve", lambda e: e.scalar_tensor_tensor(out=sc[i][:, 0:256], in0=PS[psc][:, 0:256], scalar=flg[:, 1:2],
                                                                         in1=biasT[:, bi, 0:256], op0=ALU.add, op1=ALU.add),
                                 reads=[BPS[psc], Bc], writes=[Bsc[i]])
                            k.op("dve", lambda e: e.tensor_tensor(out=sc[i][:, 256:512], in0=PS[psc][:, 256:512], in1=biasT[:, bi, 256:512],
                                                                  op=ALU.add), reads=[BPS[psc], Bc], writes=[Bsc[i]])
                        else:
                            k.op("dve", lambda e: e.tensor_tensor(out=sc[i][:, :], in0=PS[psc][:, :], in1=biasT[:, bi, :], op=ALU.add),
                                 reads=[BPS[psc], Bc], writes=[Bsc[i]])
                        pfree_(psc)
                        k.op("act", lambda e: e.activation(out=ptT[:, bi, :], in_=sc[i][:, :], func=AF.Exp, scale=0.125),
                             reads=[Bsc[i]], writes=[Bpt[bi]])
            if ti == 1:
                ck(7)
            pg = palloc()
            mm(PS[pg][:, 0:256], glrT[0:32, c0:c0 + 128], wg[0:32, :], True, True, [BglrT, Bwg], [BPS[pg]])
            ck(51)
            k.op("act", lambda e: e.activation(out=l_sb[:, :], in_=PS[pg][:, 0:256], func=AF.Exp, scale=-1.0),
                 reads=[BPS[pg]], writes=[Bl])
            pfree_(pg)
            ck(52)
            k.op("act", lambda e: e.activation(out=l_sb[:, :], in_=l_sb[:, :], func=AF.Ln, bias=epsD[:, 2:3]), reads=[Bl, Bc], writes=[Bl])
            ck(5)
            pr = palloc()
            mm(PS[pr][:, 0:256], urev[:, :], l_sb[:, :], True, True, [Bc, Bl], [BPS[pr]])
            k.op("act", lambda e: e.activation(out=er_sb[:, :], in_=PS[pr][:, 0:256], func=AF.Exp), reads=[BPS[pr]], writes=[Ber])
            pfree_(pr)
            pk = palloc()
            for kc in range(8):
                mm(PS[pk][:, 0:256], hT[:, kc, c0:c0 + 128], wv2[:, kc, 0:256], kc == 0, kc == 7, [wb2, BhT[kc]], [BPS[pk]])
            k.op("dve", lambda e: e.tensor_tensor(out=khat[:, :], in0=PS[pk][:, 0:256], in1=er_sb[:, :], op=ALU.mult),
                 reads=[BPS[pk], Ber], writes=[Bkhat])
            pfree_(pk)
            pv = palloc()
            for kc in range(8):
                mm(PS[pv][:, 0:256], hT[:, kc, c0:c0 + 128], wv2[:, kc, 256:512], kc == 0, kc == 7, [wb2, BhT[kc]], [BPS[pv]], signal=False)
            for kc in range(8):
                mm(PS[pv][:, 256:512], hT[:, kc, c0:c0 + 128], wv3[:, kc, 0:256], kc == 0, kc == 7, [wb3, BhT[kc]], [BPS[pv]])
            k.op("act", lambda e: e.copy(out=vbtok[:, :], in_=PS[pv][:, :]), reads=[BPS[pv]], writes=[Bvbtok])
            pfree_(pv)
            ck(6)
            pbk = palloc()
            for p in range(2):
                mm(PS[pbk][:, p * 128:(p + 1) * 128], l_sb[:, p * 128:(p + 1) * 128], ucum[:, :], True, True, [Bl, Bc], [BPS[pbk]])
            k.op("act", lambda e: e.activation(out=e1_sb[:, :], in_=PS[pbk][:, 0:256], func=AF.Exp), reads=[BPS[pbk]], writes=[Be1])
            if not pre:
                k.op("act", lambda e: e.activation(out=e2_sb[:, :], in_=PS[pbk][:, 0:256], func=AF.Exp, scale=-1.0),
                     reads=[BPS[pbk]], writes=[Be2])
            pfree_(pbk)
            if not pre:
                k.op("dve", lambda e: e.scalar_tensor_tensor(
                    out=qtl[:, :, :], in0=qbT[:, :, c0:c0 + 128], scalar=0.125,
                    in1=e1_sb[:, :].rearrange("p (a b) -> p a b", a=2), op0=ALU.mult, op1=ALU.mult),
                    reads=[BqbT, Be1], writes=[Bqtl])
                k.op("dve", lambda e: e.tensor_tensor(
                    out=ktl[:, :, :], in0=kbT[:, :, c0:c0 + 128], in1=e2_sb[:, :].rearrange("p (a b) -> p a b", a=2),
                    op=ALU.mult), reads=[BkbT, Be2], writes=[Bktl])
                po = palloc()
            pdc = [palloc(), palloc()]
            for h in range(4):
                p, hp = h // 2, (h % 2) * 64
                if not pre:
                    pa = palloc()
                    mm(PS[pa][:, 0:128], ktl[hp:hp + 64, p, :], qtl[hp:hp + 64, p, :], True, True, [Bktl, Bqtl], [BPS[pa]])
                    ia = rot("atm", 2)
                    k.op("dve", lambda e: e.tensor_tensor(out=atm[ia][:, :], in0=PS[pa][:, 0:128], in1=maska[:, :], op=ALU.mult),
                         reads=[BPS[pa], Bc], writes=[Batm[ia]])
                    pfree_(pa)
                    mm(PS[po][:, h * 128:(h + 1) * 128], vbtok[:, h * 128:(h + 1) * 128], atm[ia][:, :], h == 0, False,
                       [Bvbtok, Batm[ia]], [BPS[po]], signal=False, skip=True)
                for c in range(2):
                    mm(PS[pdc[c]][hp:hp + 64, p * 128:(p + 1) * 128], khat[c * 64:(c + 1) * 64, p * 128 + hp:p * 128 + hp + 64],
                       vbtok[c * 64:(c + 1) * 64, h * 128:(h + 1) * 128], True, True, [Bkhat, Bvbtok], [BPS[pdc[c]]])
            for p in range(2):
                for c in range(2):
                    if not pre:
                        for half in range(2):
                            h = p * 2 + half
                            hp = half * 64
                            mm(PS[po][:, h * 128 + c * 64:h * 128 + (c + 1) * 64], S_b[:, h, c, :],
                               qtl[:, p, c * 64:(c + 1) * 64], False, c == 1, [BSb[p][c], Bqtl], [BPS[po]], skip=True)
                    dcol = p * 128 + c * 64 + 63
                    k.op("dve", lambda e: e.scalar_tensor_tensor(
                        out=S_f[:, p, :], in0=S_f[:, p, :], scalar=e1_sb[:, dcol:dcol + 1],
                        in1=PS[pdc[c]][:, p * 128:(p + 1) * 128], op0=ALU.mult, op1=ALU.add),
                        reads=[BS, Be1, BPS[pdc[c]]], writes=[BS])
                    nslot = (c + 1) % 2
                    for half in range(2):
                        hp = half * 64
                        k.op("act", lambda e: e.copy(out=S_b[hp:hp + 64, p * 2 + half, nslot, :], in_=S_f[hp:hp + 64, p, :]),
                             reads=[BS], writes=[BSb[p][nslot]])
            pfree_(pdc[0]); pfree_(pdc[1])
            if pre:
                continue
            ck(10)
            k.op("act", lambda e: e.activation(out=osq[:, :], in_=PS[po][:, :], func=AF.Square), reads=[BPS[po]], writes=[Bosq])
            ps2 = palloc()
            mm(PS[ps2][:, :], ones[:, :], osq[:, :], True, True, [Bc, Bosq], [BPS[ps2]])
            k.op("act", lambda e: e.activation(out=orstd[:, :], in_=PS[ps2][:, :], func=AF.Ln, bias=epsD[:, 1:2]),
                 reads=[BPS[ps2], Bc], writes=[Borstd])
            k.op("act", lambda e: e.activation(out=orstd[:, :], in_=orstd[:, :], func=AF.Exp, scale=-0.5), reads=[Borstd], writes=[Borstd])
            pfree_(ps2)
            for h in range(4):
                col = PC_GN + h
                k.op("dve", lambda e: e.scalar_tensor_tensor(
                    out=otmp[:, h * 128:(h + 1) * 128], in0=PS[po][:, h * 128:(h + 1) * 128], scalar=PT[:, col:col + 1],
                    in1=orstd[:, h * 128:(h + 1) * 128], op0=ALU.mult, op1=ALU.mult), reads=[BPS[po], BPT, Borstd], writes=[Botmp])
            pfree_(po)
            k.op("pool", lambda e: e.tensor_tensor(out=mixT[:, 4:8, c0:c0 + 128], in0=otmp[:, :].rearrange("p (a b) -> p a b", a=4),
                                                   in1=srbT[:, :, c0:c0 + 128], op=ALU.mult), reads=[Botmp, BsrbT], writes=[BmixT])
            ck(11)
            ck(12)
            poa = palloc()
            pdn = palloc()
            for kv in range(2):
                for par in range(2):
                    for kb_ in range(2):
                        bi = kv * 2 + par
                        rhs = ptT[:, bi, kb_ * 256:(kb_ + 1) * 256]
                        outo = PS[poa][par * 64:(par + 1) * 64, 2 * kv * 128:(2 * kv + 2) * 128]
                        outd = PS[pdn][par * 64:(par + 1) * 64, 2 * kv * 128:(2 * kv + 2) * 128]
                        mm(outo, vtok[:, ti + kb_, kv * 64:(kv + 1) * 64], rhs, kb_ == 0, kb_ == 1, [Bvtok, Bpt[bi]], [BPS[poa]], signal=False)
                        mm(outd, ones[:, 0:64], rhs, kb_ == 0, kb_ == 1, [Bc, Bpt[bi]], [BPS[pdn]], signal=(kb_ == 1))
            k.op("dve", lambda e: e.tensor_tensor(out=rec[:, :].rearrange("p (c q) -> p c q", c=4),
                                                  in0=PS[pdn][:, :].rearrange("p (c q) -> p c q", c=4),
                                                  in1=sinkE[:, :].unsqueeze(2).to_broadcast([128, 4, 128]), op=ALU.add),
                 reads=[BPS[pdn], BsinkE], writes=[Brec])
            pfree_(pdn)
            k.op("dve", lambda e: e.reciprocal(out=rec[:, :], in_=rec[:, :]), reads=[Brec], writes=[Brec])
            k.op("dve", lambda e: e.tensor_tensor(out=mixT[:, 0:4, c0:c0 + 128], in0=PS[poa][:, :].rearrange("p (c q) -> p c q", c=4),
                                                  in1=rec[:, :].rearrange("p (c q) -> p c q", c=4), op=ALU.mult),
                 reads=[BPS[poa], Brec], writes=[BmixT])
            pfree_(poa)
        if not pre:
            w_done(); w_done(); w_done()
        if (not pre) and gtiles[-1] == NMAIN - 1:
            pb = palloc()
            k.op("pe", lambda e: e.transpose(out=PS[pb][:, 0:128], in_=kf32[:, :], identity=ident[:, :]),
                 reads=[Bkf32, Bc], writes=[BPS[pb]])
            k.op("act", lambda e: e.copy(out=osm[:, 0:128], in_=PS[pb][:, 0:128]), reads=[BPS[pb]], writes=[Bosm])
            pfree_(pb)
            k.dma(k_out[:, :], osm[:, 0:128], reads=[Bosm], sem="d_mk")
            k.dma(v_out[:, :], osm[:, 128:256], reads=[Bosm], sem="d_mv")
            k.dma(g_out.rearrange("(p h) d v -> (h d) p v", h=2), S_f[:, :, :], reads=[BS], sem="d_mg")
        k.op("dve", lambda e: e.tensor_copy(out=kaT[:, 0:128], in_=kaT[:, nt * 128:(nt + 1) * 128]), reads=[BkaT], writes=[BkaT])
        k.op("dve", lambda e: e.tensor_copy(out=kaT2[:, 0:128], in_=kaT2[:, nt * 128:(nt + 1) * 128]), reads=[BkaT2], writes=[BkaT2])
        k.op("dve", lambda e: e.tensor_copy(out=vtok[:, 0, :], in_=vtok[:, nt, :]), reads=[Bvtok], writes=[Bvtok])

    def odd_mixer(N, last, seg=False):
        wvs = {}
        for c in range(8):
            chunk_aps = []
            for q in range(3):
                idx = c * 3 + q
                bj, bq = idx // 4, idx % 4
                if bj not in wvs:
                    wvs[bj] = w_get_ahead("in1_%d" % bj, len(wvs))
                chunk_aps.append((wvs[bj][0], wvs[bj][1], bq * 128, bj))
            i = rot("cu", 2)
            wv, wb, c0, bj = chunk_aps[0]
            pcg = palloc()
            for kc in range(8):
                mm(PS[pcg][:, 0:N], wv[:, kc, c0:c0 + 128], hT[:, kc, 0:N], kc == 0, kc == 7, [wb, BhT[kc]], [BPS[pcg]])
            wv, wb, c0, bj = chunk_aps[1]
            pu = palloc()
            for kc in range(8):
                mm(PS[pu][:, 0:N], wv[:, kc, c0:c0 + 128], hT[:, kc, 0:N], kc == 0, kc == 7, [wb, BhT[kc]], [BPS[pu]])
            k.op("act", lambda e: e.copy(out=ucp[i][:, 0:N], in_=PS[pu][:, 0:N]), reads=[BPS[pu]], writes=[Bucp[i]])
            pfree_(pu)
            k.op("dve", lambda e: e.tensor_tensor(out=cu[i][:, 0:N], in0=PS[pcg][:, 0:N], in1=ucp[i][:, 0:N], op=ALU.mult),
                 reads=[BPS[pcg], Bucp[i]], writes=[Bcu[i]])
            pfree_(pcg)
            w0 = PT[:, PC_CW + 0 * 8 + c:PC_CW + 0 * 8 + c + 1]
            w1 = PT[:, PC_CW + 1 * 8 + c:PC_CW + 1 * 8 + c + 1]
            w2 = PT[:, PC_CW + 2 * 8 + c:PC_CW + 2 * 8 + c + 1]
            conv_taps(zt[i], Bzt[i], cu[i], Bcu[i], (ccs[:, c, :, :] if seg else ccar[:, c, :]), (Bccs if seg else Bccar), w0, w1, w2, None, N, seg)
            wv, wb, c0, bj = chunk_aps[2]
            pbg = palloc()
            for kc in range(8):
                mm(PS[pbg][:, 0:N], wv[:, kc, c0:c0 + 128], hT[:, kc, 0:N], kc == 0, kc == 7, [wb, BhT[kc]], [BPS[pbg]])
            k.op("dve", lambda e: e.tensor_tensor(out=mixT[:, c, 0:N], in0=PS[pbg][:, 0:N], in1=zt[i][:, 0:N], op=ALU.mult),
                 reads=[BPS[pbg], Bzt[i]], writes=[BmixT])
            pfree_(pbg)
            done_upto = (c * 3 + 3) // 4
            for bj in sorted(list(wvs.keys())):
                if bj < done_upto:
                    del wvs[bj]
                    w_done()
        assert not wvs

    def ffn(l, N, seg=False):
        pend = []

        def flush():
            while pend:
                i_, c_ = pend.pop(0)
                k.op("act", lambda e: e.activation(out=tg[i_][:, 0:N], in_=tg[i_][:, 0:N], func=AF.Gelu_apprx_tanh),
                     reads=[Btg[i_]], writes=[Btg[i_]])
                k.op("pool", lambda e: e.tensor_tensor(out=gated[:, c_, 0:N], in0=tg[i_][:, 0:N], in1=tv[i_][:, 0:N], op=ALU.mult),
                     reads=[Btg[i_], Btv[i_]], writes=[Bgated])

        for j in range(11):
            wv, wb = w_get("up%d_%d" % (l, j))
            for jj in range(2):
                c = 2 * j + jj
                i = rot("tg", 2)
                pgt = palloc()
                for kc in range(8):
                    mm(PS[pgt][:, 0:N], wv[:, kc, jj * 128:(jj + 1) * 128], hT[:, kc, 0:N], kc == 0, kc == 7, [wb, BhT[kc]], [BPS[pgt]])
                pvl = palloc()
                for kc in range(8):
                    mm(PS[pvl][:, 0:N], wv[:, kc, 256 + jj * 128:256 + (jj + 1) * 128], hT[:, kc, 0:N], kc == 0, kc == 7, [wb, BhT[kc]], [BPS[pvl]])
                first = True
                for (pp, tt, tB, cc) in ((pgt, tg[i], Btg[i], c), (pvl, tv[i], Btv[i], 22 + c)):
                    w0 = PT[:, PC_FW + (l * 3 + 0) * 44 + cc:PC_FW + (l * 3 + 0) * 44 + cc + 1]
                    w1 = PT[:, PC_FW + (l * 3 + 1) * 44 + cc:PC_FW + (l * 3 + 1) * 44 + cc + 1]
                    w2 = PT[:, PC_FW + (l * 3 + 2) * 44 + cc:PC_FW + (l * 3 + 2) * 44 + cc + 1]
                    bb = PT[:, PC_FB + l * 44 + cc:PC_FB + l * 44 + cc + 1]
                    conv_taps(tt, tB, PS[pp], BPS[pp], (fcs[:, cc, :, :] if seg else fcar[:, l, cc, :]), (Bfcs if seg else Bfcar), w0, w1, w2, bb, N, seg,
                              after_act=(flush if first else None))
                    first = False
                    pfree_(pp)
                pend.append((i, c))
            w_done()
        flush()
        proj_out(["dn%d_%d" % (l, m) for m in range(8)], gated, Bgated, 22, N, 1)

    def state_rows_out(src_cols, ncol, dst_rows_ap, stage_off):
        done = 0
        while done < ncol:
            n = min(4, ncol - done)
            pb = palloc()
            for q in range(n):
                ap_, bf_ = src_cols[done + q]
                k.op("pe", lambda e: e.transpose(out=PS[pb][0:2, q * 128:(q + 1) * 128], in_=ap_, identity=ident[:, :]),
                     reads=[bf_, Bc], writes=[BPS[pb]])
            k.op("act", lambda e: e.copy(out=rowst[:, stage_off + done * 128:stage_off + (done + n) * 128], in_=PS[pb][0:2, 0:n * 128]),
                 reads=[BPS[pb]], writes=[BmT])
            pfree_(pb)
            done += n


    def sample_unit():
        N = 64
        KcF = biasT[:, 0:2, :].rearrange("p a (s c) -> p (a s) c", c=128)
        biasS = biasT[:, 2:4, :]
        KcT = ptT[:, :, :].rearrange("p a (s c) -> p (a s) c", c=128)
        KcT2 = stg[1].bitcast(BF16)[:, :].rearrange("p (s c) -> p s c", c=128)
        Vc = gated[:, 16:20, :].rearrange("p a (s c) -> p (a s) c", c=128)
        S0bz = gated[:, 0:16, :].rearrange("p s (h v) -> p s h v", h=4)
        S0f = mT[:, :, :].rearrange("p a (s q v) -> p (a s) q v", s=2, q=2)
        khm = hT[0:64, :, :].rearrange("p a (s c) -> p (a s) c", c=256)
        Pc, BPc = osq, Bosq
        Pn, BPn = sq[0], Bsq[0]
        scC, BscC = sc[1], Bsc[1]
        scN, BscN = tmpn[0], Btmpn[0]
        BK2 = Bstg[1]
        k.dma(stg[0][0:64, :], xs[:, :], writes=[Bstg[0]], sem="d_stg0")
        k.dma(stg[0][64:96, :], scs[:, :], writes=[Bstg[0]], sem="d_stg0")
        k.dma(biasS, c_bias_s[:, :, :], writes=[Bc], sem="d_s1")
        for q in range(4):
            k.dma(Vc[:, q * 4:(q + 1) * 4, :], scv[q * 4:(q + 1) * 4].rearrange("s t c -> t s c"), writes=[Bgated], sem="d_s2", q="pool")
        k.op("pool", lambda e: e.memset(gated[:, 0:16, :], 0.0), writes=[Bgated])
        for sq_ in range(16):
            k.dma(S0f[:, sq_, :, :], sgl[sq_].rearrange("(p h) d v -> (h d) p v", h=2), writes=[BmT], sem="d_s3")
        for half in range(2):
            pb = palloc()
            for c4 in range(4):
                c = half * 4 + c4
                k.op("pe", lambda e: e.transpose(out=PS[pb][:, c4 * 64:(c4 + 1) * 64], in_=stg[0][0:64, c * 128:(c + 1) * 128],
                                                 identity=ident[0:64, 0:64]), reads=[Bstg[0], Bc], writes=[BPS[pb]])
            k.op("act", lambda e: e.copy(out=xT[:, half * 4:(half + 1) * 4, 0:64], in_=PS[pb][:, 0:256].rearrange("p (a b) -> p a b", a=4)),
                 reads=[BPS[pb]], writes=[BxT])
            pfree_(pb)
        pb = palloc()
        for c in range(8):
            k.op("pe", lambda e: e.transpose(out=PS[pb][:, c * 32:(c + 1) * 32], in_=stg[0][64:96, c * 128:(c + 1) * 128],
                                             identity=ident[64:96, 64:96]), reads=[Bstg[0], Bc], writes=[BPS[pb]])
        k.op("act", lambda e: e.copy(out=ccs[:, :, :, :].rearrange("p c s r -> p c (s r)"), in_=PS[pb][:, 0:256].rearrange("p (c x) -> p c x", c=8)),
             reads=[BPS[pb]], writes=[Bccs])
        pfree_(pb)
        for hf in range(2):
            k.dma(KcF, sck[hf * 8:(hf + 1) * 8].rearrange("s t c -> t s c"), writes=[Bc], sem="d_s4")
            for s8 in range(8):
                sq_ = hf * 8 + s8
                pb = palloc()
                k.op("pe", lambda e: e.transpose(out=PS[pb][:, 0:128], in_=KcF[:, s8, :], identity=ident[:, :]), reads=[Bc], writes=[BPS[pb]])
                mm(PS[pb][64:128, 128:256], KcF[:, s8, 0:64], ident[:, :], True, True, [Bc], [BPS[pb]])
                mm(PS[pb][0:64, 128:256], KcF[:, s8, 64:128], ident[:, :], True, True, [Bc], [BPS[pb]])
                k.op("act", lambda e: e.copy(out=KcT[:, sq_, :], in_=PS[pb][:, 0:128]), reads=[BPS[pb]], writes=[Bpt[0]])
                k.op("act", lambda e: e.copy(out=KcT2[:, sq_, :], in_=PS[pb][:, 128:256]), reads=[BPS[pb]], writes=[BK2])
                pfree_(pb)
        k.dma(ks_out[:, 0:124, :], sck[:, 4:128, :], sem="d_so")
        k.dma(vs_out[:, 0:124, :], scv[:, 4:128, :], sem="d_so")

        pre_norm(0, 0, N)
        wv, wb = w_get("in0_0")
        for c in range(4):
            fm_chunk(wv, wb, c * 128, 128, N, lambda pb: k.op(
                "act", lambda e: e.copy(out=qaT[:, c, 0:N], in_=PS[pb][:, 0:N]), reads=[BPS[pb]], writes=[BqaT]))
        w_done()
        wv, wb = w_get("in0_1")

        def ka_evac(pb):
            k.op("act", lambda e: e.copy(out=kaT[:, 0:N], in_=PS[pb][:, 0:N]), reads=[BPS[pb]], writes=[BkaT])
            k.op("act", lambda e: e.copy(out=kf32[:, 0:N], in_=PS[pb][:, 0:N]), reads=[BPS[pb]], writes=[Bkf32])
        fm_chunk(wv, wb, 0, 128, N, ka_evac)
        pb = palloc()
        for kc in range(8):
            mm(PS[pb][64:128, 0:N], wv[:, kc, 0:64], hT[:, kc, 0:N], kc == 0, kc == 7, [wb, BhT[kc]], [BPS[pb]], signal=False)
        for kc in range(8):
            mm(PS[pb][0:64, 0:N], wv[:, kc, 64:128], hT[:, kc, 0:N], kc == 0, kc == 7, [wb, BhT[kc]], [BPS[pb]])
        k.op("act", lambda e: e.copy(out=kaT2[:, 0:N], in_=PS[pb][:, 0:N]), reads=[BPS[pb]], writes=[BkaT2])
        pfree_(pb)
        pb = palloc()
        for kc in range(8):
            mm(PS[pb][0:64, 0:128], hT[:, kc, 0:64], wv[:, kc, 128:256], kc == 0, kc == 7, [wb, BhT[kc]], [BPS[pb]])
        k.op("act", lambda e: e.copy(out=vtok[0:64, 1, :], in_=PS[pb][0:64, 0:128]), reads=[BPS[pb]], writes=[Bvtok])
        k.op("act", lambda e: e.copy(out=osm[0:64, 128:256], in_=PS[pb][0:64, 0:128]), reads=[BPS[pb]], writes=[Bosm])
        pfree_(pb)
        for p in range(2):
            fm_chunk(wv, wb, 256 + p * 128, 128, N, lambda pb: k.op(
                "act", lambda e: e.copy(out=qbT[:, p, 0:N], in_=PS[pb][:, 0:N]), reads=[BPS[pb]], writes=[BqbT]))
        w_done()
        wv2, wb2 = w_get("in0_2")
        for p in range(2):
            fm_chunk(wv2, wb2, p * 128, 128, N, lambda pb: k.op(
                "act", lambda e: e.copy(out=kbT[:, p, 0:N], in_=PS[pb][:, 0:N]), reads=[BPS[pb]], writes=[BkbT]))
        wv3, wb3 = w_get_ahead("in0_3", 1)
        wv4, wb4 = w_get_ahead("in0_4", 2)
        for h in range(4):
            src_v, src_b, c0 = (wv3, wb3, 256 + h * 128) if h < 2 else (wv4, wb4, (h - 2) * 128)
            fm_chunk(src_v, src_b, c0, 128, N, lambda pb: k.op(
                "act", lambda e: e.activation(out=srbT[:, h, 0:N], in_=PS[pb][:, 0:N], func=AF.Silu), reads=[BPS[pb]], writes=[BsrbT]))
        fm_chunk(wv4, wb4, 256, 16, N, lambda pb: k.op(
            "act", lambda e: e.copy(out=glrT[0:16, 0:N], in_=PS[pb][0:16, 0:N]), reads=[BPS[pb]], writes=[BglrT]))
        pb = palloc()
        k.op("pe", lambda e: e.transpose(out=PS[pb][0:64, 0:128], in_=kf32[:, 0:64], identity=ident[:, :]), reads=[Bkf32, Bc], writes=[BPS[pb]])
        k.op("act", lambda e: e.copy(out=osm[0:64, 0:128], in_=PS[pb][0:64, 0:128]), reads=[BPS[pb]], writes=[Bosm])
        pfree_(pb)
        for i4 in range(4):
            k.dma(ks_out[:, 124 + i4, :], osm[i4:64:4, 0:128], reads=[Bosm], sem="d_so")
            k.dma(vs_out[:, 124 + i4, :], osm[i4:64:4, 128:256], reads=[Bosm], sem="d_so")

        pg = palloc()
        mm(PS[pg][0:64, 0:256], glrT[0:32, 0:64], wg[0:32, :], True, True, [BglrT, Bwg], [BPS[pg]])
        k.op("act", lambda e: e.activation(out=l_sb[0:64, :], in_=PS[pg][0:64, 0:256], func=AF.Exp, scale=-1.0), reads=[BPS[pg]], writes=[Bl])
        pfree_(pg)
        k.op("act", lambda e: e.activation(out=l_sb[0:64, :], in_=l_sb[0:64, :], func=AF.Ln, bias=epsD[0:64, 2:3]), reads=[Bl, Bc], writes=[Bl])
        pr = palloc()
        mm(PS[pr][0:64, 0:256], urev_s[:, :], l_sb[0:64, :], True, True, [Bc, Bl], [BPS[pr]])
        k.op("act", lambda e: e.activation(out=er_sb[0:64, :], in_=PS[pr][0:64, 0:256], func=AF.Exp), reads=[BPS[pr]], writes=[Ber])
        pfree_(pr)
        pk = palloc()
        for kc in range(8):
            mm(PS[pk][0:64, 0:256], hT[:, kc, 0:64], wv2[:, kc, 0:256], kc == 0, kc == 7, [wb2, BhT[kc]], [BPS[pk]])
        k.op("dve", lambda e: e.tensor_tensor(out=khat[0:64, :], in0=PS[pk][0:64, 0:256], in1=er_sb[0:64, :], op=ALU.mult),
             reads=[BPS[pk], Ber], writes=[Bkhat])
        pfree_(pk)
        pv = palloc()
        for kc in range(8):
            mm(PS[pv][0:64, 0:256], hT[:, kc, 0:64], wv2[:, kc, 256:512], kc == 0, kc == 7, [wb2, BhT[kc]], [BPS[pv]], signal=False)
        for kc in range(8):
            mm(PS[pv][0:64, 256:512], hT[:, kc, 0:64], wv3[:, kc, 0:256], kc == 0, kc == 7, [wb3, BhT[kc]], [BPS[pv]])
        k.op("act", lambda e: e.copy(out=vbtok[0:64, :], in_=PS[pv][0:64, :]), reads=[BPS[pv]], writes=[Bvbtok])
        pfree_(pv)
        w_done(); w_done(); w_done()
        pbk = palloc()
        for p in range(2):
            mm(PS[pbk][:, p * 64:(p + 1) * 64], l_sb[0:64, p * 128:(p + 1) * 128], ucum_s[:, :], True, True, [Bl, Bc], [BPS[pbk]])
        k.op("act", lambda e: e.activation(out=e1_sb[:, 0:128], in_=PS[pbk][:, 0:128], func=AF.Exp), reads=[BPS[pbk]], writes=[Be1])
        k.op("act", lambda e: e.activation(out=e2_sb[:, 0:128], in_=PS[pbk][:, 0:128], func=AF.Exp, scale=-1.0), reads=[BPS[pbk]], writes=[Be2])
        pfree_(pbk)
        k.op("dve", lambda e: e.scalar_tensor_tensor(out=qtl[:, :, 0:64], in0=qbT[:, :, 0:64], scalar=0.125,
                                                     in1=e1_sb[:, 0:128].rearrange("p (a b) -> p a b", a=2), op0=ALU.mult, op1=ALU.mult),
             reads=[BqbT, Be1], writes=[Bqtl])
        k.op("dve", lambda e: e.tensor_tensor(out=ktl[:, :, 0:64], in0=kbT[:, :, 0:64], in1=e2_sb[:, 0:128].rearrange("p (a b) -> p a b", a=2),
                                              op=ALU.mult), reads=[BkbT, Be2], writes=[Bktl])
        for p in range(2):
            for half in range(2):
                hp = half * 64
                k.op("act", lambda e: e.copy(out=S0bz[hp:hp + 64, :, p * 2 + half, :], in_=S0f[hp:hp + 64, :, p, :]), reads=[BmT], writes=[Bgated])
        k.op("dve", lambda e: e.tensor_tensor(out=khm, in0=khat[0:64, :].unsqueeze(1).to_broadcast([64, 16, 256]),
                                              in1=rowmask[:, :].unsqueeze(2).to_broadcast([64, 16, 256]), op=ALU.mult),
             reads=[Bkhat, Bc], writes=[BhT])
        po = palloc()
        for h in range(4):
            p, hp = h // 2, (h % 2) * 64
            pa = palloc()
            mm(PS[pa][0:64, 0:64], ktl[hp:hp + 64, p, 0:64], qtl[hp:hp + 64, p, 0:64], True, True, [Bktl, Bqtl], [BPS[pa]])
            ia = rot("atm", 2)
            k.op("dve", lambda e: e.tensor_tensor(out=atm[ia][0:64, 0:64], in0=PS[pa][0:64, 0:64], in1=maska_s[:, :], op=ALU.mult),
                 reads=[BPS[pa], Bc], writes=[Batm[ia]])
            pfree_(pa)
            mm(PS[po][:, h * 64:(h + 1) * 64], vbtok[0:64, h * 128:(h + 1) * 128], atm[ia][0:64, 0:64], h == 0, False,
               [Bvbtok, Batm[ia]], [BPS[po]], signal=False, skip=True)
        for h in range(4):
            p = h // 2
            for sq_ in range(16):
                mm(PS[po][:, h * 64 + sq_ * 4:h * 64 + sq_ * 4 + 4], S0bz[:, sq_, h, :], qtl[:, p, sq_ * 4:sq_ * 4 + 4], False,
                   (h == 3 and sq_ == 15), [Bgated, Bqtl], [BPS[po]], signal=(sq_ == 15), skip=True)
        for r in range(8):
            pd = palloc()
            for s2 in range(2):
                sq_ = r * 2 + s2
                for h in range(4):
                    p, hp = h // 2, (h % 2) * 64
                    mm(PS[pd][hp:hp + 64, (s2 * 2 + p) * 128:(s2 * 2 + p + 1) * 128], khm[:, sq_, p * 128 + hp:p * 128 + hp + 64],
                       vbtok[0:64, h * 128:(h + 1) * 128], True, True, [BhT, Bvbtok], [BPS[pd]])
            for s2 in range(2):
                sq_ = r * 2 + s2
                for p in range(2):
                    dcol = p * 64 + sq_ * 4 + 3
                    k.op("dve", lambda e: e.scalar_tensor_tensor(
                        out=S0f[:, sq_, p, :], in0=S0f[:, sq_, p, :], scalar=e1_sb[:, dcol:dcol + 1],
                        in1=PS[pd][:, (s2 * 2 + p) * 128:(s2 * 2 + p + 1) * 128], op0=ALU.mult, op1=ALU.add),
                        reads=[BmT, Be1, BPS[pd]], writes=[BmT])
            pfree_(pd)
        for sq_ in range(16):
            k.dma(gs_out[sq_].rearrange("(p h) d v -> (h d) p v", h=2), S0f[:, sq_, :, :], reads=[BmT], sem="d_s6")
        k.op("act", lambda e: e.activation(out=osq[:, 0:256], in_=PS[po][:, 0:256], func=AF.Square), reads=[BPS[po]], writes=[Bosq])
        ps2 = palloc()
        mm(PS[ps2][:, 0:256], ones[:, :], osq[:, 0:256], True, True, [Bc, Bosq], [BPS[ps2]])
        k.op("act", lambda e: e.activation(out=orstd[:, 0:256], in_=PS[ps2][:, 0:256], func=AF.Ln, bias=epsD[:, 1:2]), reads=[BPS[ps2], Bc], writes=[Borstd])
        k.op("act", lambda e: e.activation(out=orstd[:, 0:256], in_=orstd[:, 0:256], func=AF.Exp, scale=-0.5), reads=[Borstd], writes=[Borstd])
        pfree_(ps2)
        for h in range(4):
            col = PC_GN + h
            k.op("dve", lambda e: e.scalar_tensor_tensor(out=otmp[:, h * 64:(h + 1) * 64], in0=PS[po][:, h * 64:(h + 1) * 64], scalar=PT[:, col:col + 1],
                                                         in1=orstd[:, h * 64:(h + 1) * 64], op0=ALU.mult, op1=ALU.mult),
                 reads=[BPS[po], BPT, Borstd], writes=[Botmp])
        pfree_(po)
        k.op("pool", lambda e: e.tensor_tensor(out=mixT[:, 4:8, 0:64], in0=otmp[:, 0:256].rearrange("p (a b) -> p a b", a=4),
                                               in1=srbT[:, :, 0:64], op=ALU.mult), reads=[Botmp, BsrbT], writes=[BmixT])

        scb = [[palloc(), palloc()], [palloc(), palloc()]]
        for par in range(2):
            for kv in range(2):
                ksrc, kB = (KcT, Bpt[0]) if kv == par else (KcT2, BK2)
                nsrc, nB = (kaT, BkaT) if kv == par else (kaT2, BkaT2)
                for gi in range(2):
                    for sq_ in range(16):
                        cc0 = kv * 128 + gi * 64 + sq_ * 4
                        mm(PS[scb[0][par]][:, cc0:cc0 + 4], ksrc[par * 64:(par + 1) * 64, sq_, :],
                           qaT[par * 64:(par + 1) * 64, 2 * kv + gi, sq_ * 4:sq_ * 4 + 4], True, True, [kB, BqaT], [BPS[scb[0][par]]],
                           signal=(sq_ == 15))
                mm(PS[scb[1][par]][0:64, kv * 128:(kv + 1) * 128], nsrc[par * 64:(par + 1) * 64, 0:64],
                   qaT[par * 64:(par + 1) * 64, 2 * kv:2 * kv + 2, 0:64], True, True, [nB, BqaT], [BPS[scb[1][par]]])
        for par in range(2):
            k.op("dve", lambda e: e.tensor_tensor(out=scC[:, par * 256:(par + 1) * 256], in0=PS[scb[0][par]][:, 0:256],
                                                  in1=biasS[:, 0, par * 256:(par + 1) * 256], op=ALU.add), reads=[BPS[scb[0][par]], Bc], writes=[BscC])
            k.op("dve", lambda e: e.tensor_tensor(out=scN[0:64, par * 256:(par + 1) * 256], in0=PS[scb[1][par]][0:64, 0:256],
                                                  in1=biasS[0:64, 1, par * 256:(par + 1) * 256], op=ALU.add), reads=[BPS[scb[1][par]], Bc], writes=[BscN])
            pfree_(scb[0][par]); pfree_(scb[1][par])
        k.op("act", lambda e: e.activation(out=Pc[:, :], in_=scC[:, :], func=AF.Exp, scale=0.125), reads=[BscC], writes=[BPc])
        k.op("act", lambda e: e.activation(out=Pn[0:64, :], in_=scN[0:64, :], func=AF.Exp, scale=0.125), reads=[BscN], writes=[BPn])
        poa = palloc()
        pdn = palloc()
        for par in range(2):
            first = True
            for kv in range(2):
                for gi in range(2):
                    for sq_ in range(16):
                        cc0 = kv * 128 + gi * 64 + sq_ * 4
                        oc0 = (2 * kv + gi) * 64 + sq_ * 4
                        rhs = Pc[:, par * 256 + cc0:par * 256 + cc0 + 4]
                        mm(PS[poa][par * 64:(par + 1) * 64, oc0:oc0 + 4], Vc[:, sq_, kv * 64:(kv + 1) * 64], rhs, first, False,
                           [Bgated, BPc], [BPS[poa]], signal=False, skip=True)
                        mm(PS[pdn][par * 64:(par + 1) * 64, oc0:oc0 + 4], ones[:, 0:64], rhs, first, False,
                           [Bc, BPc], [BPS[pdn]], signal=False, skip=True)
                        first = False
                rhs = Pn[0:64, par * 256 + kv * 128:par * 256 + (kv + 1) * 128]
                mm(PS[poa][par * 64:(par + 1) * 64, 2 * kv * 64:(2 * kv + 2) * 64], vtok[0:64, 1, kv * 64:(kv + 1) * 64], rhs, False, True,
                   [Bvtok, BPn], [BPS[poa]], signal=True, skip=True)
                mm(PS[pdn][par * 64:(par + 1) * 64, 2 * kv * 64:(2 * kv + 2) * 64], ones[0:64, 0:64], rhs, False, True,
                   [Bc, BPn], [BPS[pdn]], signal=True, skip=True)
        k.op("dve", lambda e: e.tensor_tensor(out=rec[:, 0:256].rearrange("p (c q) -> p c q", c=4),
                                              in0=PS[pdn][:, 0:256].rearrange("p (c q) -> p c q", c=4),
                                              in1=sinkE[:, :].unsqueeze(2).to_broadcast([128, 4, 64]), op=ALU.add),
             reads=[BPS[pdn], BsinkE], writes=[Brec])
        pfree_(pdn)
        k.op("dve", lambda e: e.reciprocal(out=rec[:, 0:256], in_=rec[:, 0:256]), reads=[Brec], writes=[Brec])
        k.op("dve", lambda e: e.tensor_tensor(out=mixT[:, 0:4, 0:64], in0=PS[poa][:, 0:256].rearrange("p (c q) -> p c q", c=4),
                                              in1=rec[:, 0:256].rearrange("p (c q) -> p c q", c=4), op=ALU.mult),
             reads=[BPS[poa], Brec], writes=[BmixT])
        pfree_(poa)

        proj_out(["out0_0", "out0_1"], mixT, BmixT, 8, N, 4)
        post_norm_add(1, 0, N)

        def ffn_state_in(l):
            for (c0_, w_) in ((0, 4096), (4096, 1536)):
                stage = mT[0:32, :, :].rearrange("p a b -> p (a b)")
                k.dma(stage[:, 0:w_], sfs[l, :, c0_:c0_ + w_], writes=[BmT], sem="d_s5")
                for q0 in range(0, w_ // 128, 16):
                    n = min(16, w_ // 128 - q0)
                    pb = palloc()
                    for q in range(n):
                        k.op("pe", lambda e: e.transpose(out=PS[pb][:, q * 32:(q + 1) * 32], in_=stage[:, (q0 + q) * 128:(q0 + q + 1) * 128],
                                                         identity=ident[0:32, 0:32]), reads=[BmT, Bc], writes=[BPS[pb]])
                    cb = c0_ // 128 + q0
                    k.op("act", lambda e: e.copy(out=fcs[:, cb:cb + n, :, :].rearrange("p c s r -> p c (s r)"),
                                                 in_=PS[pb][:, 0:n * 32].rearrange("p (c x) -> p c x", c=n)), reads=[BPS[pb]], writes=[Bfcs])
                    pfree_(pb)

        def rows_out(src, srcB, nchunk, dst):
            stage = mT[0:32, :, :].rearrange("p a b -> p (a b)")
            done = 0
            while done < nchunk:
                nn = min(32, nchunk - done)
                for q0 in range(0, nn, 4):
                    n = min(4, nn - q0)
                    pb = palloc()
                    for q in range(n):
                        cidx = done + q0 + q
                        k.op("pe", lambda e: e.transpose(out=PS[pb][0:32, q * 128:(q + 1) * 128],
                                                         in_=src[:, cidx, :, :].rearrange("p s r -> p (s r)"), identity=ident[:, :]),
                             reads=[srcB, Bc], writes=[BPS[pb]])
                    k.op("act", lambda e: e.copy(out=stage[:, q0 * 128:(q0 + n) * 128], in_=PS[pb][0:32, 0:n * 128]), reads=[BPS[pb]], writes=[BmT])
                    pfree_(pb)
                k.dma(dst[:, done * 128:(done + nn) * 128], stage[:, 0:nn * 128], reads=[BmT], sem="d_rows")
                done += nn

        pre_norm(2, 0, N)
        ffn_state_in(0)
        ffn(0, N, seg=True)
        post_norm_add(3, 0, N)
        rows_out(fcs, Bfcs, 44, fs_out[0])
        pre_norm(0, 1, N)
        odd_mixer(N, False, seg=True)
        proj_out(["out1_0", "out1_1"], mixT, BmixT, 8, N, 4)
        post_norm_add(1, 1, N)
        rows_out(ccs, Bccs, 8, cs_out)
        pre_norm(2, 1, N)
        ffn_state_in(1)
        ffn(1, N, seg=True)
        post_norm_add(3, 1, N)
        rows_out(fcs, Bfcs, 44, fs_out[1])
        for half in range(2):
            pb = palloc()
            for c4 in range(4):
                c = half * 4 + c4
                k.op("pe", lambda e: e.transpose(out=PS[pb][0:64, c4 * 128:(c4 + 1) * 128], in_=xT[:, c, 0:64], identity=ident[:, :]),
                     reads=[BxT, Bc], writes=[BPS[pb]])
            k.op("act", lambda e: e.copy(out=stg[0][0:64, half * 512:(half + 1) * 512], in_=PS[pb][0:64, :]), reads=[BPS[pb]], writes=[Bstg[0]])
            pfree_(pb)
        k.dma(ys_out[:, :], stg[0][0:64, :], reads=[Bstg[0]], sem="d_stg0")

    def finish():
        for s_, v in k.cnt.items():
            if v and s_ != "sp":
                k.E["sp"].wait_ge(k.sems[s_], v)

    import os
    CUT = int(os.environ.get("KCUT", "99"))
    KSUB = int(os.environ.get("KSUB", "0"))

    class _Stop(Exception):
        pass

    cur = dict(ui=-1)

    def ck(n):
        if KSUB == n and cur["ui"] == CUT - 1:
            raise _Stop()
    load_resident()
    for ui, (mode, ts) in enumerate(units):
        if ui >= CUT:
            break
        cur["ui"] = ui
        nt = len(ts)
        N = nt * 128
        if mode == "pre":
            try:
                load_x(ts, N)
                ck(1)
                pre_norm(0, 0, N)
                ck(2)
                even_mixer("pre", ts, N, [-1] * nt)
            except _Stop:
                break
            continue
        try:
            load_x([NPRE + t for t in ts], N)
            pre_norm(0, 0, N)
            even_mixer("main", ts, N, ts)
            ck(13)
            proj_out(["out0_0", "out0_1"], mixT, BmixT, 8, N, 4)
            ck(141)
            post_norm_add(1, 0, N)
            ck(14)
            pre_norm(2, 0, N)
            ffn(0, N)
            ck(15)
            post_norm_add(3, 0, N)
            pre_norm(0, 1, N)
            ck(16)
            odd_mixer(N, ts[-1] == NMAIN - 1)
            ck(17)
            proj_out(["out1_0", "out1_1"], mixT, BmixT, 8, N, 4)
            post_norm_add(1, 1, N)
            pre_norm(2, 1, N)
            ffn(1, N)
            post_norm_add(3, 1, N)
            ck(18)
            store_y([(ti, t - 1) for ti, t in enumerate(ts) if t >= 1], N)
        except _Stop:
            break
        if ts == [0]:
            k.op("dve", lambda e: e.tensor_scalar(out=fcar[:, :, :, :], in0=fcar[:, :, :, :], scalar1=flg[:, 0:1], scalar2=None,
                                                  op0=ALU.mult), reads=[Bfcar, Bc], writes=[Bfcar])
            k.op("dve", lambda e: e.tensor_scalar(out=ccar[:, :, :], in0=ccar[:, :, :], scalar1=flg[:, 0:1], scalar2=None,
                                                  op0=ALU.mult), reads=[Bccar, Bc], writes=[Bccar])
        if ts[-1] == NMAIN - 1:
            state_rows_out([(ccar[:, c, :], Bccar) for c in range(8)], 8, None, 0)
            k.dma(c_out[:, :], rowst[:, 0:1024], reads=[BmT], sem="d_rows")
            for l in range(2):
                for part in range(4):
                    state_rows_out([(fcar[:, l, part * 11 + c, :], Bfcar) for c in range(11)], 11, None, 0)
                    k.dma(f_out[l, :, part * 1408:(part + 1) * 1408], rowst[:, 0:1408], reads=[BmT], sem="d_rows")

    if with_sample and CUT > len(units):
        sample_unit()
    finish()
    return nc, k


def host_consts():
    c = {}
    c["c_ident"] = np.eye(128, dtype=np.float32)
    c["c_ones"] = np.ones((128, 128), dtype=ml_dtypes.bfloat16)
    j = np.arange(128)[:, None]
    i = np.arange(128)[None, :]
    same = (j // 64) == (i // 64)
    c["c_ucum"] = np.where(same & (j <= i), -1.0 / 16.0, 0.0).astype(np.float32)
    c["c_urev"] = np.where(same & (j > i), -1.0 / 16.0, 0.0).astype(np.float32)
    c["c_maska"] = np.where(same & (j <= i), 1.0, 0.0).astype(np.float32)
    slopes = 2.0 ** (-8.0 * np.arange(1, 9) / 8.0)
    bias = np.zeros((128, 4, 4, 128), np.float32)
    s = np.arange(128)[:, None]
    q = np.arange(128)[None, :]
    for kb_ in range(2):
        dist = (128 + q - s) if kb_ == 0 else (q - s)
        valid = (dist >= 0) & (dist <= 128)
        for kv in range(2):
            for par in range(2):
                for gi in range(2):
                    b = -slopes[kv * 4 + par + 2 * gi] * dist * 8.0
                    bias[:, kv * 2 + par, kb_ * 2 + gi, :] = np.where(valid, b, NEG)
    c["c_bias"] = bias.reshape(128, 4, 512)
    j = np.arange(64)[:, None]
    i = np.arange(64)[None, :]
    same = (j // 4) == (i // 4)
    c["c_ucum_s"] = np.where(same & (j <= i), -1.0 / 16.0, 0.0).astype(np.float32)
    c["c_urev_s"] = np.where(same & (j > i), -1.0 / 16.0, 0.0).astype(np.float32)
    c["c_maska_s"] = np.where(same & (j <= i), 1.0, 0.0).astype(np.float32)
    c["c_rowmask"] = ((np.arange(64)[:, None] // 4) == np.arange(16)[None, :]).astype(np.float32)
    bs = np.full((128, 2, 2, 2, 2, 64), NEG, np.float32)
    tok = np.arange(64)
    ti = tok % 4
    srow = np.arange(128)[:, None]
    for par in range(2):
        for kv in range(2):
            for gi in range(2):
                sl = slopes[kv * 4 + par + 2 * gi]
                dist = 128 + ti[None, :] - srow
                bs[:, 0, par, kv, gi, :] = np.where(srow >= ti[None, :], -sl * dist * 8.0, NEG)
                jj = np.arange(64)[:, None]
                d2 = ti[None, :] - (jj % 4)
                ok = ((jj // 4) == (tok[None, :] // 4)) & (d2 >= 0)
                bs[0:64, 1, par, kv, gi, :] = np.where(ok, -sl * d2 * 8.0, NEG)
    c["c_bias_s"] = bs.reshape(128, 2, 512)
    return c


def host_pvec(inp):
    rows = np.zeros((512, 128), np.float32)
    for q, name in enumerate(("norm_mix_pre", "norm_mix_post", "norm_ffn_pre", "norm_ffn_post")):
        a = np.asarray(inp[name], np.float32)
        for l in range(2):
            rows[(q * 2 + l) * 8:(q * 2 + l) * 8 + 8] = a[l].reshape(8, 128)
    fw = np.asarray(inp["ffn_conv_w"], np.float32)
    for l in range(2):
        for i in range(3):
            rows[PC_FW + (l * 3 + i) * 44:PC_FW + (l * 3 + i) * 44 + 44] = fw[l, i].reshape(44, 128)
    fb = np.asarray(inp["ffn_conv_b"], np.float32)
    for l in range(2):
        rows[PC_FB + l * 44:PC_FB + l * 44 + 44] = fb[l].reshape(44, 128)
    cw = np.asarray(inp["conv_w_odd"], np.float32)
    for i in range(3):
        rows[PC_CW + i * 8:PC_CW + i * 8 + 8] = cw[0, i].reshape(8, 128)
    rows[PC_GN:PC_GN + 4] = np.asarray(inp["gla_norm"], np.float32)[0].reshape(4, 128)
    return rows


_CACHE = {}


def make_in_maps(inp):
    consts = host_consts()
    pvec = host_pvec(inp)
    xp = np.asarray(inp["x_prompt"], np.float32)
    shared = dict(consts)
    shared["pvec"] = pvec
    shared["w_in_even"] = np.ascontiguousarray(np.asarray(inp["w_in_even"], np.float32)[0])
    shared["w_out_even"] = np.ascontiguousarray(np.asarray(inp["w_out_even"], np.float32)[0])
    shared["w_in_odd"] = np.ascontiguousarray(np.asarray(inp["w_in_odd"], np.float32)[0])
    shared["w_out_odd"] = np.ascontiguousarray(np.asarray(inp["w_out_odd"], np.float32)[0])
    for l in range(2):
        shared["ffn_up%d" % l] = np.ascontiguousarray(np.asarray(inp["ffn_up"], np.float32)[l])
        shared["ffn_down%d" % l] = np.ascontiguousarray(np.asarray(inp["ffn_down"], np.float32)[l])
    shared["w_gate_up"] = np.ascontiguousarray(np.asarray(inp["w_gate_up"], np.float32)[0])
    shared["b_gate"] = np.ascontiguousarray(np.asarray(inp["b_gate"], np.float32))
    shared["attn_sinks"] = np.ascontiguousarray(np.asarray(inp["attn_sinks"], np.float32))
    in_maps = []
    for c in range(8):
        b, half = c // 2, c % 2
        xin = np.zeros(((NPRE + NMAIN) * 128, D), np.float32)
        if half == 1:
            xin[:] = xp[b]
        else:
            xin[(NPRE + 1) * 128:] = xp[b, 0:2048]
        m = dict(shared)
        m["xin"] = xin
        fl = np.zeros((128, 2), np.float32)
        fl[:, 0] = float(half)
        fl[:, 1] = (float(half) - 1.0) * (-NEG)
        m["flagv"] = fl
        sl = slice(c * NSEQ, (c + 1) * NSEQ)
        m["xs"] = np.ascontiguousarray(np.asarray(inp["x_sample"], np.float32)[sl].reshape(64, D))
        m["sck"] = np.ascontiguousarray(np.asarray(inp["cache_swa_k"], np.float32)[0, sl].reshape(16, 128, 128))
        m["scv"] = np.ascontiguousarray(np.asarray(inp["cache_swa_v"], np.float32)[0, sl].reshape(16, 128, 128))
        m["sgl"] = np.ascontiguousarray(np.asarray(inp["state_gla"], np.float32)[0, sl])
        m["scs"] = np.ascontiguousarray(np.asarray(inp["state_conv"], np.float32)[0, sl].reshape(32, D))
        m["sfs"] = np.ascontiguousarray(np.asarray(inp["state_ffn"], np.float32)[:, sl].reshape(2, 32, F2))
        in_maps.append(m)
    return in_maps


def kernel(**inp):
    if "nc" not in _CACHE:
        _CACHE["nc"] = build_program()[0]
    nc = _CACHE["nc"]
    in_maps = make_in_maps(inp)
    import os
    if os.environ.get("KCORES"):
        lc = [int(x) for x in os.environ["KCORES"].split(",")]
        res = run_bass_kernel_spmd(nc, [in_maps[c] for c in lc], core_ids=list(range(len(lc))))
        R = [None] * 8
        for i, c in enumerate(lc):
            R[c] = res.results[i]
        for c in range(8):
            if R[c] is None:
                R[c] = {kk: np.zeros_like(vv) for kk, vv in res.results[0].items()}
    else:
        res = run_bass_kernel_spmd(nc, in_maps, core_ids=list(range(8)))
        R = res.results
    y_prompt = np.zeros((4, 4096, D), np.float32)
    swa_k = np.zeros((1, 4, 128, 2, 64), np.float32)
    swa_v = np.zeros((1, 4, 128, 2, 64), np.float32)
    gla = np.zeros((1, 4, 4, 64, 128), np.float32)
    conv = np.zeros((1, 4, 2, D), np.float32)
    ffn_s = np.zeros((2, 4, 2, F2), np.float32)
    for c in range(8):
        b, half = c // 2, c % 2
        y_prompt[b, half * 2048:(half + 1) * 2048] = R[c]["y_out"]
        if half == 1:
            swa_k[0, b] = R[c]["k_out"].reshape(128, 2, 64)
            swa_v[0, b] = R[c]["v_out"].reshape(128, 2, 64)
            gla[0, b] = R[c]["g_out"]
            conv[0, b] = R[c]["c_out"]
            ffn_s[:, b] = R[c]["f_out"]
    y_s = np.zeros((128, 4, D), np.float32)
    ks_s = np.zeros((1, 128, 128, 2, 64), np.float32)
    vs_s = np.zeros((1, 128, 128, 2, 64), np.float32)
    gs_s = np.zeros((1, 128, 4, 64, 128), np.float32)
    cs_s = np.zeros((1, 128, 2, D), np.float32)
    fs_s = np.zeros((2, 128, 2, F2), np.float32)
    for c in range(8):
        sl = slice(c * NSEQ, (c + 1) * NSEQ)
        y_s[sl] = R[c]["ys_out"].reshape(16, 4, D)
        ks_s[0, sl] = R[c]["ks_out"].reshape(16, 128, 2, 64)
        vs_s[0, sl] = R[c]["vs_out"].reshape(16, 128, 2, 64)
        gs_s[0, sl] = R[c]["gs_out"]
        cs_s[0, sl] = R[c]["cs_out"].reshape(16, 2, D)
        fs_s[:, sl] = R[c]["fs_out"].reshape(2, 16, 2, F2)
    return (y_prompt, y_s, swa_k, swa_v, gla, conv, ffn_s, ks_s, vs_s, gs_s, cs_s, fs_s)
```

```python
import numpy as np
import ml_dtypes
import concourse.bass as bass
import concourse.mybir as mybir
from concourse.bass_utils import run_bass_kernel_spmd

F32 = mybir.dt.float32
BF16 = mybir.dt.bfloat16
AF = mybir.ActivationFunctionType
ALU = mybir.AluOpType
AX = mybir.AxisListType

D = 1024
DFF = 2816
F2 = 5632
NPRE = 15
NMAIN = 17
NSEQ = 16
EPS = 1e-6
NEG = -240000.0
NSLOT = 5
SLOT_E = 4096


class Buf:
    __slots__ = ("name", "w", "r")

    def __init__(self, name):
        self.name = name
        self.w = None
        self.r = {}


class KB:
    ENG = ("pe", "act", "dve", "pool", "sp")

    def __init__(self, nc):
        self.nc = nc
        self.E = dict(pe=nc.tensor, act=nc.scalar, dve=nc.vector, pool=nc.gpsimd, sp=nc.sync)
        self.sems = {}
        self.cnt = {}
        self.seen = {e: {} for e in self.ENG}
        for e in self.ENG:
            self.new_sem(e)
        self.n_inst = {e: 0 for e in self.ENG}

    def new_sem(self, name):
        self.sems[name] = self.nc.alloc_semaphore(name="s_" + name)
        self.cnt[name] = 0
        return name

    def sb(self, name, shape, dt):
        return self.nc.alloc_sbuf_tensor(name, list(shape), dt)

    def _wait(self, eng, ev, kind):
        if ev is None:
            return
        s, v = ev
        if s == eng and kind != "raw":
            return
        if self.seen[eng].get(s, 0) >= v:
            return
        self.E[eng].wait_ge(self.sems[s], v)
        self.seen[eng][s] = v

    def _deps(self, eng, reads, writes, force=False):
        for b in reads:
            self._wait(eng, b.w, "raw")
        for b in writes:
            self._wait(eng, b.w, "raw" if force else "waw")
            for s, v in b.r.items():
                self._wait(eng, (s, v), "raw" if force else "war")

    def _record(self, ev, reads, writes):
        s, v = ev
        for b in reads:
            if b.r.get(s, 0) < v:
                b.r[s] = v
        for b in writes:
            b.w = ev
            b.r = {}

    @staticmethod
    def _flat(bs):
        out = []
        for b in bs:
            if isinstance(b, (list, tuple)):
                out.extend(KB._flat(b))
            else:
                out.append(b)
        return out

    def op(self, eng, fn, reads=(), writes=(), signal=True):
        reads, writes = self._flat(reads), self._flat(writes)
        self._deps(eng, reads, writes)
        ins = fn(self.E[eng])
        self.n_inst[eng] += 1
        ev = (eng, self.cnt[eng] + 1)
        if signal:
            ins.then_inc(self.sems[eng], 1)
            self.cnt[eng] += 1
        self._record(ev, reads, writes)
        return ins

    def dma(self, out, in_, reads=(), writes=(), sem=None, q="sp", **kw):
        reads, writes = self._flat(reads), self._flat(writes)
        self._deps(q, reads, writes, force=True)
        ins = self.E[q].dma_start(out=out, in_=in_, **kw)
        ins.then_inc(self.sems[sem], 16)
        self.cnt[sem] += 16
        self.n_inst[q] += 1
        self._record((sem, self.cnt[sem]), reads, writes)
        return ins


def weight_blocks():
    blocks = []
    for j in range(5):
        w = min(512, 2320 - 512 * j)
        blocks.append(("in0_%d" % j, 8, w, [("w_in_even", 0, 512 * j, w, 0)]))
    for j in range(2):
        blocks.append(("out0_%d" % j, 8, 512, [("w_out_even", 0, 512 * j, 512, 0)]))
    for l in range(2):
        for j in range(11):
            blocks.append(("up%d_%d" % (l, j), 8, 512, [("ffn_up%d" % l, 0, 256 * j, 256, 0),
                                                        ("ffn_up%d" % l, 0, DFF + 256 * j, 256, 256)]))
        for m in range(8):
            blocks.append(("dn%d_%d" % (l, m), 22, 128, [("ffn_down%d" % l, 0, 128 * m, 128, 0)]))
    chunks = []
    for c in range(8):
        chunks += [1024 + 128 * c, 2048 + 128 * c, 128 * c]
    for j in range(6):
        blocks.append(("in1_%d" % j, 8, 512, [("w_in_odd", 0, chunks[4 * j + q], 128, 128 * q) for q in range(4)]))
    for j in range(2):
        blocks.append(("out1_%d" % j, 8, 512, [("w_out_odd", 0, 512 * j, 512, 0)]))
    return blocks


def unit_block_order(mode):
    if mode == "pre":
        return ["in0_%d" % j for j in range(1, 5)]
    o = ["in0_%d" % j for j in range(5)] + ["out0_%d" % j for j in range(2)]
    o += ["up0_%d" % j for j in range(11)] + ["dn0_%d" % m for m in range(8)]
    o += ["in1_%d" % j for j in range(6)] + ["out1_%d" % j for j in range(2)]
    o += ["up1_%d" % j for j in range(11)] + ["dn1_%d" % m for m in range(8)]
    return o


def pc_norm(q, l, kc):
    return (q * 2 + l) * 8 + kc
PC_FW = 64
PC_FB = 328
PC_CW = 416
PC_GN = 440


def build_program(with_sample=True):
    nc = bass.Bass("TRN2", target_bir_lowering=False)
    k = KB(nc)
    Dm = {}

    def din(name, shape, dt=F32):
        Dm[name] = nc.dram_tensor(name, list(shape), dt, kind="ExternalInput").ap()
        return Dm[name]

    def dout(name, shape):
        Dm[name] = nc.dram_tensor(name, list(shape), F32, kind="ExternalOutput").ap()
        return Dm[name]

    xin = din("xin", [(NPRE + NMAIN) * 128, D])
    flagv = din("flagv", [128, 2])
    pvec = din("pvec", [512, 128])
    c_ident = din("c_ident", [128, 128])
    c_ones = din("c_ones", [128, 128], BF16)
    c_ucum = din("c_ucum", [128, 128])
    c_urev = din("c_urev", [128, 128])
    c_maska = din("c_maska", [128, 128])
    c_bias = din("c_bias", [128, 4, 512])
    din("w_in_even", [D, 2320]); din("w_out_even", [D, D]); din("w_in_odd", [D, 3 * D]); din("w_out_odd", [D, D])
    for l in range(2):
        din("ffn_up%d" % l, [D, F2]); din("ffn_down%d" % l, [DFF, D])
    w_gate_up = din("w_gate_up", [16, 256]); b_gate = din("b_gate", [1, 256]); attn_sinks = din("attn_sinks", [1, 8])
    xs = din("xs", [64, D]); sck = din("sck", [16, 128, 128]); scv = din("scv", [16, 128, 128])
    sgl = din("sgl", [16, 4, 64, 128]); scs = din("scs", [32, D]); sfs = din("sfs", [2, 32, F2])
    c_ucum_s = din("c_ucum_s", [64, 64]); c_urev_s = din("c_urev_s", [64, 64]); c_maska_s = din("c_maska_s", [64, 64])
    c_rowmask = din("c_rowmask", [64, 16]); c_bias_s = din("c_bias_s", [128, 2, 512])
    ys_out = dout("ys_out", [64, D]); ks_out = dout("ks_out", [16, 128, 128]); vs_out = dout("vs_out", [16, 128, 128])
    gs_out = dout("gs_out", [16, 4, 64, 128]); cs_out = dout("cs_out", [32, D]); fs_out = dout("fs_out", [2, 32, F2])
    y_out = dout("y_out", [16 * 128, D])
    k_out = dout("k_out", [128, 128]); v_out = dout("v_out", [128, 128])
    g_out = dout("g_out", [4, 64, 128])
    c_out = dout("c_out", [2, D])
    f_out = dout("f_out", [2, 2, F2])

    for s in ("d_cast", "d_const", "d_pv", "d_sink", "d_stg0", "d_stg1", "d_misc", "d_rows", "d_s1", "d_s2", "d_s3", "d_s4", "d_s5", "d_s6", "d_so", "d_mk", "d_mv", "d_mg", "d_res"):
        k.new_sem(s)
    for i in range(NSLOT):
        k.new_sem("d_w%d" % i)

    blocks = {b[0]: b for b in weight_blocks()}
    scr = {}
    for src in ("w_in_even", "w_out_even", "ffn_up0", "ffn_down0", "w_in_odd", "w_out_odd", "ffn_up1", "ffn_down1"):
        R, C = Dm[src].shape
        t = nc.dram_tensor("scr_" + src, [R, C], BF16, kind="Internal").ap()
        b = Buf("scr_" + src)
        scr[src] = (t, b)
        k.new_sem("d_c_" + src)

    def emit_casts(names):
        for src in names:
            t, b = scr[src]
            R, C = Dm[src].shape
            step = 512
            for r0 in range(0, R, step):
                r1 = min(R, r0 + step)
                k.dma(t[r0:r1, :], Dm[src][r0:r1, :], writes=[b], sem="d_c_" + src, q="pool")

    emit_casts(["w_in_even"])

    ring = [k.sb("ring%d" % i, [128, SLOT_E], BF16) for i in range(NSLOT)]
    ringB = [Buf("ring%d" % i) for i in range(NSLOT)]
    xT = k.sb("xT", [128, 8, 512], F32); BxT = [Buf("xT%d" % i) for i in range(8)]
    hT = k.sb("hT", [128, 8, 512], BF16); BhT = [Buf("hT%d" % i) for i in range(8)]
    mT = k.sb("mT", [128, 8, 512], F32); BmT = [Buf("mT%d" % i) for i in range(8)]
    sq = [k.sb("sq%d" % i, [128, 512], BF16) for i in range(2)]; Bsq = [Buf("sq%d" % i) for i in range(2)]
    rstd = k.sb("rstd", [128, 512], F32); Brstd = Buf("rstd")
    tmpn = [k.sb("tmpn%d" % i, [128, 512], F32) for i in range(2)]; Btmpn = [Buf("tmpn%d" % i) for i in range(2)]
    stg = [k.sb("stg%d" % i, [128, 1024], F32) for i in range(2)]; Bstg = [Buf("stg%d" % i) for i in range(2)]
    qaT = k.sb("qaT", [128, 4, 512], BF16); BqaT = Buf("qaT")
    kaT = k.sb("kaT", [128, 640], BF16); BkaT = Buf("kaT")
    kaT2 = k.sb("kaT2", [128, 640], BF16); BkaT2 = Buf("kaT2")
    vtok = k.sb("vtok", [128, 5, 128], BF16); Bvtok = Buf("vtok")
    qbT = k.sb("qbT", [128, 2, 512], F32); BqbT = Buf("qbT")
    kbT = k.sb("kbT", [128, 2, 512], F32); BkbT = Buf("kbT")
    srbT = k.sb("srbT", [128, 4, 512], F32); BsrbT = Buf("srbT")
    glrT = k.sb("glrT", [32, 512], BF16); BglrT = Buf("glrT")
    mixT = k.sb("mixT", [128, 8, 512], BF16); BmixT = Buf("mixT")
    gated = k.sb("gated", [128, 22, 512], BF16); Bgated = Buf("gated")
    tg = [k.sb("tg%d" % i, [128, 512], F32) for i in range(2)]; Btg = [Buf("tg%d" % i) for i in range(2)]
    tv = [k.sb("tv%d" % i, [128, 512], F32) for i in range(2)]; Btv = [Buf("tv%d" % i) for i in range(2)]
    l_sb = k.sb("l_sb", [128, 256], F32); Bl = Buf("l_sb")
    er_sb = k.sb("er_sb", [128, 256], F32); Ber = Buf("er")
    e1_sb = k.sb("e1_sb", [128, 256], F32); Be1 = Buf("e1")
    e2_sb = k.sb("e2_sb", [128, 256], F32); Be2 = Buf("e2")
    qtl = k.sb("qtl", [128, 2, 128], BF16); Bqtl = Buf("qtl")
    ktl = k.sb("ktl", [128, 2, 128], BF16); Bktl = Buf("ktl")
    khat = k.sb("khat", [128, 256], BF16); Bkhat = Buf("khat")
    vbtok = k.sb("vbtok", [128, 512], BF16); Bvbtok = Buf("vbtok")
    atm = [k.sb("atm%d" % i, [128, 128], BF16) for i in range(2)]; Batm = [Buf("atm%d" % i) for i in range(2)]
    S_f = k.sb("S_f", [128, 2, 128], F32); BS = Buf("S_f")
    S_b = k.sb("S_b", [128, 4, 2, 128], BF16); BSb = [[Buf("Sb%d%d" % (p, i)) for i in range(2)] for p in range(2)]
    osq = k.sb("osq", [128, 512], BF16); Bosq = Buf("osq")
    orstd = k.sb("orstd", [128, 512], F32); Borstd = Buf("orstd")
    otmp = k.sb("otmp", [128, 512], F32); Botmp = Buf("otmp")
    kf32 = k.sb("kf32", [128, 128], F32); Bkf32 = Buf("kf32")
    sc = [k.sb("sc%d" % i, [128, 512], F32) for i in range(2)]; Bsc = [Buf("sc%d" % i) for i in range(2)]
    ptT = k.sb("ptT", [128, 4, 512], BF16); Bpt = [Buf("pt%d" % i) for i in range(4)]
    rec, Brec = sc[0], Bsc[0]
    ident = k.sb("ident", [128, 128], F32); ones = k.sb("ones", [128, 128], BF16)
    ucum = k.sb("ucum", [128, 128], F32); urev = k.sb("urev", [128, 128], F32); maska = k.sb("maska", [128, 128], F32)
    biasT = k.sb("biasT", [128, 4, 512], F32)
    PT = k.sb("PT", [128, 512], F32)
    pv_in = k.sb("pv_in", [128, 4, 128], F32)
    flg = k.sb("flg", [128, 2], F32)
    epsD = k.sb("epsD", [128, 4], F32)
    wg = k.sb("wg", [32, 256], BF16)
    sinkE = k.sb("sinkE", [128, 4], F32)
    Bc = Buf("consts"); BPT = Buf("PT"); Bpv = Buf("pv_in"); Bwg = Buf("wg"); BsinkE = Buf("sinkE")
    fcar = k.sb("fcar", [128, 2, 44, 2], F32); Bfcar = Buf("fcar")
    ccar = k.sb("ccar", [128, 8, 2], F32); Bccar = Buf("ccar")
    ucum_s = k.sb("ucum_s", [64, 64], F32); urev_s = k.sb("urev_s", [64, 64], F32); maska_s = k.sb("maska_s", [64, 64], F32)
    rowmask = k.sb("rowmask", [64, 16], F32)
    fcs = k.sb("fcs", [128, 44, 16, 2], F32); Bfcs = Buf("fcs")
    ccs = k.sb("ccs", [128, 8, 16, 2], F32); Bccs = Buf("ccs")
    cu, Bcu = tg, Btg
    ucp, Bucp = tv, Btv
    zt, Bzt = tmpn, Btmpn
    osm = k.sb("osm", [128, 256], F32); Bosm = Buf("osm")
    rowst = mT[0:2, 0:3, :].rearrange("p a b -> p (a b)")

    PS = [nc.alloc_psum_tensor("ps%d" % i, [128, 512], F32) for i in range(8)]
    BPS = [Buf("ps%d" % i) for i in range(8)]
    pfree = list(range(8))

    def palloc():
        assert pfree, "out of PSUM banks"
        return pfree.pop(0)

    def pfree_(i):
        pfree.append(i)

    k.dma(ident[:, :], c_ident[:, :], writes=[Bc], sem="d_const")
    k.dma(ones[:, :], c_ones[:, :], writes=[Bc], sem="d_const")
    k.dma(ucum[:, :], c_ucum[:, :], writes=[Bc], sem="d_const")
    k.dma(urev[:, :], c_urev[:, :], writes=[Bc], sem="d_const")
    k.dma(maska[:, :], c_maska[:, :], writes=[Bc], sem="d_const")
    k.dma(biasT[:, :, :], c_bias[:, :, :], writes=[Bc], sem="d_const")
    k.dma(flg[:, :], flagv[:, :], writes=[Bc], sem="d_const")
    k.dma(ucum_s[:, :], c_ucum_s[:, :], writes=[Bc], sem="d_const")
    k.dma(urev_s[:, :], c_urev_s[:, :], writes=[Bc], sem="d_const")
    k.dma(maska_s[:, :], c_maska_s[:, :], writes=[Bc], sem="d_const")
    k.dma(rowmask[:, :], c_rowmask[:, :], writes=[Bc], sem="d_const")
    k.dma(pv_in[:, :, :], pvec.rearrange("(a p) c -> p a c", p=128), writes=[Bpv], sem="d_pv")
    k.op("pool", lambda e: e.memset(wg[:, :], 0.0), writes=[Bwg])
    k.dma(wg[0:16, :], w_gate_up[:, :], writes=[Bwg], sem="d_cast", q="pool")
    k.dma(wg[16:17, :], b_gate[:, :], writes=[Bwg], sem="d_cast", q="pool")
    sink1 = k.sb("sink1", [1, 8], F32); onesf = k.sb("onesf", [1, 128], F32); sink8 = k.sb("sink8", [128, 8], F32)
    Bs1 = Buf("sink1"); Bs8 = Buf("sink8")
    k.dma(sink1[:, :], attn_sinks[:, :], writes=[Bs1], sem="d_sink")
    k.op("pool", lambda e: e.memset(onesf[:, :], 1.0), writes=[Bs1])
    pb = palloc()
    k.op("pe", lambda e: e.matmul(PS[pb][:, 0:8], lhsT=onesf[0:1, :], rhs=sink1[0:1, :], start=True, stop=True),
         reads=[Bs1], writes=[BPS[pb]])
    k.op("act", lambda e: e.activation(out=sink8[:, :], in_=PS[pb][:, 0:8], func=AF.Exp), reads=[BPS[pb]], writes=[Bs8])
    pfree_(pb)
    k.op("dve", lambda e: e.tensor_copy(out=sinkE[0:64, :], in_=sink8[0:64, 0:8:2]), reads=[Bs8], writes=[BsinkE])
    k.op("dve", lambda e: e.tensor_copy(out=sinkE[64:128, :], in_=sink8[64:128, 1:8:2]), reads=[Bs8], writes=[BsinkE])
    pb = palloc()
    for a in range(4):
        k.op("pe", lambda e: e.transpose(out=PS[pb][:, a * 128:(a + 1) * 128], in_=pv_in[:, a, :], identity=ident[:, :]),
             reads=[Bpv, Bc], writes=[BPS[pb]])
    k.op("act", lambda e: e.copy(out=PT[:, :], in_=PS[pb][:, :]), reads=[BPS[pb]], writes=[BPT])
    pfree_(pb)
    k.op("act", lambda e: e.mul(out=PT[:, 0:64], in_=PT[:, 0:64], mul=32.0), reads=[BPT], writes=[BPT])
    k.op("act", lambda e: e.mul(out=PT[:, PC_GN:PC_GN + 4], in_=PT[:, PC_GN:PC_GN + 4], mul=float(np.sqrt(128.0))),
         reads=[BPT], writes=[BPT])
    k.op("pool", lambda e: e.memset(glrT[:, :], 1.0), writes=[BglrT])
    k.op("pool", lambda e: e.memset(epsD[:, 0:1], float(D * EPS)), writes=[Bc])
    k.op("pool", lambda e: e.memset(epsD[:, 1:2], float(128 * EPS)), writes=[Bc])
    k.op("pool", lambda e: e.memset(epsD[:, 2:3], 1.0), writes=[Bc])
    k.op("pool", lambda e: e.memset(epsD[:, 3:4], 0.0), writes=[Bc])
    k.op("pool", lambda e: e.memset(S_f[:, :, :], 0.0), writes=[BS])
    k.op("pool", lambda e: e.memset(S_b[:, :, :, :], 0.0), writes=[BSb[0][0], BSb[0][1], BSb[1][0], BSb[1][1]])
    k.op("pool", lambda e: e.memset(fcar[:, :, :, :], 0.0), writes=[Bfcar])
    k.op("pool", lambda e: e.memset(ccar[:, :, :], 0.0), writes=[Bccar])
    k.op("pool", lambda e: e.memset(kaT[:, :], 0.0), writes=[BkaT])
    k.op("pool", lambda e: e.memset(kaT2[:, :], 0.0), writes=[BkaT2])
    k.op("pool", lambda e: e.memset(vtok[:, :, :], 0.0), writes=[Bvtok])
    emit_casts(["w_out_even", "ffn_up0", "ffn_down0", "w_in_odd", "w_out_odd", "ffn_up1", "ffn_down1"])

    units = []
    for i in range(0, NPRE, 4):
        units.append(("pre", list(range(i, min(i + 4, NPRE)))))
    main_split = [[0], [1, 2, 3, 4], [5, 6, 7, 8], [9, 10, 11, 12], [13, 14, 15, 16]]
    for ts in main_split:
        units.append(("main", ts))
    plan = []
    for mode, ts in units:
        if mode != "pre":
            plan += unit_block_order(mode)
    if with_sample:
        plan += unit_block_order("main")
    wstate = dict(next_issue=0, next_use=0)

    def w_issue():
        u = wstate["next_issue"]
        if u >= len(plan):
            return
        name, KC, W, pieces = blocks[plan[u]]
        s = u % NSLOT
        dst = ring[s][:, 0:KC * W].rearrange("p (kc w) -> p kc w", kc=KC)
        for (src, r0, c0, w, off) in pieces:
            t, b = scr[src]
            k.dma(dst[:, :, off:off + w], t[r0:r0 + KC * 128, c0:c0 + w].rearrange("(kc p) w -> p kc w", p=128),
                  reads=[b], writes=[ringB[s]], sem="d_w%d" % s)
        wstate["next_issue"] += 1

    def w_get(name):
        u = wstate["next_use"]
        assert plan[u] == name, (plan[u], name)
        while wstate["next_issue"] <= u:
            w_issue()
        s = u % NSLOT
        _, KC, W, _ = blocks[name]
        return ring[s][:, 0:KC * W].rearrange("p (kc w) -> p kc w", kc=KC), ringB[s]

    gflat = gated[:, :, :].rearrange("p a b -> p (a b)")
    RES = {"in0_1": (gflat[:, 0:2048].rearrange("p (kc w) -> p kc w", kc=8), 512, 256),
           "in0_2": (gflat[:, 2048:6144].rearrange("p (kc w) -> p kc w", kc=8), 1024, 512),
           "in0_3": (gflat[:, 6144:8192].rearrange("p (kc w) -> p kc w", kc=8), 1536, 256),
           "in0_4": (gflat[:, 8192:10368].rearrange("p (kc w) -> p kc w", kc=8), 2048, 272)}

    def load_resident():
        t, b = scr["w_in_even"]
        for name, (view, c0, w) in RES.items():
            k.dma(view, t[0:1024, c0:c0 + w].rearrange("(kc p) w -> p kc w", p=128), reads=[b], writes=[Bgated], sem="d_res")

    def w_get_ahead(name, ahead):
        wstate["next_use"] += ahead
        r = w_get(name)
        wstate["next_use"] -= ahead
        return r

    def w_done():
        wstate["next_use"] += 1
        while wstate["next_issue"] < min(len(plan), wstate["next_use"] + NSLOT):
            w_issue()

    for _ in range(NSLOT):
        w_issue()

    def mm(out, lhsT, rhs, start, stop, reads, writes, signal=None, skip=False):
        if signal is None:
            signal = stop
        if skip:
            k.op("pe", lambda e: e.matmul(out, lhsT=lhsT, rhs=rhs, start=start, stop=stop, skip_group_check=True),
                 reads=reads, writes=writes, signal=signal)
        else:
            k.op("pe", lambda e: e.matmul(out, lhsT=lhsT, rhs=rhs, start=start, stop=stop), reads=reads, writes=writes,
                 signal=signal)

    cnt = dict(sq=0, tmpn=0, stg=0, sc=0, atm=0, tg=0, cu=0)

    def rot(name, n):
        i = cnt[name] % n
        cnt[name] += 1
        return i

    def rms_stats(srcs, N, src_bufs, eps_scaled):
        pb = palloc()
        for c in range(8):
            i = rot("sq", 2)
            k.op("act", lambda e: e.activation(out=sq[i][:, 0:N], in_=srcs[c], func=AF.Square), reads=[src_bufs[c]],
                 writes=[Bsq[i]])
            mm(PS[pb][:, 0:N], ones[:, :], sq[i][:, 0:N], c == 0, c == 7, [Bsq[i], Bc], [BPS[pb]], signal=True)
        k.op("act", lambda e: e.activation(out=rstd[:, 0:N], in_=PS[pb][:, 0:N], func=AF.Ln, bias=epsD[:, 0:1]),
             reads=[BPS[pb], Bc], writes=[Brstd])
        k.op("act", lambda e: e.activation(out=rstd[:, 0:N], in_=rstd[:, 0:N], func=AF.Exp, scale=-0.5), reads=[Brstd], writes=[Brstd])
        pfree_(pb)

    def pre_norm(q, l, N):
        rms_stats([xT[:, c, 0:N] for c in range(8)], N, BxT, D * EPS)
        for c in range(8):
            col = pc_norm(q, l, c)
            k.op("dve", lambda e: e.scalar_tensor_tensor(out=hT[:, c, 0:N], in0=xT[:, c, 0:N], scalar=PT[:, col:col + 1],
                                                         in1=rstd[:, 0:N], op0=ALU.mult, op1=ALU.mult),
                 reads=[BxT[c], BPT, Brstd], writes=[BhT[c]])

    def post_norm_add(q, l, N):
        rms_stats([mT[:, c, 0:N] for c in range(8)], N, BmT, D * EPS)
        ck(142)
        for c in range(8):
            col = pc_norm(q, l, c)
            i = rot("tmpn", 2)
            if c == 1:
                ck(144)
            k.op("dve", lambda e: e.scalar_tensor_tensor(out=tmpn[i][:, 0:N], in0=mT[:, c, 0:N], scalar=PT[:, col:col + 1],
                                                         in1=rstd[:, 0:N], op0=ALU.mult, op1=ALU.mult),
                 reads=[BmT[c], BPT, Brstd], writes=[Btmpn[i]])
            if c == 0:
                ck(143)
            k.op("pool", lambda e: e.tensor_tensor(out=xT[:, c, 0:N], in0=xT[:, c, 0:N], in1=tmpn[i][:, 0:N], op=ALU.add),
                 reads=[Btmpn[i], BxT[c]], writes=[BxT[c]])

    def proj_out(wnames, rhs_tile, rhs_buf, KC, N, per_block):
        m = 0
        for wn in wnames:
            wv, wb = w_get(wn)
            for j in range(per_block):
                pb = palloc()
                for kc in range(KC):
                    mm(PS[pb][:, 0:N], wv[:, kc, j * 128:(j + 1) * 128], rhs_tile[:, kc, 0:N], kc == 0, kc == KC - 1,
                       [wb, rhs_buf], [BPS[pb]])
                k.op("act", lambda e: e.copy(out=mT[:, m, 0:N], in_=PS[pb][:, 0:N]), reads=[BPS[pb]], writes=[BmT[m]])
                pfree_(pb)
                m += 1
            w_done()

    def conv_taps(t_ap, tB, src, srcB, car, carB, w0, w1, w2, bias, N, seg=False, after_act=None):
        if seg:
            V = lambda ap, a, b: ap[:, 0:64].rearrange("p (s i) -> p s i", i=4)[:, :, a:b]
            C = lambda a, b: car[:, :, a:b]
            L = 4
        else:
            V = lambda ap, a, b: ap[:, a:b]
            C = lambda a, b: car[:, a:b]
            L = N
        k.op("act", lambda e: e.activation(out=t_ap[:, 0:N], in_=src[:, 0:N], func=AF.Identity, scale=w2,
                                           bias=(epsD[:, 3:4] if bias is None else bias)), reads=[srcB, BPT, Bc], writes=[tB])
        if after_act is not None:
            after_act()
        k.op("dve", lambda e: e.scalar_tensor_tensor(out=V(t_ap, 1, L), in0=V(src, 0, L - 1), scalar=w1, in1=V(t_ap, 1, L),
                                                     op0=ALU.mult, op1=ALU.add), reads=[srcB, BPT, tB], writes=[tB])
        k.op("dve", lambda e: e.scalar_tensor_tensor(out=V(t_ap, 2, L), in0=V(src, 0, L - 2), scalar=w0, in1=V(t_ap, 2, L),
                                                     op0=ALU.mult, op1=ALU.add), reads=[srcB, BPT, tB], writes=[tB])
        k.op("dve", lambda e: e.scalar_tensor_tensor(out=V(t_ap, 0, 1), in0=C(1, 2), scalar=w1, in1=V(t_ap, 0, 1),
                                                     op0=ALU.mult, op1=ALU.add), reads=[carB, BPT, tB], writes=[tB])
        k.op("dve", lambda e: e.scalar_tensor_tensor(out=V(t_ap, 0, 2), in0=C(0, 2), scalar=w0, in1=V(t_ap, 0, 2),
                                                     op0=ALU.mult, op1=ALU.add), reads=[carB, BPT, tB], writes=[tB])
        k.op("dve", lambda e: e.tensor_copy(out=C(0, 2), in_=V(src, L - 2, L)), reads=[srcB, tB], writes=[carB])

    def load_x(tiles_abs, N):
        for ti, ta in enumerate(tiles_abs):
            i = rot("stg", 2)
            k.dma(stg[i][:, :], xin[ta * 128:(ta + 1) * 128, :], writes=[Bstg[i]], sem="d_stg%d" % i)
            for half in range(2):
                pb = palloc()
                for c4 in range(4):
                    c = half * 4 + c4
                    k.op("pe", lambda e: e.transpose(out=PS[pb][:, c4 * 128:(c4 + 1) * 128], in_=stg[i][:, c * 128:(c + 1) * 128],
                                                     identity=ident[:, :]), reads=[Bstg[i], Bc], writes=[BPS[pb]])
                k.op("act", lambda e: e.copy(out=xT[:, half * 4:(half + 1) * 4, ti * 128:(ti + 1) * 128],
                                             in_=PS[pb][:, :].rearrange("p (a b) -> p a b", a=4)),
                     reads=[BPS[pb]], writes=BxT[half * 4:(half + 1) * 4])
                pfree_(pb)

    def store_y(tiles_real, N):
        for ti, to in tiles_real:
            i = rot("stg", 2)
            for half in range(2):
                pb = palloc()
                for c4 in range(4):
                    c = half * 4 + c4
                    k.op("pe", lambda e: e.transpose(out=PS[pb][:, c4 * 128:(c4 + 1) * 128], in_=xT[:, c, ti * 128:(ti + 1) * 128],
                                                     identity=ident[:, :]), reads=[BxT[c], Bc], writes=[BPS[pb]])
                k.op("act", lambda e: e.copy(out=stg[i][:, half * 512:(half + 1) * 512], in_=PS[pb][:, :]),
                     reads=[BPS[pb]], writes=[Bstg[i]])
                pfree_(pb)
            k.dma(y_out[to * 128:(to + 1) * 128, :], stg[i][:, :], reads=[Bstg[i]], sem="d_stg%d" % i)

    def fm_chunk(wv, wb, c0, M, N, evac):
        pb = palloc()
        for kc in range(8):
            mm(PS[pb][0:M, 0:N], wv[:, kc, c0:c0 + M], hT[:, kc, 0:N], kc == 0, kc == 7, [wb, BhT[kc]], [BPS[pb]])
        evac(pb)
        pfree_(pb)

    def even_mixer(mode, tiles, N, gtiles):
        nt = len(tiles)
        pre = mode == "pre"
        if not pre:
            wv, wb = w_get("in0_0")
            for c in range(4):
                fm_chunk(wv, wb, c * 128, 128, N, lambda pb: k.op(
                    "act", lambda e: e.copy(out=qaT[:, c, 0:N], in_=PS[pb][:, 0:N]), reads=[BPS[pb]], writes=[BqaT]))
            w_done()
        wv, wb = (RES["in0_1"][0], Bgated) if pre else w_get("in0_1")
        def ka_evac(pb):
            k.op("act", lambda e: e.copy(out=kaT[:, 128:128 + N], in_=PS[pb][:, 0:N]), reads=[BPS[pb]], writes=[BkaT])
            if (not pre) and gtiles[-1] == NMAIN - 1:
                k.op("act", lambda e: e.copy(out=kf32[:, :], in_=PS[pb][:, N - 128:N]), reads=[BPS[pb]], writes=[Bkf32])
        fm_chunk(wv, wb, 0, 128, N, ka_evac)
        pb = palloc()
        for kc in range(8):
            mm(PS[pb][64:128, 0:N], wv[:, kc, 0:64], hT[:, kc, 0:N], kc == 0, kc == 7, [wb, BhT[kc]], [BPS[pb]], signal=False)
        for kc in range(8):
            mm(PS[pb][0:64, 0:N], wv[:, kc, 64:128], hT[:, kc, 0:N], kc == 0, kc == 7, [wb, BhT[kc]], [BPS[pb]])
        k.op("act", lambda e: e.copy(out=kaT2[:, 128:128 + N], in_=PS[pb][:, 0:N]), reads=[BPS[pb]], writes=[BkaT2])
        pfree_(pb)
        for ti in range(nt):
            pb = palloc()
            for kc in range(8):
                mm(PS[pb][:, 0:128], hT[:, kc, ti * 128:(ti + 1) * 128], wv[:, kc, 128:256], kc == 0, kc == 7, [wb, BhT[kc]], [BPS[pb]])
            k.op("act", lambda e: e.copy(out=vtok[:, 1 + ti, :], in_=PS[pb][:, 0:128]), reads=[BPS[pb]], writes=[Bvtok])
            if (not pre) and gtiles[ti] == NMAIN - 1:
                k.op("act", lambda e: e.copy(out=osm[:, 128:256], in_=PS[pb][:, 0:128]), reads=[BPS[pb]], writes=[Bosm])
            pfree_(pb)
        if not pre:
            for p in range(2):
                fm_chunk(wv, wb, 256 + p * 128, 128, N, lambda pb: k.op(
                    "act", lambda e: e.copy(out=qbT[:, p, 0:N], in_=PS[pb][:, 0:N]), reads=[BPS[pb]], writes=[BqbT]))
        if not pre:
            w_done()
        ck(3)
        if pre:
            wv2, wb2 = RES["in0_2"][0], Bgated
            wv3, wb3 = RES["in0_3"][0], Bgated
            wv4, wb4 = RES["in0_4"][0], Bgated
        else:
            wv2, wb2 = w_get("in0_2")
            for p in range(2):
                fm_chunk(wv2, wb2, p * 128, 128, N, lambda pb: k.op(
                    "act", lambda e: e.copy(out=kbT[:, p, 0:N], in_=PS[pb][:, 0:N]), reads=[BPS[pb]], writes=[BkbT]))
            wv3, wb3 = w_get_ahead("in0_3", 1)
            wv4, wb4 = w_get_ahead("in0_4", 2)
        if not pre:
            for h in range(4):
                src_v, src_b, c0 = (wv3, wb3, 256 + h * 128) if h < 2 else (wv4, wb4, (h - 2) * 128)
                fm_chunk(src_v, src_b, c0, 128, N, lambda pb: k.op(
                    "act", lambda e: e.activation(out=srbT[:, h, 0:N], in_=PS[pb][:, 0:N], func=AF.Silu),
                    reads=[BPS[pb]], writes=[BsrbT]))
        fm_chunk(wv4, wb4, 256, 16, N, lambda pb: k.op(
            "act", lambda e: e.copy(out=glrT[0:16, 0:N], in_=PS[pb][0:16, 0:N]), reads=[BPS[pb]], writes=[BglrT]))

        ck(4)
        def gla_body(ti):
            c0 = ti * 128
            gt = gtiles[ti]
            if ti == 1:
                ck(7)
            pg = palloc()
            mm(PS[pg][:, 0:256], glrT[0:32, c0:c0 + 128], wg[0:32, :], True, True, [BglrT, Bwg], [BPS[pg]])
            ck(51)
            k.op("act", lambda e: e.activation(out=l_sb[:, :], in_=PS[pg][:, 0:256], func=AF.Exp, scale=-1.0),
                 reads=[BPS[pg]], writes=[Bl])
            pfree_(pg)
            ck(52)
            k.op("act", lambda e: e.activation(out=l_sb[:, :], in_=l_sb[:, :], func=AF.Ln, bias=epsD[:, 2:3]), reads=[Bl, Bc], writes=[Bl])
            ck(5)
            yield
            pr = palloc()
            mm(PS[pr][:, 0:256], urev[:, :], l_sb[:, :], True, True, [Bc, Bl], [BPS[pr]])
            k.op("act", lambda e: e.activation(out=er_sb[:, :], in_=PS[pr][:, 0:256], func=AF.Exp), reads=[BPS[pr]], writes=[Ber])
            pfree_(pr)
            yield
            pk = palloc()
            for kc in range(8):
                mm(PS[pk][:, 0:256], hT[:, kc, c0:c0 + 128], wv2[:, kc, 0:256], kc == 0, kc == 7, [wb2, BhT[kc]], [BPS[pk]])
            k.op("dve", lambda e: e.tensor_tensor(out=khat[:, :], in0=PS[pk][:, 0:256], in1=er_sb[:, :], op=ALU.mult),
                 reads=[BPS[pk], Ber], writes=[Bkhat])
            pfree_(pk)
            yield
            pv = palloc()
            for kc in range(8):
                mm(PS[pv][:, 0:256], hT[:, kc, c0:c0 + 128], wv2[:, kc, 256:512], kc == 0, kc == 7, [wb2, BhT[kc]], [BPS[pv]], signal=False)
            for kc in range(8):
                mm(PS[pv][:, 256:512], hT[:, kc, c0:c0 + 128], wv3[:, kc, 0:256], kc == 0, kc == 7, [wb3, BhT[kc]], [BPS[pv]])
            k.op("act", lambda e: e.copy(out=vbtok[:, :], in_=PS[pv][:, :]), reads=[BPS[pv]], writes=[Bvbtok])
            pfree_(pv)
            ck(6)
            yield
            pbk = palloc()
            for p in range(2):
                mm(PS[pbk][:, p * 128:(p + 1) * 128], l_sb[:, p * 128:(p + 1) * 128], ucum[:, :], True, True, [Bl, Bc], [BPS[pbk]])
            k.op("act", lambda e: e.activation(out=e1_sb[:, :], in_=PS[pbk][:, 0:256], func=AF.Exp), reads=[BPS[pbk]], writes=[Be1])
            if not pre:
                k.op("act", lambda e: e.activation(out=e2_sb[:, :], in_=PS[pbk][:, 0:256], func=AF.Exp, scale=-1.0),
                     reads=[BPS[pbk]], writes=[Be2])
            pfree_(pbk)
            if not pre:
                k.op("dve", lambda e: e.scalar_tensor_tensor(
                    out=qtl[:, :, :], in0=qbT[:, :, c0:c0 + 128], scalar=0.125,
                    in1=e1_sb[:, :].rearrange("p (a b) -> p a b", a=2), op0=ALU.mult, op1=ALU.mult),
                    reads=[BqbT, Be1], writes=[Bqtl])
                k.op("dve", lambda e: e.tensor_tensor(
                    out=ktl[:, :, :], in0=kbT[:, :, c0:c0 + 128], in1=e2_sb[:, :].rearrange("p (a b) -> p a b", a=2),
                    op=ALU.mult), reads=[BkbT, Be2], writes=[Bktl])
                po = palloc()
            yield
            pdc = [palloc(), palloc()]
            for h in range(4):
                p, hp = h // 2, (h % 2) * 64
                if not pre:
                    pa = palloc()
                    mm(PS[pa][:, 0:128], ktl[hp:hp + 64, p, :], qtl[hp:hp + 64, p, :], True, True, [Bktl, Bqtl], [BPS[pa]])
                    ia = rot("atm", 2)
                    k.op("dve", lambda e: e.tensor_tensor(out=atm[ia][:, :], in0=PS[pa][:, 0:128], in1=maska[:, :], op=ALU.mult),
                         reads=[BPS[pa], Bc], writes=[Batm[ia]])
                    pfree_(pa)
                    mm(PS[po][:, h * 128:(h + 1) * 128], vbtok[:, h * 128:(h + 1) * 128], atm[ia][:, :], h == 0, False,
                       [Bvbtok, Batm[ia]], [BPS[po]], signal=False, skip=True)
                for c in range(2):
                    mm(PS[pdc[c]][hp:hp + 64, p * 128:(p + 1) * 128], khat[c * 64:(c + 1) * 64, p * 128 + hp:p * 128 + hp + 64],
                       vbtok[c * 64:(c + 1) * 64, h * 128:(h + 1) * 128], True, True, [Bkhat, Bvbtok], [BPS[pdc[c]]])
            yield
            for p in range(2):
                for c in range(2):
                    if not pre:
                        for half in range(2):
                            h = p * 2 + half
                            hp = half * 64
                            mm(PS[po][:, h * 128 + c * 64:h * 128 + (c + 1) * 64], S_b[:, h, c, :],
                               qtl[:, p, c * 64:(c + 1) * 64], False, c == 1, [BSb[p][c], Bqtl], [BPS[po]], skip=True)
                    dcol = p * 128 + c * 64 + 63
                    k.op("dve", lambda e: e.scalar_tensor_tensor(
                        out=S_f[:, p, :], in0=S_f[:, p, :], scalar=e1_sb[:, dcol:dcol + 1],
                        in1=PS[pdc[c]][:, p * 128:(p + 1) * 128], op0=ALU.mult, op1=ALU.add),
                        reads=[BS, Be1, BPS[pdc[c]]], writes=[BS])
                    nslot = (c + 1) % 2
                    for half in range(2):
                        hp = half * 64
                        k.op("act", lambda e: e.copy(out=S_b[hp:hp + 64, p * 2 + half, nslot, :], in_=S_f[hp:hp + 64, p, :]),
                             reads=[BS], writes=[BSb[p][nslot]])
            pfree_(pdc[0]); pfree_(pdc[1])
            if pre:
                return
            yield
            ck(10)
            k.op("act", lambda e: e.activation(out=osq[:, :], in_=PS[po][:, :], func=AF.Square), reads=[BPS[po]], writes=[Bosq])
            yield
            ps2 = palloc()
            mm(PS[ps2][:, :], ones[:, :], osq[:, :], True, True, [Bc, Bosq], [BPS[ps2]])
            k.op("act", lambda e: e.activation(out=orstd[:, :], in_=PS[ps2][:, :], func=AF.Ln, bias=epsD[:, 1:2]),
                 reads=[BPS[ps2], Bc], writes=[Borstd])
            k.op("act", lambda e: e.activation(out=orstd[:, :], in_=orstd[:, :], func=AF.Exp, scale=-0.5), reads=[Borstd], writes=[Borstd])
            pfree_(ps2)
            yield
            for h in range(4):
                col = PC_GN + h
                k.op("dve", lambda e: e.scalar_tensor_tensor(
                    out=otmp[:, h * 128:(h + 1) * 128], in0=PS[po][:, h * 128:(h + 1) * 128], scalar=PT[:, col:col + 1],
                    in1=orstd[:, h * 128:(h + 1) * 128], op0=ALU.mult, op1=ALU.mult), reads=[BPS[po], BPT, Borstd], writes=[Botmp])
            pfree_(po)
            k.op("pool", lambda e: e.tensor_tensor(out=mixT[:, 4:8, c0:c0 + 128], in0=otmp[:, :].rearrange("p (a b) -> p a b", a=4),
                                                   in1=srbT[:, :, c0:c0 + 128], op=ALU.mult), reads=[Botmp, BsrbT], writes=[BmixT])
        def swa_body(ti):
            c0 = ti * 128
            gt = gtiles[ti]
            ck(11)
            for kv in range(2):
                pscs = [palloc(), palloc()]
                for kb_ in range(2):
                    kc0 = c0 + kb_ * 128
                    for par in range(2):
                        ksrc, kB = (kaT, BkaT) if kv == par else (kaT2, BkaT2)
                        mm(PS[pscs[par]][:, kb_ * 256:(kb_ + 1) * 256], ksrc[par * 64:(par + 1) * 64, kc0:kc0 + 128],
                           qaT[par * 64:(par + 1) * 64, 2 * kv:2 * kv + 2, c0:c0 + 128], True, True, [kB, BqaT], [BPS[pscs[par]]])
                yield
                for par in range(2):
                    i = rot("sc", 2)
                    bi = kv * 2 + par
                    psc = pscs[par]
                    if gt == 1:
                        k.op("dve", lambda e: e.scalar_tensor_tensor(out=sc[i][:, 0:256], in0=PS[psc][:, 0:256], scalar=flg[:, 1:2],
                                                                     in1=biasT[:, bi, 0:256], op0=ALU.add, op1=ALU.add),
                             reads=[BPS[psc], Bc], writes=[Bsc[i]])
                        k.op("dve", lambda e: e.tensor_tensor(out=sc[i][:, 256:512], in0=PS[psc][:, 256:512], in1=biasT[:, bi, 256:512],
                                                              op=ALU.add), reads=[BPS[psc], Bc], writes=[Bsc[i]])
                    else:
                        k.op("dve", lambda e: e.tensor_tensor(out=sc[i][:, :], in0=PS[psc][:, :], in1=biasT[:, bi, :], op=ALU.add),
                             reads=[BPS[psc], Bc], writes=[Bsc[i]])
                    pfree_(psc)
                    k.op("act", lambda e: e.activation(out=ptT[:, bi, :], in_=sc[i][:, :], func=AF.Exp, scale=0.125),
                         reads=[Bsc[i]], writes=[Bpt[bi]])
            ck(12)
            yield
            poa = palloc()
            pdn = palloc()
            for kv in range(2):
                for par in range(2):
                    for kb_ in range(2):
                        bi = kv * 2 + par
                        rhs = ptT[:, bi, kb_ * 256:(kb_ + 1) * 256]
                        outo = PS[poa][par * 64:(par + 1) * 64, 2 * kv * 128:(2 * kv + 2) * 128]
                        outd = PS[pdn][par * 64:(par + 1) * 64, 2 * kv * 128:(2 * kv + 2) * 128]
                        mm(outo, vtok[:, ti + kb_, kv * 64:(kv + 1) * 64], rhs, kb_ == 0, kb_ == 1, [Bvtok, Bpt[bi]], [BPS[poa]], signal=False)
                        mm(outd, ones[:, 0:64], rhs, kb_ == 0, kb_ == 1, [Bc, Bpt[bi]], [BPS[pdn]], signal=(kb_ == 1))
            k.op("dve", lambda e: e.tensor_tensor(out=rec[:, :].rearrange("p (c q) -> p c q", c=4),
                                                  in0=PS[pdn][:, :].rearrange("p (c q) -> p c q", c=4),
                                                  in1=sinkE[:, :].unsqueeze(2).to_broadcast([128, 4, 128]), op=ALU.add),
                 reads=[BPS[pdn], BsinkE], writes=[Brec])
            pfree_(pdn)
            yield
            k.op("dve", lambda e: e.reciprocal(out=rec[:, :], in_=rec[:, :]), reads=[Brec], writes=[Brec])
            k.op("dve", lambda e: e.tensor_tensor(out=mixT[:, 0:4, c0:c0 + 128], in0=PS[poa][:, :].rearrange("p (c q) -> p c q", c=4),
                                                  in1=rec[:, :].rearrange("p (c q) -> p c q", c=4), op=ALU.mult),
                 reads=[BPS[poa], Brec], writes=[BmixT])
            pfree_(poa)
        for ti in range(nt):
            alive = [gla_body(ti)] if pre else [gla_body(ti), swa_body(ti)]
            while alive:
                for g_ in list(alive):
                    try:
                        next(g_)
                    except StopIteration:
                        alive.remove(g_)
        if not pre:
            w_done(); w_done(); w_done()
        if (not pre) and gtiles[-1] == NMAIN - 1:
            pb = palloc()
            k.op("pe", lambda e: e.transpose(out=PS[pb][:, 0:128], in_=kf32[:, :], identity=ident[:, :]),
                 reads=[Bkf32, Bc], writes=[BPS[pb]])
            k.op("act", lambda e: e.copy(out=osm[:, 0:128], in_=PS[pb][:, 0:128]), reads=[BPS[pb]], writes=[Bosm])
            pfree_(pb)
            k.dma(k_out[:, :], osm[:, 0:128], reads=[Bosm], sem="d_mk")
            k.dma(v_out[:, :], osm[:, 128:256], reads=[Bosm], sem="d_mv")
            k.dma(g_out.rearrange("(p h) d v -> (h d) p v", h=2), S_f[:, :, :], reads=[BS], sem="d_mg")
        k.op("dve", lambda e: e.tensor_copy(out=kaT[:, 0:128], in_=kaT[:, nt * 128:(nt + 1) * 128]), reads=[BkaT], writes=[BkaT])
        k.op("dve", lambda e: e.tensor_copy(out=kaT2[:, 0:128], in_=kaT2[:, nt * 128:(nt + 1) * 128]), reads=[BkaT2], writes=[BkaT2])
        k.op("dve", lambda e: e.tensor_copy(out=vtok[:, 0, :], in_=vtok[:, nt, :]), reads=[Bvtok], writes=[Bvtok])

    def odd_mixer(N, last, seg=False):
        wvs = {}
        for c in range(8):
            chunk_aps = []
            for q in range(3):
                idx = c * 3 + q
                bj, bq = idx // 4, idx % 4
                if bj not in wvs:
                    wvs[bj] = w_get_ahead("in1_%d" % bj, len(wvs))
                chunk_aps.append((wvs[bj][0], wvs[bj][1], bq * 128, bj))
            i = rot("cu", 2)
            wv, wb, c0, bj = chunk_aps[0]
            pcg = palloc()
            for kc in range(8):
                mm(PS[pcg][:, 0:N], wv[:, kc, c0:c0 + 128], hT[:, kc, 0:N], kc == 0, kc == 7, [wb, BhT[kc]], [BPS[pcg]])
            wv, wb, c0, bj = chunk_aps[1]
            pu = palloc()
            for kc in range(8):
                mm(PS[pu][:, 0:N], wv[:, kc, c0:c0 + 128], hT[:, kc, 0:N], kc == 0, kc == 7, [wb, BhT[kc]], [BPS[pu]])
            k.op("act", lambda e: e.copy(out=ucp[i][:, 0:N], in_=PS[pu][:, 0:N]), reads=[BPS[pu]], writes=[Bucp[i]])
            pfree_(pu)
            k.op("dve", lambda e: e.tensor_tensor(out=cu[i][:, 0:N], in0=PS[pcg][:, 0:N], in1=ucp[i][:, 0:N], op=ALU.mult),
                 reads=[BPS[pcg], Bucp[i]], writes=[Bcu[i]])
            pfree_(pcg)
            w0 = PT[:, PC_CW + 0 * 8 + c:PC_CW + 0 * 8 + c + 1]
            w1 = PT[:, PC_CW + 1 * 8 + c:PC_CW + 1 * 8 + c + 1]
            w2 = PT[:, PC_CW + 2 * 8 + c:PC_CW + 2 * 8 + c + 1]
            conv_taps(zt[i], Bzt[i], cu[i], Bcu[i], (ccs[:, c, :, :] if seg else ccar[:, c, :]), (Bccs if seg else Bccar), w0, w1, w2, None, N, seg)
            wv, wb, c0, bj = chunk_aps[2]
            pbg = palloc()
            for kc in range(8):
                mm(PS[pbg][:, 0:N], wv[:, kc, c0:c0 + 128], hT[:, kc, 0:N], kc == 0, kc == 7, [wb, BhT[kc]], [BPS[pbg]])
            k.op("dve", lambda e: e.tensor_tensor(out=mixT[:, c, 0:N], in0=PS[pbg][:, 0:N], in1=zt[i][:, 0:N], op=ALU.mult),
                 reads=[BPS[pbg], Bzt[i]], writes=[BmixT])
            pfree_(pbg)
            done_upto = (c * 3 + 3) // 4
            for bj in sorted(list(wvs.keys())):
                if bj < done_upto:
                    del wvs[bj]
                    w_done()
        assert not wvs

    def ffn(l, N, seg=False):
        pend = []

        def flush():
            while pend:
                i_, c_ = pend.pop(0)
                k.op("act", lambda e: e.activation(out=tg[i_][:, 0:N], in_=tg[i_][:, 0:N], func=AF.Gelu_apprx_tanh),
                     reads=[Btg[i_]], writes=[Btg[i_]])
                k.op("pool", lambda e: e.tensor_tensor(out=gated[:, c_, 0:N], in0=tg[i_][:, 0:N], in1=tv[i_][:, 0:N], op=ALU.mult),
                     reads=[Btg[i_], Btv[i_]], writes=[Bgated])

        for j in range(11):
            wv, wb = w_get("up%d_%d" % (l, j))
            for jj in range(2):
                c = 2 * j + jj
                i = rot("tg", 2)
                pgt = palloc()
                for kc in range(8):
                    mm(PS[pgt][:, 0:N], wv[:, kc, jj * 128:(jj + 1) * 128], hT[:, kc, 0:N], kc == 0, kc == 7, [wb, BhT[kc]], [BPS[pgt]])
                pvl = palloc()
                for kc in range(8):
                    mm(PS[pvl][:, 0:N], wv[:, kc, 256 + jj * 128:256 + (jj + 1) * 128], hT[:, kc, 0:N], kc == 0, kc == 7, [wb, BhT[kc]], [BPS[pvl]])
                first = True
                for (pp, tt, tB, cc) in ((pgt, tg[i], Btg[i], c), (pvl, tv[i], Btv[i], 22 + c)):
                    w0 = PT[:, PC_FW + (l * 3 + 0) * 44 + cc:PC_FW + (l * 3 + 0) * 44 + cc + 1]
                    w1 = PT[:, PC_FW + (l * 3 + 1) * 44 + cc:PC_FW + (l * 3 + 1) * 44 + cc + 1]
                    w2 = PT[:, PC_FW + (l * 3 + 2) * 44 + cc:PC_FW + (l * 3 + 2) * 44 + cc + 1]
                    bb = PT[:, PC_FB + l * 44 + cc:PC_FB + l * 44 + cc + 1]
                    conv_taps(tt, tB, PS[pp], BPS[pp], (fcs[:, cc, :, :] if seg else fcar[:, l, cc, :]), (Bfcs if seg else Bfcar), w0, w1, w2, bb, N, seg,
                              after_act=(flush if first else None))
                    first = False
                    pfree_(pp)
                pend.append((i, c))
            w_done()
        flush()
        proj_out(["dn%d_%d" % (l, m) for m in range(8)], gated, Bgated, 22, N, 1)

    def state_rows_out(src_cols, ncol, dst_rows_ap, stage_off):
        done = 0
        while done < ncol:
            n = min(4, ncol - done)
            pb = palloc()
            for q in range(n):
                ap_, bf_ = src_cols[done + q]
                k.op("pe", lambda e: e.transpose(out=PS[pb][0:2, q * 128:(q + 1) * 128], in_=ap_, identity=ident[:, :]),
                     reads=[bf_, Bc], writes=[BPS[pb]])
            k.op("act", lambda e: e.copy(out=rowst[:, stage_off + done * 128:stage_off + (done + n) * 128], in_=PS[pb][0:2, 0:n * 128]),
                 reads=[BPS[pb]], writes=[BmT])
            pfree_(pb)
            done += n


    def sample_unit():
        N = 64
        KcF = biasT[:, 0:2, :].rearrange("p a (s c) -> p (a s) c", c=128)
        biasS = biasT[:, 2:4, :]
        KcT = ptT[:, :, :].rearrange("p a (s c) -> p (a s) c", c=128)
        KcT2 = stg[1].bitcast(BF16)[:, :].rearrange("p (s c) -> p s c", c=128)
        Vc = gated[:, 16:20, :].rearrange("p a (s c) -> p (a s) c", c=128)
        S0bz = gated[:, 0:16, :].rearrange("p s (h v) -> p s h v", h=4)
        S0f = mT[:, :, :].rearrange("p a (s q v) -> p (a s) q v", s=2, q=2)
        khm = hT[0:64, :, :].rearrange("p a (s c) -> p (a s) c", c=256)
        Pc, BPc = osq, Bosq
        Pn, BPn = sq[0], Bsq[0]
        scC, BscC = sc[1], Bsc[1]
        scN, BscN = tmpn[0], Btmpn[0]
        BK2 = Bstg[1]
        k.dma(stg[0][0:64, :], xs[:, :], writes=[Bstg[0]], sem="d_stg0")
        k.dma(stg[0][64:96, :], scs[:, :], writes=[Bstg[0]], sem="d_stg0")
        k.dma(biasS, c_bias_s[:, :, :], writes=[Bc], sem="d_s1")
        for q in range(4):
            k.dma(Vc[:, q * 4:(q + 1) * 4, :], scv[q * 4:(q + 1) * 4].rearrange("s t c -> t s c"), writes=[Bgated], sem="d_s2", q="pool")
        k.op("pool", lambda e: e.memset(gated[:, 0:16, :], 0.0), writes=[Bgated])
        for sq_ in range(16):
            k.dma(S0f[:, sq_, :, :], sgl[sq_].rearrange("(p h) d v -> (h d) p v", h=2), writes=[BmT], sem="d_s3")
        for half in range(2):
            pb = palloc()
            for c4 in range(4):
                c = half * 4 + c4
                k.op("pe", lambda e: e.transpose(out=PS[pb][:, c4 * 64:(c4 + 1) * 64], in_=stg[0][0:64, c * 128:(c + 1) * 128],
                                                 identity=ident[0:64, 0:64]), reads=[Bstg[0], Bc], writes=[BPS[pb]])
            k.op("act", lambda e: e.copy(out=xT[:, half * 4:(half + 1) * 4, 0:64], in_=PS[pb][:, 0:256].rearrange("p (a b) -> p a b", a=4)),
                 reads=[BPS[pb]], writes=[BxT])
            pfree_(pb)
        pb = palloc()
        for c in range(8):
            k.op("pe", lambda e: e.transpose(out=PS[pb][:, c * 32:(c + 1) * 32], in_=stg[0][64:96, c * 128:(c + 1) * 128],
                                             identity=ident[64:96, 64:96]), reads=[Bstg[0], Bc], writes=[BPS[pb]])
        k.op("act", lambda e: e.copy(out=ccs[:, :, :, :].rearrange("p c s r -> p c (s r)"), in_=PS[pb][:, 0:256].rearrange("p (c x) -> p c x", c=8)),
             reads=[BPS[pb]], writes=[Bccs])
        pfree_(pb)
        for hf in range(2):
            k.dma(KcF, sck[hf * 8:(hf + 1) * 8].rearrange("s t c -> t s c"), writes=[Bc], sem="d_s4")
            for s8 in range(8):
                sq_ = hf * 8 + s8
                pb = palloc()
                k.op("pe", lambda e: e.transpose(out=PS[pb][:, 0:128], in_=KcF[:, s8, :], identity=ident[:, :]), reads=[Bc], writes=[BPS[pb]])
                mm(PS[pb][64:128, 128:256], KcF[:, s8, 0:64], ident[:, :], True, True, [Bc], [BPS[pb]])
                mm(PS[pb][0:64, 128:256], KcF[:, s8, 64:128], ident[:, :], True, True, [Bc], [BPS[pb]])
                k.op("act", lambda e: e.copy(out=KcT[:, sq_, :], in_=PS[pb][:, 0:128]), reads=[BPS[pb]], writes=[Bpt[0]])
                k.op("act", lambda e: e.copy(out=KcT2[:, sq_, :], in_=PS[pb][:, 128:256]), reads=[BPS[pb]], writes=[BK2])
                pfree_(pb)
        k.dma(ks_out[:, 0:124, :], sck[:, 4:128, :], sem="d_so")
        k.dma(vs_out[:, 0:124, :], scv[:, 4:128, :], sem="d_so")

        pre_norm(0, 0, N)
        wv, wb = w_get("in0_0")
        for c in range(4):
            fm_chunk(wv, wb, c * 128, 128, N, lambda pb: k.op(
                "act", lambda e: e.copy(out=qaT[:, c, 0:N], in_=PS[pb][:, 0:N]), reads=[BPS[pb]], writes=[BqaT]))
        w_done()
        wv, wb = w_get("in0_1")

        def ka_evac(pb):
            k.op("act", lambda e: e.copy(out=kaT[:, 0:N], in_=PS[pb][:, 0:N]), reads=[BPS[pb]], writes=[BkaT])
            k.op("act", lambda e: e.copy(out=kf32[:, 0:N], in_=PS[pb][:, 0:N]), reads=[BPS[pb]], writes=[Bkf32])
        fm_chunk(wv, wb, 0, 128, N, ka_evac)
        pb = palloc()
        for kc in range(8):
            mm(PS[pb][64:128, 0:N], wv[:, kc, 0:64], hT[:, kc, 0:N], kc == 0, kc == 7, [wb, BhT[kc]], [BPS[pb]], signal=False)
        for kc in range(8):
            mm(PS[pb][0:64, 0:N], wv[:, kc, 64:128], hT[:, kc, 0:N], kc == 0, kc == 7, [wb, BhT[kc]], [BPS[pb]])
        k.op("act", lambda e: e.copy(out=kaT2[:, 0:N], in_=PS[pb][:, 0:N]), reads=[BPS[pb]], writes=[BkaT2])
        pfree_(pb)
        pb = palloc()
        for kc in range(8):
            mm(PS[pb][0:64, 0:128], hT[:, kc, 0:64], wv[:, kc, 128:256], kc == 0, kc == 7, [wb, BhT[kc]], [BPS[pb]])
        k.op("act", lambda e: e.copy(out=vtok[0:64, 1, :], in_=PS[pb][0:64, 0:128]), reads=[BPS[pb]], writes=[Bvtok])
        k.op("act", lambda e: e.copy(out=osm[0:64, 128:256], in_=PS[pb][0:64, 0:128]), reads=[BPS[pb]], writes=[Bosm])
        pfree_(pb)
        for p in range(2):
            fm_chunk(wv, wb, 256 + p * 128, 128, N, lambda pb: k.op(
                "act", lambda e: e.copy(out=qbT[:, p, 0:N], in_=PS[pb][:, 0:N]), reads=[BPS[pb]], writes=[BqbT]))
        w_done()
        wv2, wb2 = w_get("in0_2")
        for p in range(2):
            fm_chunk(wv2, wb2, p * 128, 128, N, lambda pb: k.op(
                "act", lambda e: e.copy(out=kbT[:, p, 0:N], in_=PS[pb][:, 0:N]), reads=[BPS[pb]], writes=[BkbT]))
        wv3, wb3 = w_get_ahead("in0_3", 1)
        wv4, wb4 = w_get_ahead("in0_4", 2)
        for h in range(4):
            src_v, src_b, c0 = (wv3, wb3, 256 + h * 128) if h < 2 else (wv4, wb4, (h - 2) * 128)
            fm_chunk(src_v, src_b, c0, 128, N, lambda pb: k.op(
                "act", lambda e: e.activation(out=srbT[:, h, 0:N], in_=PS[pb][:, 0:N], func=AF.Silu), reads=[BPS[pb]], writes=[BsrbT]))
        fm_chunk(wv4, wb4, 256, 16, N, lambda pb: k.op(
            "act", lambda e: e.copy(out=glrT[0:16, 0:N], in_=PS[pb][0:16, 0:N]), reads=[BPS[pb]], writes=[BglrT]))
        pb = palloc()
        k.op("pe", lambda e: e.transpose(out=PS[pb][0:64, 0:128], in_=kf32[:, 0:64], identity=ident[:, :]), reads=[Bkf32, Bc], writes=[BPS[pb]])
        k.op("act", lambda e: e.copy(out=osm[0:64, 0:128], in_=PS[pb][0:64, 0:128]), reads=[BPS[pb]], writes=[Bosm])
        pfree_(pb)
        for i4 in range(4):
            k.dma(ks_out[:, 124 + i4, :], osm[i4:64:4, 0:128], reads=[Bosm], sem="d_so")
            k.dma(vs_out[:, 124 + i4, :], osm[i4:64:4, 128:256], reads=[Bosm], sem="d_so")

        pg = palloc()
        mm(PS[pg][0:64, 0:256], glrT[0:32, 0:64], wg[0:32, :], True, True, [BglrT, Bwg], [BPS[pg]])
        k.op("act", lambda e: e.activation(out=l_sb[0:64, :], in_=PS[pg][0:64, 0:256], func=AF.Exp, scale=-1.0), reads=[BPS[pg]], writes=[Bl])
        pfree_(pg)
        k.op("act", lambda e: e.activation(out=l_sb[0:64, :], in_=l_sb[0:64, :], func=AF.Ln, bias=epsD[0:64, 2:3]), reads=[Bl, Bc], writes=[Bl])
        pr = palloc()
        mm(PS[pr][0:64, 0:256], urev_s[:, :], l_sb[0:64, :], True, True, [Bc, Bl], [BPS[pr]])
        k.op("act", lambda e: e.activation(out=er_sb[0:64, :], in_=PS[pr][0:64, 0:256], func=AF.Exp), reads=[BPS[pr]], writes=[Ber])
        pfree_(pr)
        pk = palloc()
        for kc in range(8):
            mm(PS[pk][0:64, 0:256], hT[:, kc, 0:64], wv2[:, kc, 0:256], kc == 0, kc == 7, [wb2, BhT[kc]], [BPS[pk]])
        k.op("dve", lambda e: e.tensor_tensor(out=khat[0:64, :], in0=PS[pk][0:64, 0:256], in1=er_sb[0:64, :], op=ALU.mult),
             reads=[BPS[pk], Ber], writes=[Bkhat])
        pfree_(pk)
        pv = palloc()
        for kc in range(8):
            mm(PS[pv][0:64, 0:256], hT[:, kc, 0:64], wv2[:, kc, 256:512], kc == 0, kc == 7, [wb2, BhT[kc]], [BPS[pv]], signal=False)
        for kc in range(8):
            mm(PS[pv][0:64, 256:512], hT[:, kc, 0:64], wv3[:, kc, 0:256], kc == 0, kc == 7, [wb3, BhT[kc]], [BPS[pv]])
        k.op("act", lambda e: e.copy(out=vbtok[0:64, :], in_=PS[pv][0:64, :]), reads=[BPS[pv]], writes=[Bvbtok])
        pfree_(pv)
        w_done(); w_done(); w_done()
        pbk = palloc()
        for p in range(2):
            mm(PS[pbk][:, p * 64:(p + 1) * 64], l_sb[0:64, p * 128:(p + 1) * 128], ucum_s[:, :], True, True, [Bl, Bc], [BPS[pbk]])
        k.op("act", lambda e: e.activation(out=e1_sb[:, 0:128], in_=PS[pbk][:, 0:128], func=AF.Exp), reads=[BPS[pbk]], writes=[Be1])
        k.op("act", lambda e: e.activation(out=e2_sb[:, 0:128], in_=PS[pbk][:, 0:128], func=AF.Exp, scale=-1.0), reads=[BPS[pbk]], writes=[Be2])
        pfree_(pbk)
        k.op("dve", lambda e: e.scalar_tensor_tensor(out=qtl[:, :, 0:64], in0=qbT[:, :, 0:64], scalar=0.125,
                                                     in1=e1_sb[:, 0:128].rearrange("p (a b) -> p a b", a=2), op0=ALU.mult, op1=ALU.mult),
             reads=[BqbT, Be1], writes=[Bqtl])
        k.op("dve", lambda e: e.tensor_tensor(out=ktl[:, :, 0:64], in0=kbT[:, :, 0:64], in1=e2_sb[:, 0:128].rearrange("p (a b) -> p a b", a=2),
                                              op=ALU.mult), reads=[BkbT, Be2], writes=[Bktl])
        for p in range(2):
            for half in range(2):
                hp = half * 64
                k.op("act", lambda e: e.copy(out=S0bz[hp:hp + 64, :, p * 2 + half, :], in_=S0f[hp:hp + 64, :, p, :]), reads=[BmT], writes=[Bgated])
        k.op("dve", lambda e: e.tensor_tensor(out=khm, in0=khat[0:64, :].unsqueeze(1).to_broadcast([64, 16, 256]),
                                              in1=rowmask[:, :].unsqueeze(2).to_broadcast([64, 16, 256]), op=ALU.mult),
             reads=[Bkhat, Bc], writes=[BhT])
        po = palloc()
        for h in range(4):
            p, hp = h // 2, (h % 2) * 64
            pa = palloc()
            mm(PS[pa][0:64, 0:64], ktl[hp:hp + 64, p, 0:64], qtl[hp:hp + 64, p, 0:64], True, True, [Bktl, Bqtl], [BPS[pa]])
            ia = rot("atm", 2)
            k.op("dve", lambda e: e.tensor_tensor(out=atm[ia][0:64, 0:64], in0=PS[pa][0:64, 0:64], in1=maska_s[:, :], op=ALU.mult),
                 reads=[BPS[pa], Bc], writes=[Batm[ia]])
            pfree_(pa)
            mm(PS[po][:, h * 64:(h + 1) * 64], vbtok[0:64, h * 128:(h + 1) * 128], atm[ia][0:64, 0:64], h == 0, False,
               [Bvbtok, Batm[ia]], [BPS[po]], signal=False, skip=True)
        for h in range(4):
            p = h // 2
            for sq_ in range(16):
                mm(PS[po][:, h * 64 + sq_ * 4:h * 64 + sq_ * 4 + 4], S0bz[:, sq_, h, :], qtl[:, p, sq_ * 4:sq_ * 4 + 4], False,
                   (h == 3 and sq_ == 15), [Bgated, Bqtl], [BPS[po]], signal=(sq_ == 15), skip=True)
        for r in range(8):
            pd = palloc()
            for s2 in range(2):
                sq_ = r * 2 + s2
                for h in range(4):
                    p, hp = h // 2, (h % 2) * 64
                    mm(PS[pd][hp:hp + 64, (s2 * 2 + p) * 128:(s2 * 2 + p + 1) * 128], khm[:, sq_, p * 128 + hp:p * 128 + hp + 64],
                       vbtok[0:64, h * 128:(h + 1) * 128], True, True, [BhT, Bvbtok], [BPS[pd]])
            for s2 in range(2):
                sq_ = r * 2 + s2
                for p in range(2):
                    dcol = p * 64 + sq_ * 4 + 3
                    k.op("dve", lambda e: e.scalar_tensor_tensor(
                        out=S0f[:, sq_, p, :], in0=S0f[:, sq_, p, :], scalar=e1_sb[:, dcol:dcol + 1],
                        in1=PS[pd][:, (s2 * 2 + p) * 128:(s2 * 2 + p + 1) * 128], op0=ALU.mult, op1=ALU.add),
                        reads=[BmT, Be1, BPS[pd]], writes=[BmT])
            pfree_(pd)
        for sq_ in range(16):
            k.dma(gs_out[sq_].rearrange("(p h) d v -> (h d) p v", h=2), S0f[:, sq_, :, :], reads=[BmT], sem="d_s6")
        k.op("act", lambda e: e.activation(out=osq[:, 0:256], in_=PS[po][:, 0:256], func=AF.Square), reads=[BPS[po]], writes=[Bosq])
        ps2 = palloc()
        mm(PS[ps2][:, 0:256], ones[:, :], osq[:, 0:256], True, True, [Bc, Bosq], [BPS[ps2]])
        k.op("act", lambda e: e.activation(out=orstd[:, 0:256], in_=PS[ps2][:, 0:256], func=AF.Ln, bias=epsD[:, 1:2]), reads=[BPS[ps2], Bc], writes=[Borstd])
        k.op("act", lambda e: e.activation(out=orstd[:, 0:256], in_=orstd[:, 0:256], func=AF.Exp, scale=-0.5), reads=[Borstd], writes=[Borstd])
        pfree_(ps2)
        for h in range(4):
            col = PC_GN + h
            k.op("dve", lambda e: e.scalar_tensor_tensor(out=otmp[:, h * 64:(h + 1) * 64], in0=PS[po][:, h * 64:(h + 1) * 64], scalar=PT[:, col:col + 1],
                                                         in1=orstd[:, h * 64:(h + 1) * 64], op0=ALU.mult, op1=ALU.mult),
                 reads=[BPS[po], BPT, Borstd], writes=[Botmp])
        pfree_(po)
        k.op("pool", lambda e: e.tensor_tensor(out=mixT[:, 4:8, 0:64], in0=otmp[:, 0:256].rearrange("p (a b) -> p a b", a=4),
                                               in1=srbT[:, :, 0:64], op=ALU.mult), reads=[Botmp, BsrbT], writes=[BmixT])

        scb = [[palloc(), palloc()], [palloc(), palloc()]]
        for par in range(2):
            for kv in range(2):
                ksrc, kB = (KcT, Bpt[0]) if kv == par else (KcT2, BK2)
                nsrc, nB = (kaT, BkaT) if kv == par else (kaT2, BkaT2)
                for gi in range(2):
                    for sq_ in range(16):
                        cc0 = kv * 128 + gi * 64 + sq_ * 4
                        mm(PS[scb[0][par]][:, cc0:cc0 + 4], ksrc[par * 64:(par + 1) * 64, sq_, :],
                           qaT[par * 64:(par + 1) * 64, 2 * kv + gi, sq_ * 4:sq_ * 4 + 4], True, True, [kB, BqaT], [BPS[scb[0][par]]],
                           signal=(sq_ == 15))
                mm(PS[scb[1][par]][0:64, kv * 128:(kv + 1) * 128], nsrc[par * 64:(par + 1) * 64, 0:64],
                   qaT[par * 64:(par + 1) * 64, 2 * kv:2 * kv + 2, 0:64], True, True, [nB, BqaT], [BPS[scb[1][par]]])
        for par in range(2):
            k.op("dve", lambda e: e.tensor_tensor(out=scC[:, par * 256:(par + 1) * 256], in0=PS[scb[0][par]][:, 0:256],
                                                  in1=biasS[:, 0, par * 256:(par + 1) * 256], op=ALU.add), reads=[BPS[scb[0][par]], Bc], writes=[BscC])
            k.op("dve", lambda e: e.tensor_tensor(out=scN[0:64, par * 256:(par + 1) * 256], in0=PS[scb[1][par]][0:64, 0:256],
                                                  in1=biasS[0:64, 1, par * 256:(par + 1) * 256], op=ALU.add), reads=[BPS[scb[1][par]], Bc], writes=[BscN])
            pfree_(scb[0][par]); pfree_(scb[1][par])
        k.op("act", lambda e: e.activation(out=Pc[:, :], in_=scC[:, :], func=AF.Exp, scale=0.125), reads=[BscC], writes=[BPc])
        k.op("act", lambda e: e.activation(out=Pn[0:64, :], in_=scN[0:64, :], func=AF.Exp, scale=0.125), reads=[BscN], writes=[BPn])
        poa = palloc()
        pdn = palloc()
        for par in range(2):
            first = True
            for kv in range(2):
                for gi in range(2):
                    for sq_ in range(16):
                        cc0 = kv * 128 + gi * 64 + sq_ * 4
                        oc0 = (2 * kv + gi) * 64 + sq_ * 4
                        rhs = Pc[:, par * 256 + cc0:par * 256 + cc0 + 4]
                        mm(PS[poa][par * 64:(par + 1) * 64, oc0:oc0 + 4], Vc[:, sq_, kv * 64:(kv + 1) * 64], rhs, first, False,
                           [Bgated, BPc], [BPS[poa]], signal=False, skip=True)
                        mm(PS[pdn][par * 64:(par + 1) * 64, oc0:oc0 + 4], ones[:, 0:64], rhs, first, False,
                           [Bc, BPc], [BPS[pdn]], signal=False, skip=True)
                        first = False
                rhs = Pn[0:64, par * 256 + kv * 128:par * 256 + (kv + 1) * 128]
                mm(PS[poa][par * 64:(par + 1) * 64, 2 * kv * 64:(2 * kv + 2) * 64], vtok[0:64, 1, kv * 64:(kv + 1) * 64], rhs, False, True,
                   [Bvtok, BPn], [BPS[poa]], signal=True, skip=True)
                mm(PS[pdn][par * 64:(par + 1) * 64, 2 * kv * 64:(2 * kv + 2) * 64], ones[0:64, 0:64], rhs, False, True,
                   [Bc, BPn], [BPS[pdn]], signal=True, skip=True)
        k.op("dve", lambda e: e.tensor_tensor(out=rec[:, 0:256].rearrange("p (c q) -> p c q", c=4),
                                              in0=PS[pdn][:, 0:256].rearrange("p (c q) -> p c q", c=4),
                                              in1=sinkE[:, :].unsqueeze(2).to_broadcast([128, 4, 64]), op=ALU.add),
             reads=[BPS[pdn], BsinkE], writes=[Brec])
        pfree_(pdn)
        k.op("dve", lambda e: e.reciprocal(out=rec[:, 0:256], in_=rec[:, 0:256]), reads=[Brec], writes=[Brec])
        k.op("dve", lambda e: e.tensor_tensor(out=mixT[:, 0:4, 0:64], in0=PS[poa][:, 0:256].rearrange("p (c q) -> p c q", c=4),
                                              in1=rec[:, 0:256].rearrange("p (c q) -> p c q", c=4), op=ALU.mult),
             reads=[BPS[poa], Brec], writes=[BmixT])
        pfree_(poa)

        proj_out(["out0_0", "out0_1"], mixT, BmixT, 8, N, 4)
        post_norm_add(1, 0, N)

        def ffn_state_in(l):
            for (c0_, w_) in ((0, 4096), (4096, 1536)):
                stage = mT[0:32, :, :].rearrange("p a b -> p (a b)")
                k.dma(stage[:, 0:w_], sfs[l, :, c0_:c0_ + w_], writes=[BmT], sem="d_s5")
                for q0 in range(0, w_ // 128, 16):
                    n = min(16, w_ // 128 - q0)
                    pb = palloc()
                    for q in range(n):
                        k.op("pe", lambda e: e.transpose(out=PS[pb][:, q * 32:(q + 1) * 32], in_=stage[:, (q0 + q) * 128:(q0 + q + 1) * 128],
                                                         identity=ident[0:32, 0:32]), reads=[BmT, Bc], writes=[BPS[pb]])
                    cb = c0_ // 128 + q0
                    k.op("act", lambda e: e.copy(out=fcs[:, cb:cb + n, :, :].rearrange("p c s r -> p c (s r)"),
                                                 in_=PS[pb][:, 0:n * 32].rearrange("p (c x) -> p c x", c=n)), reads=[BPS[pb]], writes=[Bfcs])
                    pfree_(pb)

        def rows_out(src, srcB, nchunk, dst):
            stage = mT[0:32, :, :].rearrange("p a b -> p (a b)")
            done = 0
            while done < nchunk:
                nn = min(32, nchunk - done)
                for q0 in range(0, nn, 4):
                    n = min(4, nn - q0)
                    pb = palloc()
                    for q in range(n):
                        cidx = done + q0 + q
                        k.op("pe", lambda e: e.transpose(out=PS[pb][0:32, q * 128:(q + 1) * 128],
                                                         in_=src[:, cidx, :, :].rearrange("p s r -> p (s r)"), identity=ident[:, :]),
                             reads=[srcB, Bc], writes=[BPS[pb]])
                    k.op("act", lambda e: e.copy(out=stage[:, q0 * 128:(q0 + n) * 128], in_=PS[pb][0:32, 0:n * 128]), reads=[BPS[pb]], writes=[BmT])
                    pfree_(pb)
                k.dma(dst[:, done * 128:(done + nn) * 128], stage[:, 0:nn * 128], reads=[BmT], sem="d_rows")
                done += nn

        pre_norm(2, 0, N)
        ffn_state_in(0)
        ffn(0, N, seg=True)
        post_norm_add(3, 0, N)
        rows_out(fcs, Bfcs, 44, fs_out[0])
        pre_norm(0, 1, N)
        odd_mixer(N, False, seg=True)
        proj_out(["out1_0", "out1_1"], mixT, BmixT, 8, N, 4)
        post_norm_add(1, 1, N)
        rows_out(ccs, Bccs, 8, cs_out)
        pre_norm(2, 1, N)
        ffn_state_in(1)
        ffn(1, N, seg=True)
        post_norm_add(3, 1, N)
        rows_out(fcs, Bfcs, 44, fs_out[1])
        for half in range(2):
            pb = palloc()
            for c4 in range(4):
                c = half * 4 + c4
                k.op("pe", lambda e: e.transpose(out=PS[pb][0:64, c4 * 128:(c4 + 1) * 128], in_=xT[:, c, 0:64], identity=ident[:, :]),
                     reads=[BxT, Bc], writes=[BPS[pb]])
            k.op("act", lambda e: e.copy(out=stg[0][0:64, half * 512:(half + 1) * 512], in_=PS[pb][0:64, :]), reads=[BPS[pb]], writes=[Bstg[0]])
            pfree_(pb)
        k.dma(ys_out[:, :], stg[0][0:64, :], reads=[Bstg[0]], sem="d_stg0")

    def finish():
        for s_, v in k.cnt.items():
            if v and s_ != "sp":
                k.E["sp"].wait_ge(k.sems[s_], v)

    import os
    CUT = int(os.environ.get("KCUT", "99"))
    KSUB = int(os.environ.get("KSUB", "0"))

    class _Stop(Exception):
        pass

    cur = dict(ui=-1)

    def ck(n):
        if KSUB == n and cur["ui"] == CUT - 1:
            raise _Stop()
    load_resident()
    for ui, (mode, ts) in enumerate(units):
        if ui >= CUT:
            break
        cur["ui"] = ui
        nt = len(ts)
        N = nt * 128
        if mode == "pre":
            try:
                load_x(ts, N)
                ck(1)
                pre_norm(0, 0, N)
                ck(2)
                even_mixer("pre", ts, N, [-1] * nt)
            except _Stop:
                break
            continue
        try:
            load_x([NPRE + t for t in ts], N)
            pre_norm(0, 0, N)
            even_mixer("main", ts, N, ts)
            ck(13)
            proj_out(["out0_0", "out0_1"], mixT, BmixT, 8, N, 4)
            ck(141)
            post_norm_add(1, 0, N)
            ck(14)
            pre_norm(2, 0, N)
            ffn(0, N)
            ck(15)
            post_norm_add(3, 0, N)
            pre_norm(0, 1, N)
            ck(16)
            odd_mixer(N, ts[-1] == NMAIN - 1)
            ck(17)
            proj_out(["out1_0", "out1_1"], mixT, BmixT, 8, N, 4)
            post_norm_add(1, 1, N)
            pre_norm(2, 1, N)
            ffn(1, N)
            post_norm_add(3, 1, N)
            ck(18)
            store_y([(ti, t - 1) for ti, t in enumerate(ts) if t >= 1], N)
        except _Stop:
            break
        if ts == [0]:
            k.op("dve", lambda e: e.tensor_scalar(out=fcar[:, :, :, :], in0=fcar[:, :, :, :], scalar1=flg[:, 0:1], scalar2=None,
                                                  op0=ALU.mult), reads=[Bfcar, Bc], writes=[Bfcar])
            k.op("dve", lambda e: e.tensor_scalar(out=ccar[:, :, :], in0=ccar[:, :, :], scalar1=flg[:, 0:1], scalar2=None,
                                                  op0=ALU.mult), reads=[Bccar, Bc], writes=[Bccar])
        if ts[-1] == NMAIN - 1:
            state_rows_out([(ccar[:, c, :], Bccar) for c in range(8)], 8, None, 0)
            k.dma(c_out[:, :], rowst[:, 0:1024], reads=[BmT], sem="d_rows")
            for l in range(2):
                for part in range(4):
                    state_rows_out([(fcar[:, l, part * 11 + c, :], Bfcar) for c in range(11)], 11, None, 0)
                    k.dma(f_out[l, :, part * 1408:(part + 1) * 1408], rowst[:, 0:1408], reads=[BmT], sem="d_rows")

    if with_sample and CUT > len(units):
        sample_unit()
    finish()
    return nc, k


def host_consts():
    c = {}
    c["c_ident"] = np.eye(128, dtype=np.float32)
    c["c_ones"] = np.ones((128, 128), dtype=ml_dtypes.bfloat16)
    j = np.arange(128)[:, None]
    i = np.arange(128)[None, :]
    same = (j // 64) == (i // 64)
    c["c_ucum"] = np.where(same & (j <= i), -1.0 / 16.0, 0.0).astype(np.float32)
    c["c_urev"] = np.where(same & (j > i), -1.0 / 16.0, 0.0).astype(np.float32)
    c["c_maska"] = np.where(same & (j <= i), 1.0, 0.0).astype(np.float32)
    slopes = 2.0 ** (-8.0 * np.arange(1, 9) / 8.0)
    bias = np.zeros((128, 4, 4, 128), np.float32)
    s = np.arange(128)[:, None]
    q = np.arange(128)[None, :]
    for kb_ in range(2):
        dist = (128 + q - s) if kb_ == 0 else (q - s)
        valid = (dist >= 0) & (dist <= 128)
        for kv in range(2):
            for par in range(2):
                for gi in range(2):
                    b = -slopes[kv * 4 + par + 2 * gi] * dist * 8.0
                    bias[:, kv * 2 + par, kb_ * 2 + gi, :] = np.where(valid, b, NEG)
    c["c_bias"] = bias.reshape(128, 4, 512)
    j = np.arange(64)[:, None]
    i = np.arange(64)[None, :]
    same = (j // 4) == (i // 4)
    c["c_ucum_s"] = np.where(same & (j <= i), -1.0 / 16.0, 0.0).astype(np.float32)
    c["c_urev_s"] = np.where(same & (j > i), -1.0 / 16.0, 0.0).astype(np.float32)
    c["c_maska_s"] = np.where(same & (j <= i), 1.0, 0.0).astype(np.float32)
    c["c_rowmask"] = ((np.arange(64)[:, None] // 4) == np.arange(16)[None, :]).astype(np.float32)
    bs = np.full((128, 2, 2, 2, 2, 64), NEG, np.float32)
    tok = np.arange(64)
    ti = tok % 4
    srow = np.arange(128)[:, None]
    for par in range(2):
        for kv in range(2):
            for gi in range(2):
                sl = slopes[kv * 4 + par + 2 * gi]
                dist = 128 + ti[None, :] - srow
                bs[:, 0, par, kv, gi, :] = np.where(srow >= ti[None, :], -sl * dist * 8.0, NEG)
                jj = np.arange(64)[:, None]
                d2 = ti[None, :] - (jj % 4)
                ok = ((jj // 4) == (tok[None, :] // 4)) & (d2 >= 0)
                bs[0:64, 1, par, kv, gi, :] = np.where(ok, -sl * d2 * 8.0, NEG)
    c["c_bias_s"] = bs.reshape(128, 2, 512)
    return c


def host_pvec(inp):
    rows = np.zeros((512, 128), np.float32)
    for q, name in enumerate(("norm_mix_pre", "norm_mix_post", "norm_ffn_pre", "norm_ffn_post")):
        a = np.asarray(inp[name], np.float32)
        for l in range(2):
            rows[(q * 2 + l) * 8:(q * 2 + l) * 8 + 8] = a[l].reshape(8, 128)
    fw = np.asarray(inp["ffn_conv_w"], np.float32)
    for l in range(2):
        for i in range(3):
            rows[PC_FW + (l * 3 + i) * 44:PC_FW + (l * 3 + i) * 44 + 44] = fw[l, i].reshape(44, 128)
    fb = np.asarray(inp["ffn_conv_b"], np.float32)
    for l in range(2):
        rows[PC_FB + l * 44:PC_FB + l * 44 + 44] = fb[l].reshape(44, 128)
    cw = np.asarray(inp["conv_w_odd"], np.float32)
    for i in range(3):
        rows[PC_CW + i * 8:PC_CW + i * 8 + 8] = cw[0, i].reshape(8, 128)
    rows[PC_GN:PC_GN + 4] = np.asarray(inp["gla_norm"], np.float32)[0].reshape(4, 128)
    return rows


_CACHE = {}


def make_in_maps(inp):
    consts = host_consts()
    pvec = host_pvec(inp)
    xp = np.asarray(inp["x_prompt"], np.float32)
    shared = dict(consts)
    shared["pvec"] = pvec
    shared["w_in_even"] = np.ascontiguousarray(np.asarray(inp["w_in_even"], np.float32)[0])
    shared["w_out_even"] = np.ascontiguousarray(np.asarray(inp["w_out_even"], np.float32)[0])
    shared["w_in_odd"] = np.ascontiguousarray(np.asarray(inp["w_in_odd"], np.float32)[0])
    shared["w_out_odd"] = np.ascontiguousarray(np.asarray(inp["w_out_odd"], np.float32)[0])
    for l in range(2):
        shared["ffn_up%d" % l] = np.ascontiguousarray(np.asarray(inp["ffn_up"], np.float32)[l])
        shared["ffn_down%d" % l] = np.ascontiguousarray(np.asarray(inp["ffn_down"], np.float32)[l])
    shared["w_gate_up"] = np.ascontiguousarray(np.asarray(inp["w_gate_up"], np.float32)[0])
    shared["b_gate"] = np.ascontiguousarray(np.asarray(inp["b_gate"], np.float32))
    shared["attn_sinks"] = np.ascontiguousarray(np.asarray(inp["attn_sinks"], np.float32))
    in_maps = []
    for c in range(8):
        b, half = c // 2, c % 2
        xin = np.zeros(((NPRE + NMAIN) * 128, D), np.float32)
        if half == 1:
            xin[:] = xp[b]
        else:
            xin[(NPRE + 1) * 128:] = xp[b, 0:2048]
        m = dict(shared)
        m["xin"] = xin
        fl = np.zeros((128, 2), np.float32)
        fl[:, 0] = float(half)
        fl[:, 1] = (float(half) - 1.0) * (-NEG)
        m["flagv"] = fl
        sl = slice(c * NSEQ, (c + 1) * NSEQ)
        m["xs"] = np.ascontiguousarray(np.asarray(inp["x_sample"], np.float32)[sl].reshape(64, D))
        m["sck"] = np.ascontiguousarray(np.asarray(inp["cache_swa_k"], np.float32)[0, sl].reshape(16, 128, 128))
        m["scv"] = np.ascontiguousarray(np.asarray(inp["cache_swa_v"], np.float32)[0, sl].reshape(16, 128, 128))
        m["sgl"] = np.ascontiguousarray(np.asarray(inp["state_gla"], np.float32)[0, sl])
        m["scs"] = np.ascontiguousarray(np.asarray(inp["state_conv"], np.float32)[0, sl].reshape(32, D))
        m["sfs"] = np.ascontiguousarray(np.asarray(inp["state_ffn"], np.float32)[:, sl].reshape(2, 32, F2))
        in_maps.append(m)
    return in_maps


def kernel(**inp):
    if "nc" not in _CACHE:
        _CACHE["nc"] = build_program()[0]
    nc = _CACHE["nc"]
    in_maps = make_in_maps(inp)
    import os
    if os.environ.get("KCORES"):
        lc = [int(x) for x in os.environ["KCORES"].split(",")]
        res = run_bass_kernel_spmd(nc, [in_maps[c] for c in lc], core_ids=list(range(len(lc))))
        R = [None] * 8
        for i, c in enumerate(lc):
            R[c] = res.results[i]
        for c in range(8):
            if R[c] is None:
                R[c] = {kk: np.zeros_like(vv) for kk, vv in res.results[0].items()}
    else:
        res = run_bass_kernel_spmd(nc, in_maps, core_ids=list(range(8)))
        R = res.results
    y_prompt = np.zeros((4, 4096, D), np.float32)
    swa_k = np.zeros((1, 4, 128, 2, 64), np.float32)
    swa_v = np.zeros((1, 4, 128, 2, 64), np.float32)
    gla = np.zeros((1, 4, 4, 64, 128), np.float32)
    conv = np.zeros((1, 4, 2, D), np.float32)
    ffn_s = np.zeros((2, 4, 2, F2), np.float32)
    for c in range(8):
        b, half = c // 2, c % 2
        y_prompt[b, half * 2048:(half + 1) * 2048] = R[c]["y_out"]
        if half == 1:
            swa_k[0, b] = R[c]["k_out"].reshape(128, 2, 64)
            swa_v[0, b] = R[c]["v_out"].reshape(128, 2, 64)
            gla[0, b] = R[c]["g_out"]
            conv[0, b] = R[c]["c_out"]
            ffn_s[:, b] = R[c]["f_out"]
    y_s = np.zeros((128, 4, D), np.float32)
    ks_s = np.zeros((1, 128, 128, 2, 64), np.float32)
    vs_s = np.zeros((1, 128, 128, 2, 64), np.float32)
    gs_s = np.zeros((1, 128, 4, 64, 128), np.float32)
    cs_s = np.zeros((1, 128, 2, D), np.float32)
    fs_s = np.zeros((2, 128, 2, F2), np.float32)
    for c in range(8):
        sl = slice(c * NSEQ, (c + 1) * NSEQ)
        y_s[sl] = R[c]["ys_out"].reshape(16, 4, D)
        ks_s[0, sl] = R[c]["ks_out"].reshape(16, 128, 2, 64)
        vs_s[0, sl] = R[c]["vs_out"].reshape(16, 128, 2, 64)
        gs_s[0, sl] = R[c]["gs_out"]
        cs_s[0, sl] = R[c]["cs_out"].reshape(16, 2, D)
        fs_s[:, sl] = R[c]["fs_out"].reshape(2, 16, 2, F2)
    return (y_prompt, y_s, swa_k, swa_v, gla, conv, ffn_s, ks_s, vs_s, gs_s, cs_s, fs_s)
```

```python
import numpy as np
import ml_dtypes
import concourse.bass as bass
import concourse.mybir as mybir
from concourse.bass_utils import run_bass_kernel_spmd

F32 = mybir.dt.float32
BF16 = mybir.dt.bfloat16
AF = mybir.ActivationFunctionType
ALU = mybir.AluOpType
AX = mybir.AxisListType

D = 1024
DFF = 2816
F2 = 5632
NPRE = 15
NMAIN = 17
NSEQ = 16
EPS = 1e-6
NEG = -240000.0
NSLOT = 5
SLOT_E = 4096


class Buf:
    __slots__ = ("name", "w", "r")

    def __init__(self, name):
        self.name = name
        self.w = None
        self.r = {}


class KB:
    ENG = ("pe", "act", "dve", "pool", "sp")

    def __init__(self, nc):
        self.nc = nc
        self.E = dict(pe=nc.tensor, act=nc.scalar, dve=nc.vector, pool=nc.gpsimd, sp=nc.sync)
        self.sems = {}
        self.cnt = {}
        self.seen = {e: {} for e in self.ENG}
        for e in self.ENG:
            self.new_sem(e)
        self.n_inst = {e: 0 for e in self.ENG}

    def new_sem(self, name):
        self.sems[name] = self.nc.alloc_semaphore(name="s_" + name)
        self.cnt[name] = 0
        return name

    def sb(self, name, shape, dt):
        return self.nc.alloc_sbuf_tensor(name, list(shape), dt)

    def _wait(self, eng, ev, kind):
        if ev is None:
            return
        s, v = ev
        if s == eng and kind != "raw":
            return
        if self.seen[eng].get(s, 0) >= v:
            return
        self.E[eng].wait_ge(self.sems[s], v)
        self.seen[eng][s] = v

    def _deps(self, eng, reads, writes, force=False):
        for b in reads:
            self._wait(eng, b.w, "raw")
        for b in writes:
            self._wait(eng, b.w, "raw" if force else "waw")
            for s, v in b.r.items():
                self._wait(eng, (s, v), "raw" if force else "war")

    def _record(self, ev, reads, writes):
        s, v = ev
        for b in reads:
            if b.r.get(s, 0) < v:
                b.r[s] = v
        for b in writes:
            b.w = ev
            b.r = {}

    @staticmethod
    def _flat(bs):
        out = []
        for b in bs:
            if isinstance(b, (list, tuple)):
                out.extend(KB._flat(b))
            else:
                out.append(b)
        return out

    def op(self, eng, fn, reads=(), writes=(), signal=True):
        reads, writes = self._flat(reads), self._flat(writes)
        self._deps(eng, reads, writes)
        ins = fn(self.E[eng])
        self.n_inst[eng] += 1
        ev = (eng, self.cnt[eng] + 1)
        if signal:
            ins.then_inc(self.sems[eng], 1)
            self.cnt[eng] += 1
        self._record(ev, reads, writes)
        return ins

    def dma(self, out, in_, reads=(), writes=(), sem=None, q="sp", **kw):
        reads, writes = self._flat(reads), self._flat(writes)
        self._deps(q, reads, writes, force=True)
        ins = self.E[q].dma_start(out=out, in_=in_, **kw)
        ins.then_inc(self.sems[sem], 16)
        self.cnt[sem] += 16
        self.n_inst[q] += 1
        self._record((sem, self.cnt[sem]), reads, writes)
        return ins


def weight_blocks():
    blocks = []
    for j in range(5):
        w = min(512, 2320 - 512 * j)
        blocks.append(("in0_%d" % j, 8, w, [("w_in_even", 0, 512 * j, w, 0)]))
    for j in range(2):
        blocks.append(("out0_%d" % j, 8, 512, [("w_out_even", 0, 512 * j, 512, 0)]))
    for l in range(2):
        for j in range(11):
            blocks.append(("up%d_%d" % (l, j), 8, 512, [("ffn_up%d" % l, 0, 256 * j, 256, 0),
                                                        ("ffn_up%d" % l, 0, DFF + 256 * j, 256, 256)]))
        for m in range(8):
            blocks.append(("dn%d_%d" % (l, m), 22, 128, [("ffn_down%d" % l, 0, 128 * m, 128, 0)]))
    chunks = []
    for c in range(8):
        chunks += [1024 + 128 * c, 2048 + 128 * c, 128 * c]
    for j in range(6):
        blocks.append(("in1_%d" % j, 8, 512, [("w_in_odd", 0, chunks[4 * j + q], 128, 128 * q) for q in range(4)]))
    for j in range(2):
        blocks.append(("out1_%d" % j, 8, 512, [("w_out_odd", 0, 512 * j, 512, 0)]))
    return blocks


def unit_block_order(mode):
    if mode == "pre":
        return ["in0_%d" % j for j in range(1, 5)]
    o = ["in0_%d" % j for j in range(5)] + ["out0_%d" % j for j in range(2)]
    o += ["up0_%d" % j for j in range(11)] + ["dn0_%d" % m for m in range(8)]
    o += ["in1_%d" % j for j in range(6)] + ["out1_%d" % j for j in range(2)]
    o += ["up1_%d" % j for j in range(11)] + ["dn1_%d" % m for m in range(8)]
    return o


def pc_norm(q, l, kc):
    return (q * 2 + l) * 8 + kc
PC_FW = 64
PC_FB = 328
PC_CW = 416
PC_GN = 440


def build_program(with_sample=True):
    nc = bass.Bass("TRN2", target_bir_lowering=False)
    k = KB(nc)
    Dm = {}

    def din(name, shape, dt=F32):
        Dm[name] = nc.dram_tensor(name, list(shape), dt, kind="ExternalInput").ap()
        return Dm[name]

    def dout(name, shape):
        Dm[name] = nc.dram_tensor(name, list(shape), F32, kind="ExternalOutput").ap()
        return Dm[name]

    xin = din("xin", [(NPRE + NMAIN) * 128, D])
    flagv = din("flagv", [128, 2])
    pvec = din("pvec", [512, 128])
    c_ident = din("c_ident", [128, 128])
    c_ones = din("c_ones", [128, 128], BF16)
    c_ucum = din("c_ucum", [128, 128])
    c_urev = din("c_urev", [128, 128])
    c_maska = din("c_maska", [128, 128])
    c_bias = din("c_bias", [128, 4, 512])
    din("w_in_even", [D, 2320]); din("w_out_even", [D, D]); din("w_in_odd", [D, 3 * D]); din("w_out_odd", [D, D])
    for l in range(2):
        din("ffn_up%d" % l, [D, F2]); din("ffn_down%d" % l, [DFF, D])
    w_gate_up = din("w_gate_up", [16, 256]); b_gate = din("b_gate", [1, 256]); attn_sinks = din("attn_sinks", [1, 8])
    xs = din("xs", [64, D]); sck = din("sck", [16, 128, 128]); scv = din("scv", [16, 128, 128])
    sgl = din("sgl", [16, 4, 64, 128]); scs = din("scs", [32, D]); sfs = din("sfs", [2, 32, F2])
    c_ucum_s = din("c_ucum_s", [64, 64]); c_urev_s = din("c_urev_s", [64, 64]); c_maska_s = din("c_maska_s", [64, 64])
    c_rowmask = din("c_rowmask", [64, 16]); c_bias_s = din("c_bias_s", [128, 2, 512])
    ys_out = dout("ys_out", [64, D]); ks_out = dout("ks_out", [16, 128, 128]); vs_out = dout("vs_out", [16, 128, 128])
    gs_out = dout("gs_out", [16, 4, 64, 128]); cs_out = dout("cs_out", [32, D]); fs_out = dout("fs_out", [2, 32, F2])
    y_out = dout("y_out", [16 * 128, D])
    k_out = dout("k_out", [128, 128]); v_out = dout("v_out", [128, 128])
    g_out = dout("g_out", [4, 64, 128])
    c_out = dout("c_out", [2, D])
    f_out = dout("f_out", [2, 2, F2])

    for s in ("d_cast", "d_const", "d_pv", "d_sink", "d_stg0", "d_stg1", "d_misc", "d_rows", "d_s1", "d_s2", "d_s3", "d_s4", "d_s5", "d_s6", "d_so", "d_mk", "d_mv", "d_mg", "d_res"):
        k.new_sem(s)
    for i in range(NSLOT):
        k.new_sem("d_w%d" % i)

    blocks = {b[0]: b for b in weight_blocks()}
    scr = {}
    for src in ("w_in_even", "w_out_even", "ffn_up0", "ffn_down0", "w_in_odd", "w_out_odd", "ffn_up1", "ffn_down1"):
        R, C = Dm[src].shape
        t = nc.dram_tensor("scr_" + src, [R, C], BF16, kind="Internal").ap()
        b = Buf("scr_" + src)
        scr[src] = (t, b)
        k.new_sem("d_c_" + src)

    def emit_casts(names):
        for src in names:
            t, b = scr[src]
            R, C = Dm[src].shape
            step = 512
            for r0 in range(0, R, step):
                r1 = min(R, r0 + step)
                k.dma(t[r0:r1, :], Dm[src][r0:r1, :], writes=[b], sem="d_c_" + src, q="pool")

    emit_casts(["w_in_even"])

    ring = [k.sb("ring%d" % i, [128, SLOT_E], BF16) for i in range(NSLOT)]
    ringB = [Buf("ring%d" % i) for i in range(NSLOT)]
    xT = k.sb("xT", [128, 8, 512], F32); BxT = [Buf("xT%d" % i) for i in range(8)]
    hT = k.sb("hT", [128, 8, 512], BF16); BhT = [Buf("hT%d" % i) for i in range(8)]
    mT = k.sb("mT", [128, 8, 512], F32); BmT = [Buf("mT%d" % i) for i in range(8)]
    sq = [k.sb("sq%d" % i, [128, 512], BF16) for i in range(2)]; Bsq = [Buf("sq%d" % i) for i in range(2)]
    rstd = k.sb("rstd", [128, 512], F32); Brstd = Buf("rstd")
    tmpn = [k.sb("tmpn%d" % i, [128, 512], F32) for i in range(2)]; Btmpn = [Buf("tmpn%d" % i) for i in range(2)]
    stg = [k.sb("stg%d" % i, [128, 1024], F32) for i in range(2)]; Bstg = [Buf("stg%d" % i) for i in range(2)]
    qaT = k.sb("qaT", [128, 4, 512], BF16); BqaT = Buf("qaT")
    kaT = k.sb("kaT", [128, 640], BF16); BkaT = Buf("kaT")
    kaT2 = k.sb("kaT2", [128, 640], BF16); BkaT2 = Buf("kaT2")
    vtok = k.sb("vtok", [128, 5, 128], BF16); Bvtok = Buf("vtok")
    qbT = k.sb("qbT", [128, 2, 512], F32); BqbT = Buf("qbT")
    kbT = k.sb("kbT", [128, 2, 512], F32); BkbT = Buf("kbT")
    srbT = k.sb("srbT", [128, 4, 512], F32); BsrbT = Buf("srbT")
    glrT = k.sb("glrT", [32, 512], BF16); BglrT = Buf("glrT")
    mixT = k.sb("mixT", [128, 8, 512], BF16); BmixT = Buf("mixT")
    gated = k.sb("gated", [128, 22, 512], BF16); Bgated = Buf("gated")
    tg = [k.sb("tg%d" % i, [128, 512], F32) for i in range(2)]; Btg = [Buf("tg%d" % i) for i in range(2)]
    tv = [k.sb("tv%d" % i, [128, 512], F32) for i in range(2)]; Btv = [Buf("tv%d" % i) for i in range(2)]
    l_sb = k.sb("l_sb", [128, 256], F32); Bl = Buf("l_sb")
    er_sb = k.sb("er_sb", [128, 256], F32); Ber = Buf("er")
    e1_sb = k.sb("e1_sb", [128, 256], F32); Be1 = Buf("e1")
    e2_sb = k.sb("e2_sb", [128, 256], F32); Be2 = Buf("e2")
    qtl = k.sb("qtl", [128, 2, 128], BF16); Bqtl = Buf("qtl")
    ktl = k.sb("ktl", [128, 2, 128], BF16); Bktl = Buf("ktl")
    khat = k.sb("khat", [128, 256], BF16); Bkhat = Buf("khat")
    vbtok = k.sb("vbtok", [128, 512], BF16); Bvbtok = Buf("vbtok")
    atm = [k.sb("atm%d" % i, [128, 128], BF16) for i in range(2)]; Batm = [Buf("atm%d" % i) for i in range(2)]
    S_f = k.sb("S_f", [128, 2, 128], F32); BS = Buf("S_f")
    S_b = k.sb("S_b", [128, 4, 2, 128], BF16); BSb = [[Buf("Sb%d%d" % (p, i)) for i in range(2)] for p in range(2)]
    osq = k.sb("osq", [128, 512], BF16); Bosq = Buf("osq")
    orstd = k.sb("orstd", [128, 512], F32); Borstd = Buf("orstd")
    otmp = k.sb("otmp", [128, 512], F32); Botmp = Buf("otmp")
    kf32 = k.sb("kf32", [128, 128], F32); Bkf32 = Buf("kf32")
    sc = [k.sb("sc%d" % i, [128, 512], F32) for i in range(2)]; Bsc = [Buf("sc%d" % i) for i in range(2)]
    ptT = k.sb("ptT", [128, 4, 512], BF16); Bpt = [Buf("pt%d" % i) for i in range(4)]
    rec, Brec = sc[0], Bsc[0]
    ident = k.sb("ident", [128, 128], F32); ones = k.sb("ones", [128, 128], BF16)
    ucum = k.sb("ucum", [128, 128], F32); urev = k.sb("urev", [128, 128], F32); maska = k.sb("maska", [128, 128], F32)
    biasT = k.sb("biasT", [128, 4, 512], F32)
    PT = k.sb("PT", [128, 512], F32)
    pv_in = k.sb("pv_in", [128, 4, 128], F32)
    flg = k.sb("flg", [128, 2], F32)
    epsD = k.sb("epsD", [128, 4], F32)
    wg = k.sb("wg", [32, 256], BF16)
    sinkE = k.sb("sinkE", [128, 4], F32)
    Bc = Buf("consts"); BPT = Buf("PT"); Bpv = Buf("pv_in"); Bwg = Buf("wg"); BsinkE = Buf("sinkE")
    fcar = k.sb("fcar", [128, 2, 44, 2], F32); Bfcar = Buf("fcar")
    ccar = k.sb("ccar", [128, 8, 2], F32); Bccar = Buf("ccar")
    ucum_s = k.sb("ucum_s", [64, 64], F32); urev_s = k.sb("urev_s", [64, 64], F32); maska_s = k.sb("maska_s", [64, 64], F32)
    rowmask = k.sb("rowmask", [64, 16], F32)
    fcs = k.sb("fcs", [128, 44, 16, 2], F32); Bfcs = Buf("fcs")
    ccs = k.sb("ccs", [128, 8, 16, 2], F32); Bccs = Buf("ccs")
    cu, Bcu = tg, Btg
    ucp, Bucp = tv, Btv
    zt, Bzt = tmpn, Btmpn
    osm = k.sb("osm", [128, 256], F32); Bosm = Buf("osm")
    rowst = mT[0:2, 0:3, :].rearrange("p a b -> p (a b)")

    PS = [nc.alloc_psum_tensor("ps%d" % i, [128, 512], F32) for i in range(8)]
    BPS = [Buf("ps%d" % i) for i in range(8)]
    pfree = list(range(8))

    def palloc():
        assert pfree, "out of PSUM banks"
        return pfree.pop(0)

    def pfree_(i):
        pfree.append(i)

    k.dma(ident[:, :], c_ident[:, :], writes=[Bc], sem="d_const")
    k.dma(ones[:, :], c_ones[:, :], writes=[Bc], sem="d_const")
    k.dma(ucum[:, :], c_ucum[:, :], writes=[Bc], sem="d_const")
    k.dma(urev[:, :], c_urev[:, :], writes=[Bc], sem="d_const")
    k.dma(maska[:, :], c_maska[:, :], writes=[Bc], sem="d_const")
    k.dma(biasT[:, :, :], c_bias[:, :, :], writes=[Bc], sem="d_const")
    k.dma(flg[:, :], flagv[:, :], writes=[Bc], sem="d_const")
    k.dma(ucum_s[:, :], c_ucum_s[:, :], writes=[Bc], sem="d_const")
    k.dma(urev_s[:, :], c_urev_s[:, :], writes=[Bc], sem="d_const")
    k.dma(maska_s[:, :], c_maska_s[:, :], writes=[Bc], sem="d_const")
    k.dma(rowmask[:, :], c_rowmask[:, :], writes=[Bc], sem="d_const")
    k.dma(pv_in[:, :, :], pvec.rearrange("(a p) c -> p a c", p=128), writes=[Bpv], sem="d_pv")
    k.op("pool", lambda e: e.memset(wg[:, :], 0.0), writes=[Bwg])
    k.dma(wg[0:16, :], w_gate_up[:, :], writes=[Bwg], sem="d_cast", q="pool")
    k.dma(wg[16:17, :], b_gate[:, :], writes=[Bwg], sem="d_cast", q="pool")
    sink1 = k.sb("sink1", [1, 8], F32); onesf = k.sb("onesf", [1, 128], F32); sink8 = k.sb("sink8", [128, 8], F32)
    Bs1 = Buf("sink1"); Bs8 = Buf("sink8")
    k.dma(sink1[:, :], attn_sinks[:, :], writes=[Bs1], sem="d_sink")
    k.op("pool", lambda e: e.memset(onesf[:, :], 1.0), writes=[Bs1])
    pb = palloc()
    k.op("pe", lambda e: e.matmul(PS[pb][:, 0:8], lhsT=onesf[0:1, :], rhs=sink1[0:1, :], start=True, stop=True),
         reads=[Bs1], writes=[BPS[pb]])
    k.op("act", lambda e: e.activation(out=sink8[:, :], in_=PS[pb][:, 0:8], func=AF.Exp), reads=[BPS[pb]], writes=[Bs8])
    pfree_(pb)
    k.op("dve", lambda e: e.tensor_copy(out=sinkE[0:64, :], in_=sink8[0:64, 0:8:2]), reads=[Bs8], writes=[BsinkE])
    k.op("dve", lambda e: e.tensor_copy(out=sinkE[64:128, :], in_=sink8[64:128, 1:8:2]), reads=[Bs8], writes=[BsinkE])
    pb = palloc()
    for a in range(4):
        k.op("pe", lambda e: e.transpose(out=PS[pb][:, a * 128:(a + 1) * 128], in_=pv_in[:, a, :], identity=ident[:, :]),
             reads=[Bpv, Bc], writes=[BPS[pb]])
    k.op("act", lambda e: e.copy(out=PT[:, :], in_=PS[pb][:, :]), reads=[BPS[pb]], writes=[BPT])
    pfree_(pb)
    k.op("act", lambda e: e.mul(out=PT[:, 0:64], in_=PT[:, 0:64], mul=32.0), reads=[BPT], writes=[BPT])
    k.op("act", lambda e: e.mul(out=PT[:, PC_GN:PC_GN + 4], in_=PT[:, PC_GN:PC_GN + 4], mul=float(np.sqrt(128.0))),
         reads=[BPT], writes=[BPT])
    k.op("pool", lambda e: e.memset(glrT[:, :], 1.0), writes=[BglrT])
    k.op("pool", lambda e: e.memset(epsD[:, 0:1], float(D * EPS)), writes=[Bc])
    k.op("pool", lambda e: e.memset(epsD[:, 1:2], float(128 * EPS)), writes=[Bc])
    k.op("pool", lambda e: e.memset(epsD[:, 2:3], 1.0), writes=[Bc])
    k.op("pool", lambda e: e.memset(epsD[:, 3:4], 0.0), writes=[Bc])
    k.op("pool", lambda e: e.memset(S_f[:, :, :], 0.0), writes=[BS])
    k.op("pool", lambda e: e.memset(S_b[:, :, :, :], 0.0), writes=[BSb[0][0], BSb[0][1], BSb[1][0], BSb[1][1]])
    k.op("pool", lambda e: e.memset(fcar[:, :, :, :], 0.0), writes=[Bfcar])
    k.op("pool", lambda e: e.memset(ccar[:, :, :], 0.0), writes=[Bccar])
    k.op("pool", lambda e: e.memset(kaT[:, :], 0.0), writes=[BkaT])
    k.op("pool", lambda e: e.memset(kaT2[:, :], 0.0), writes=[BkaT2])
    k.op("pool", lambda e: e.memset(vtok[:, :, :], 0.0), writes=[Bvtok])
    emit_casts(["w_out_even", "ffn_up0", "ffn_down0", "w_in_odd", "w_out_odd", "ffn_up1", "ffn_down1"])

    units = []
    for i in range(0, NPRE, 4):
        units.append(("pre", list(range(i, min(i + 4, NPRE)))))
    main_split = [[0], [1, 2, 3, 4], [5, 6, 7, 8], [9, 10, 11, 12], [13, 14, 15, 16]]
    for ts in main_split:
        units.append(("main", ts))
    plan = []
    for mode, ts in units:
        if mode != "pre":
            plan += unit_block_order(mode)
    if with_sample:
        plan += unit_block_order("main")
    wstate = dict(next_issue=0, next_use=0)

    def w_issue():
        u = wstate["next_issue"]
        if u >= len(plan):
            return
        name, KC, W, pieces = blocks[plan[u]]
        s = u % NSLOT
        dst = ring[s][:, 0:KC * W].rearrange("p (kc w) -> p kc w", kc=KC)
        for (src, r0, c0, w, off) in pieces:
            t, b = scr[src]
            k.dma(dst[:, :, off:off + w], t[r0:r0 + KC * 128, c0:c0 + w].rearrange("(kc p) w -> p kc w", p=128),
                  reads=[b], writes=[ringB[s]], sem="d_w%d" % s)
        wstate["next_issue"] += 1

    def w_get(name):
        u = wstate["next_use"]
        assert plan[u] == name, (plan[u], name)
        while wstate["next_issue"] <= u:
            w_issue()
        s = u % NSLOT
        _, KC, W, _ = blocks[name]
        return ring[s][:, 0:KC * W].rearrange("p (kc w) -> p kc w", kc=KC), ringB[s]

    gflat = gated[:, :, :].rearrange("p a b -> p (a b)")
    RES = {"in0_1": (gflat[:, 0:2048].rearrange("p (kc w) -> p kc w", kc=8), 512, 256),
           "in0_2": (gflat[:, 2048:6144].rearrange("p (kc w) -> p kc w", kc=8), 1024, 512),
           "in0_3": (gflat[:, 6144:8192].rearrange("p (kc w) -> p kc w", kc=8), 1536, 256),
           "in0_4": (gflat[:, 8192:10368].rearrange("p (kc w) -> p kc w", kc=8), 2048, 272)}

    def load_resident():
        t, b = scr["w_in_even"]
        for name, (view, c0, w) in RES.items():
            k.dma(view, t[0:1024, c0:c0 + w].rearrange("(kc p) w -> p kc w", p=128), reads=[b], writes=[Bgated], sem="d_res")

    def w_get_ahead(name, ahead):
        wstate["next_use"] += ahead
        r = w_get(name)
        wstate["next_use"] -= ahead
        return r

    def w_done():
        wstate["next_use"] += 1
        while wstate["next_issue"] < min(len(plan), wstate["next_use"] + NSLOT):
            w_issue()

    for _ in range(NSLOT):
        w_issue()

    def mm(out, lhsT, rhs, start, stop, reads, writes, signal=None, skip=False):
        if signal is None:
            signal = stop
        if skip:
            k.op("pe", lambda e: e.matmul(out, lhsT=lhsT, rhs=rhs, start=start, stop=stop, skip_group_check=True),
                 reads=reads, writes=writes, signal=signal)
        else:
            k.op("pe", lambda e: e.matmul(out, lhsT=lhsT, rhs=rhs, start=start, stop=stop), reads=reads, writes=writes,
                 signal=signal)

    cnt = dict(sq=0, tmpn=0, stg=0, sc=0, atm=0, tg=0, cu=0, bc=0)
    bcT = k.sb("bcT", [128, 4, 4], F32); Bbc = [Buf("bc%d" % i) for i in range(4)]

    def rot(name, n):
        i = cnt[name] % n
        cnt[name] += 1
        return i

    def rms_stats(srcs, N, src_bufs, eps_scaled):
        pb = palloc()
        for c in range(8):
            i = rot("sq", 2)
            k.op("act", lambda e: e.activation(out=sq[i][:, 0:N], in_=srcs[c], func=AF.Square), reads=[src_bufs[c]],
                 writes=[Bsq[i]])
            mm(PS[pb][:, 0:N], ones[:, :], sq[i][:, 0:N], c == 0, c == 7, [Bsq[i], Bc], [BPS[pb]], signal=True)
        k.op("act", lambda e: e.activation(out=rstd[:, 0:N], in_=PS[pb][:, 0:N], func=AF.Ln, bias=epsD[:, 0:1]),
             reads=[BPS[pb], Bc], writes=[Brstd])
        k.op("act", lambda e: e.activation(out=rstd[:, 0:N], in_=rstd[:, 0:N], func=AF.Exp, scale=-0.5), reads=[Brstd], writes=[Brstd])
        pfree_(pb)

    def pre_norm(q, l, N):
        rms_stats([xT[:, c, 0:N] for c in range(8)], N, BxT, D * EPS)
        for c in range(8):
            col = pc_norm(q, l, c)
            k.op("dve", lambda e: e.scalar_tensor_tensor(out=hT[:, c, 0:N], in0=xT[:, c, 0:N], scalar=PT[:, col:col + 1],
                                                         in1=rstd[:, 0:N], op0=ALU.mult, op1=ALU.mult),
                 reads=[BxT[c], BPT, Brstd], writes=[BhT[c]])

    def post_norm_add(q, l, N):
        rms_stats([mT[:, c, 0:N] for c in range(8)], N, BmT, D * EPS)
        ck(142)
        for c in range(8):
            col = pc_norm(q, l, c)
            i = rot("tmpn", 2)
            if c == 1:
                ck(144)
            k.op("dve", lambda e: e.scalar_tensor_tensor(out=tmpn[i][:, 0:N], in0=mT[:, c, 0:N], scalar=PT[:, col:col + 1],
                                                         in1=rstd[:, 0:N], op0=ALU.mult, op1=ALU.mult),
                 reads=[BmT[c], BPT, Brstd], writes=[Btmpn[i]])
            if c == 0:
                ck(143)
            k.op("pool", lambda e: e.tensor_tensor(out=xT[:, c, 0:N], in0=xT[:, c, 0:N], in1=tmpn[i][:, 0:N], op=ALU.add),
                 reads=[Btmpn[i], BxT[c]], writes=[BxT[c]])

    def proj_out(wnames, rhs_tile, rhs_buf, KC, N, per_block):
        m = 0
        for wn in wnames:
            wv, wb = w_get(wn)
            for j in range(per_block):
                pb = palloc()
                for kc in range(KC):
                    mm(PS[pb][:, 0:N], wv[:, kc, j * 128:(j + 1) * 128], rhs_tile[:, kc, 0:N], kc == 0, kc == KC - 1,
                       [wb, rhs_buf], [BPS[pb]])
                k.op("act", lambda e: e.copy(out=mT[:, m, 0:N], in_=PS[pb][:, 0:N]), reads=[BPS[pb]], writes=[BmT[m]])
                pfree_(pb)
                m += 1
            w_done()

    def conv_taps(t_ap, tB, src, srcB, car, carB, w0, w1, w2, bias, N, seg=False, after_act=None):
        if seg:
            V = lambda ap, a, b: ap[:, 0:64].rearrange("p (s i) -> p s i", i=4)[:, :, a:b]
            C = lambda a, b: car[:, :, a:b]
            L = 4
        else:
            V = lambda ap, a, b: ap[:, a:b]
            C = lambda a, b: car[:, a:b]
            L = N
        if not seg:
            ib = rot("bc", 4)
            k.op("act", lambda e: e.activation(out=bcT[:, ib, 0:2], in_=car[:, 0:2], func=AF.Identity, scale=w0, bias=epsD[:, 3:4]),
                 reads=[carB, BPT, Bc], writes=[Bbc[ib]])
            k.op("act", lambda e: e.activation(out=bcT[:, ib, 0:1], in_=car[:, 1:2], func=AF.Identity, scale=w1, bias=bcT[:, ib, 0:1]),
                 reads=[carB, BPT, Bbc[ib]], writes=[Bbc[ib]])
        k.op("act", lambda e: e.activation(out=t_ap[:, 0:N], in_=src[:, 0:N], func=AF.Identity, scale=w2,
                                           bias=(epsD[:, 3:4] if bias is None else bias)), reads=[srcB, BPT, Bc], writes=[tB])
        if not seg:
            k.op("act", lambda e: e.copy(out=car[:, 0:2], in_=src[:, N - 2:N]), reads=[srcB, Bbc[ib]], writes=[carB])
        if after_act is not None:
            after_act()
        k.op("dve", lambda e: e.scalar_tensor_tensor(out=V(t_ap, 1, L), in0=V(src, 0, L - 1), scalar=w1, in1=V(t_ap, 1, L),
                                                     op0=ALU.mult, op1=ALU.add), reads=[srcB, BPT, tB] + ([] if seg else [carB]), writes=[tB])
        k.op("dve", lambda e: e.scalar_tensor_tensor(out=V(t_ap, 2, L), in0=V(src, 0, L - 2), scalar=w0, in1=V(t_ap, 2, L),
                                                     op0=ALU.mult, op1=ALU.add), reads=[srcB, BPT, tB], writes=[tB])
        if not seg:
            k.op("dve", lambda e: e.tensor_tensor(out=t_ap[:, 0:2], in0=t_ap[:, 0:2], in1=bcT[:, ib, 0:2], op=ALU.add),
                 reads=[Bbc[ib], tB], writes=[tB])
            return
        k.op("dve", lambda e: e.scalar_tensor_tensor(out=V(t_ap, 0, 1), in0=C(1, 2), scalar=w1, in1=V(t_ap, 0, 1),
                                                     op0=ALU.mult, op1=ALU.add), reads=[carB, BPT, tB], writes=[tB])
        k.op("dve", lambda e: e.scalar_tensor_tensor(out=V(t_ap, 0, 2), in0=C(0, 2), scalar=w0, in1=V(t_ap, 0, 2),
                                                     op0=ALU.mult, op1=ALU.add), reads=[carB, BPT, tB], writes=[tB])
        k.op("dve", lambda e: e.tensor_copy(out=C(0, 2), in_=V(src, L - 2, L)), reads=[srcB, tB], writes=[carB])

    def load_x(tiles_abs, N):
        for ti, ta in enumerate(tiles_abs):
            i = rot("stg", 2)
            k.dma(stg[i][:, :], xin[ta * 128:(ta + 1) * 128, :], writes=[Bstg[i]], sem="d_stg%d" % i)
            for half in range(2):
                pb = palloc()
                for c4 in range(4):
                    c = half * 4 + c4
                    k.op("pe", lambda e: e.transpose(out=PS[pb][:, c4 * 128:(c4 + 1) * 128], in_=stg[i][:, c * 128:(c + 1) * 128],
                                                     identity=ident[:, :]), reads=[Bstg[i], Bc], writes=[BPS[pb]])
                k.op("act", lambda e: e.copy(out=xT[:, half * 4:(half + 1) * 4, ti * 128:(ti + 1) * 128],
                                             in_=PS[pb][:, :].rearrange("p (a b) -> p a b", a=4)),
                     reads=[BPS[pb]], writes=BxT[half * 4:(half + 1) * 4])
                pfree_(pb)

    def store_y(tiles_real, N):
        for ti, to in tiles_real:
            i = rot("stg", 2)
            for half in range(2):
                pb = palloc()
                for c4 in range(4):
                    c = half * 4 + c4
                    k.op("pe", lambda e: e.transpose(out=PS[pb][:, c4 * 128:(c4 + 1) * 128], in_=xT[:, c, ti * 128:(ti + 1) * 128],
                                                     identity=ident[:, :]), reads=[BxT[c], Bc], writes=[BPS[pb]])
                k.op("act", lambda e: e.copy(out=stg[i][:, half * 512:(half + 1) * 512], in_=PS[pb][:, :]),
                     reads=[BPS[pb]], writes=[Bstg[i]])
                pfree_(pb)
            k.dma(y_out[to * 128:(to + 1) * 128, :], stg[i][:, :], reads=[Bstg[i]], sem="d_stg%d" % i)

    def fm_chunk(wv, wb, c0, M, N, evac):
        pb = palloc()
        for kc in range(8):
            mm(PS[pb][0:M, 0:N], wv[:, kc, c0:c0 + M], hT[:, kc, 0:N], kc == 0, kc == 7, [wb, BhT[kc]], [BPS[pb]])
        evac(pb)
        pfree_(pb)

    def even_mixer(mode, tiles, N, gtiles):
        nt = len(tiles)
        pre = mode == "pre"
        if not pre:
            wv, wb = w_get("in0_0")
            for c in range(4):
                fm_chunk(wv, wb, c * 128, 128, N, lambda pb: k.op(
                    "act", lambda e: e.copy(out=qaT[:, c, 0:N], in_=PS[pb][:, 0:N]), reads=[BPS[pb]], writes=[BqaT]))
            w_done()
        wv, wb = (RES["in0_1"][0], Bgated) if pre else w_get("in0_1")
        def ka_evac(pb):
            k.op("act", lambda e: e.copy(out=kaT[:, 128:128 + N], in_=PS[pb][:, 0:N]), reads=[BPS[pb]], writes=[BkaT])
            if (not pre) and gtiles[-1] == NMAIN - 1:
                k.op("act", lambda e: e.copy(out=kf32[:, :], in_=PS[pb][:, N - 128:N]), reads=[BPS[pb]], writes=[Bkf32])
        fm_chunk(wv, wb, 0, 128, N, ka_evac)
        pb = palloc()
        for kc in range(8):
            mm(PS[pb][64:128, 0:N], wv[:, kc, 0:64], hT[:, kc, 0:N], kc == 0, kc == 7, [wb, BhT[kc]], [BPS[pb]], signal=False)
        for kc in range(8):
            mm(PS[pb][0:64, 0:N], wv[:, kc, 64:128], hT[:, kc, 0:N], kc == 0, kc == 7, [wb, BhT[kc]], [BPS[pb]])
        k.op("act", lambda e: e.copy(out=kaT2[:, 128:128 + N], in_=PS[pb][:, 0:N]), reads=[BPS[pb]], writes=[BkaT2])
        pfree_(pb)
        for ti in range(nt):
            pb = palloc()
            for kc in range(8):
                mm(PS[pb][:, 0:128], hT[:, kc, ti * 128:(ti + 1) * 128], wv[:, kc, 128:256], kc == 0, kc == 7, [wb, BhT[kc]], [BPS[pb]])
            k.op("act", lambda e: e.copy(out=vtok[:, 1 + ti, :], in_=PS[pb][:, 0:128]), reads=[BPS[pb]], writes=[Bvtok])
            if (not pre) and gtiles[ti] == NMAIN - 1:
                k.op("act", lambda e: e.copy(out=osm[:, 128:256], in_=PS[pb][:, 0:128]), reads=[BPS[pb]], writes=[Bosm])
            pfree_(pb)
        if not pre:
            for p in range(2):
                fm_chunk(wv, wb, 256 + p * 128, 128, N, lambda pb: k.op(
                    "act", lambda e: e.copy(out=qbT[:, p, 0:N], in_=PS[pb][:, 0:N]), reads=[BPS[pb]], writes=[BqbT]))
        if not pre:
            w_done()
        ck(3)
        if pre:
            wv2, wb2 = RES["in0_2"][0], Bgated
            wv3, wb3 = RES["in0_3"][0], Bgated
            wv4, wb4 = RES["in0_4"][0], Bgated
        else:
            wv2, wb2 = w_get("in0_2")
            for p in range(2):
                fm_chunk(wv2, wb2, p * 128, 128, N, lambda pb: k.op(
                    "act", lambda e: e.copy(out=kbT[:, p, 0:N], in_=PS[pb][:, 0:N]), reads=[BPS[pb]], writes=[BkbT]))
            wv3, wb3 = w_get_ahead("in0_3", 1)
            wv4, wb4 = w_get_ahead("in0_4", 2)
        if not pre:
            for h in range(4):
                src_v, src_b, c0 = (wv3, wb3, 256 + h * 128) if h < 2 else (wv4, wb4, (h - 2) * 128)
                fm_chunk(src_v, src_b, c0, 128, N, lambda pb: k.op(
                    "act", lambda e: e.activation(out=srbT[:, h, 0:N], in_=PS[pb][:, 0:N], func=AF.Silu),
                    reads=[BPS[pb]], writes=[BsrbT]))
        fm_chunk(wv4, wb4, 256, 16, N, lambda pb: k.op(
            "act", lambda e: e.copy(out=glrT[0:16, 0:N], in_=PS[pb][0:16, 0:N]), reads=[BPS[pb]], writes=[BglrT]))

        ck(4)
        def gla_body(ti):
            c0 = ti * 128
            gt = gtiles[ti]
            if ti == 1:
                ck(7)
            pg = palloc()
            mm(PS[pg][:, 0:256], glrT[0:32, c0:c0 + 128], wg[0:32, :], True, True, [BglrT, Bwg], [BPS[pg]])
            ck(51)
            k.op("act", lambda e: e.activation(out=l_sb[:, :], in_=PS[pg][:, 0:256], func=AF.Exp, scale=-1.0),
                 reads=[BPS[pg]], writes=[Bl])
            pfree_(pg)
            ck(52)
            k.op("act", lambda e: e.activation(out=l_sb[:, :], in_=l_sb[:, :], func=AF.Ln, bias=epsD[:, 2:3]), reads=[Bl, Bc], writes=[Bl])
            ck(5)
            yield
            pr = palloc()
            mm(PS[pr][:, 0:256], urev[:, :], l_sb[:, :], True, True, [Bc, Bl], [BPS[pr]])
            k.op("act", lambda e: e.activation(out=er_sb[:, :], in_=PS[pr][:, 0:256], func=AF.Exp), reads=[BPS[pr]], writes=[Ber])
            pfree_(pr)
            yield
            pk = palloc()
            for kc in range(8):
                mm(PS[pk][:, 0:256], hT[:, kc, c0:c0 + 128], wv2[:, kc, 0:256], kc == 0, kc == 7, [wb2, BhT[kc]], [BPS[pk]])
            k.op("dve", lambda e: e.tensor_tensor(out=khat[:, :], in0=PS[pk][:, 0:256], in1=er_sb[:, :], op=ALU.mult),
                 reads=[BPS[pk], Ber], writes=[Bkhat])
            pfree_(pk)
            yield
            pv = palloc()
            for kc in range(8):
                mm(PS[pv][:, 0:256], hT[:, kc, c0:c0 + 128], wv2[:, kc, 256:512], kc == 0, kc == 7, [wb2, BhT[kc]], [BPS[pv]], signal=False)
            for kc in range(8):
                mm(PS[pv][:, 256:512], hT[:, kc, c0:c0 + 128], wv3[:, kc, 0:256], kc == 0, kc == 7, [wb3, BhT[kc]], [BPS[pv]])
            k.op("act", lambda e: e.copy(out=vbtok[:, :], in_=PS[pv][:, :]), reads=[BPS[pv]], writes=[Bvbtok])
            pfree_(pv)
            ck(6)
            yield
            pbk = palloc()
            for p in range(2):
                mm(PS[pbk][:, p * 128:(p + 1) * 128], l_sb[:, p * 128:(p + 1) * 128], ucum[:, :], True, True, [Bl, Bc], [BPS[pbk]])
            k.op("act", lambda e: e.activation(out=e1_sb[:, :], in_=PS[pbk][:, 0:256], func=AF.Exp), reads=[BPS[pbk]], writes=[Be1])
            if not pre:
                k.op("act", lambda e: e.activation(out=e2_sb[:, :], in_=PS[pbk][:, 0:256], func=AF.Exp, scale=-1.0),
                     reads=[BPS[pbk]], writes=[Be2])
            pfree_(pbk)
            if not pre:
                k.op("dve", lambda e: e.scalar_tensor_tensor(
                    out=qtl[:, :, :], in0=qbT[:, :, c0:c0 + 128], scalar=0.125,
                    in1=e1_sb[:, :].rearrange("p (a b) -> p a b", a=2), op0=ALU.mult, op1=ALU.mult),
                    reads=[BqbT, Be1], writes=[Bqtl])
                k.op("dve", lambda e: e.tensor_tensor(
                    out=ktl[:, :, :], in0=kbT[:, :, c0:c0 + 128], in1=e2_sb[:, :].rearrange("p (a b) -> p a b", a=2),
                    op=ALU.mult), reads=[BkbT, Be2], writes=[Bktl])
                po = palloc()
            yield
            pdc = [palloc(), palloc()]
            for h in range(4):
                p, hp = h // 2, (h % 2) * 64
                if not pre:
                    pa = palloc()
                    mm(PS[pa][:, 0:128], ktl[hp:hp + 64, p, :], qtl[hp:hp + 64, p, :], True, True, [Bktl, Bqtl], [BPS[pa]])
                    ia = rot("atm", 2)
                    k.op("dve", lambda e: e.tensor_tensor(out=atm[ia][:, :], in0=PS[pa][:, 0:128], in1=maska[:, :], op=ALU.mult),
                         reads=[BPS[pa], Bc], writes=[Batm[ia]])
                    pfree_(pa)
                    mm(PS[po][:, h * 128:(h + 1) * 128], vbtok[:, h * 128:(h + 1) * 128], atm[ia][:, :], h == 0, False,
                       [Bvbtok, Batm[ia]], [BPS[po]], signal=False, skip=True)
                for c in range(2):
                    mm(PS[pdc[c]][hp:hp + 64, p * 128:(p + 1) * 128], khat[c * 64:(c + 1) * 64, p * 128 + hp:p * 128 + hp + 64],
                       vbtok[c * 64:(c + 1) * 64, h * 128:(h + 1) * 128], True, True, [Bkhat, Bvbtok], [BPS[pdc[c]]])
            yield
            for p in range(2):
                for c in range(2):
                    if not pre:
                        for half in range(2):
                            h = p * 2 + half
                            hp = half * 64
                            mm(PS[po][:, h * 128 + c * 64:h * 128 + (c + 1) * 64], S_b[:, h, c, :],
                               qtl[:, p, c * 64:(c + 1) * 64], False, c == 1, [BSb[p][c], Bqtl], [BPS[po]], skip=True)
                    dcol = p * 128 + c * 64 + 63
                    k.op("dve", lambda e: e.scalar_tensor_tensor(
                        out=S_f[:, p, :], in0=S_f[:, p, :], scalar=e1_sb[:, dcol:dcol + 1],
                        in1=PS[pdc[c]][:, p * 128:(p + 1) * 128], op0=ALU.mult, op1=ALU.add),
                        reads=[BS, Be1, BPS[pdc[c]]], writes=[BS])
                    nslot = (c + 1) % 2
                    for half in range(2):
                        hp = half * 64
                        k.op("act", lambda e: e.copy(out=S_b[hp:hp + 64, p * 2 + half, nslot, :], in_=S_f[hp:hp + 64, p, :]),
                             reads=[BS], writes=[BSb[p][nslot]])
            pfree_(pdc[0]); pfree_(pdc[1])
            if pre:
                return
            yield
            ck(10)
            k.op("act", lambda e: e.activation(out=osq[:, :], in_=PS[po][:, :], func=AF.Square), reads=[BPS[po]], writes=[Bosq])
            yield
            ps2 = palloc()
            mm(PS[ps2][:, :], ones[:, :], osq[:, :], True, True, [Bc, Bosq], [BPS[ps2]])
            k.op("act", lambda e: e.activation(out=orstd[:, :], in_=PS[ps2][:, :], func=AF.Ln, bias=epsD[:, 1:2]),
                 reads=[BPS[ps2], Bc], writes=[Borstd])
            k.op("act", lambda e: e.activation(out=orstd[:, :], in_=orstd[:, :], func=AF.Exp, scale=-0.5), reads=[Borstd], writes=[Borstd])
            pfree_(ps2)
            yield
            for h in range(4):
                col = PC_GN + h
                k.op("dve", lambda e: e.scalar_tensor_tensor(
                    out=otmp[:, h * 128:(h + 1) * 128], in0=PS[po][:, h * 128:(h + 1) * 128], scalar=PT[:, col:col + 1],
                    in1=orstd[:, h * 128:(h + 1) * 128], op0=ALU.mult, op1=ALU.mult), reads=[BPS[po], BPT, Borstd], writes=[Botmp])
            pfree_(po)
            k.op("pool", lambda e: e.tensor_tensor(out=mixT[:, 4:8, c0:c0 + 128], in0=otmp[:, :].rearrange("p (a b) -> p a b", a=4),
                                                   in1=srbT[:, :, c0:c0 + 128], op=ALU.mult), reads=[Botmp, BsrbT], writes=[BmixT])
        def swa_body(ti):
            c0 = ti * 128
            gt = gtiles[ti]
            ck(11)
            for kv in range(2):
                pscs = [palloc(), palloc()]
                for kb_ in range(2):
                    kc0 = c0 + kb_ * 128
                    for par in range(2):
                        ksrc, kB = (kaT, BkaT) if kv == par else (kaT2, BkaT2)
                        mm(PS[pscs[par]][:, kb_ * 256:(kb_ + 1) * 256], ksrc[par * 64:(par + 1) * 64, kc0:kc0 + 128],
                           qaT[par * 64:(par + 1) * 64, 2 * kv:2 * kv + 2, c0:c0 + 128], True, True, [kB, BqaT], [BPS[pscs[par]]])
                yield
                for par in range(2):
                    i = rot("sc", 2)
                    bi = kv * 2 + par
                    psc = pscs[par]
                    if gt == 1:
                        k.op("dve", lambda e: e.scalar_tensor_tensor(out=sc[i][:, 0:256], in0=PS[psc][:, 0:256], scalar=flg[:, 1:2],
                                                                     in1=biasT[:, bi, 0:256], op0=ALU.add, op1=ALU.add),
                             reads=[BPS[psc], Bc], writes=[Bsc[i]])
                        k.op("dve", lambda e: e.tensor_tensor(out=sc[i][:, 256:512], in0=PS[psc][:, 256:512], in1=biasT[:, bi, 256:512],
                                                              op=ALU.add), reads=[BPS[psc], Bc], writes=[Bsc[i]])
                    else:
                        k.op("dve", lambda e: e.tensor_tensor(out=sc[i][:, :], in0=PS[psc][:, :], in1=biasT[:, bi, :], op=ALU.add),
                             reads=[BPS[psc], Bc], writes=[Bsc[i]])
                    pfree_(psc)
                    k.op("act", lambda e: e.activation(out=ptT[:, bi, :], in_=sc[i][:, :], func=AF.Exp, scale=0.125),
                         reads=[Bsc[i]], writes=[Bpt[bi]])
            ck(12)
            yield
            poa = palloc()
            pdn = palloc()
            for kv in range(2):
                for par in range(2):
                    for kb_ in range(2):
                        bi = kv * 2 + par
                        rhs = ptT[:, bi, kb_ * 256:(kb_ + 1) * 256]
                        outo = PS[poa][par * 64:(par + 1) * 64, 2 * kv * 128:(2 * kv + 2) * 128]
                        outd = PS[pdn][par * 64:(par + 1) * 64, 2 * kv * 128:(2 * kv + 2) * 128]
                        mm(outo, vtok[:, ti + kb_, kv * 64:(kv + 1) * 64], rhs, kb_ == 0, kb_ == 1, [Bvtok, Bpt[bi]], [BPS[poa]], signal=False)
                        mm(outd, ones[:, 0:64], rhs, kb_ == 0, kb_ == 1, [Bc, Bpt[bi]], [BPS[pdn]], signal=(kb_ == 1))
            k.op("dve", lambda e: e.tensor_tensor(out=rec[:, :].rearrange("p (c q) -> p c q", c=4),
                                                  in0=PS[pdn][:, :].rearrange("p (c q) -> p c q", c=4),
                                                  in1=sinkE[:, :].unsqueeze(2).to_broadcast([128, 4, 128]), op=ALU.add),
                 reads=[BPS[pdn], BsinkE], writes=[Brec])
            pfree_(pdn)
            yield
            k.op("dve", lambda e: e.reciprocal(out=rec[:, :], in_=rec[:, :]), reads=[Brec], writes=[Brec])
            k.op("dve", lambda e: e.tensor_tensor(out=mixT[:, 0:4, c0:c0 + 128], in0=PS[poa][:, :].rearrange("p (c q) -> p c q", c=4),
                                                  in1=rec[:, :].rearrange("p (c q) -> p c q", c=4), op=ALU.mult),
                 reads=[BPS[poa], Brec], writes=[BmixT])
            pfree_(poa)
        for ti in range(nt):
            alive = [gla_body(ti)] if pre else [gla_body(ti), swa_body(ti)]
            while alive:
                for g_ in list(alive):
                    try:
                        next(g_)
                    except StopIteration:
                        alive.remove(g_)
        if not pre:
            w_done(); w_done(); w_done()
        if (not pre) and gtiles[-1] == NMAIN - 1:
            pb = palloc()
            k.op("pe", lambda e: e.transpose(out=PS[pb][:, 0:128], in_=kf32[:, :], identity=ident[:, :]),
                 reads=[Bkf32, Bc], writes=[BPS[pb]])
            k.op("act", lambda e: e.copy(out=osm[:, 0:128], in_=PS[pb][:, 0:128]), reads=[BPS[pb]], writes=[Bosm])
            pfree_(pb)
            k.dma(k_out[:, :], osm[:, 0:128], reads=[Bosm], sem="d_mk")
            k.dma(v_out[:, :], osm[:, 128:256], reads=[Bosm], sem="d_mv")
            k.dma(g_out.rearrange("(p h) d v -> (h d) p v", h=2), S_f[:, :, :], reads=[BS], sem="d_mg")
        k.op("dve", lambda e: e.tensor_copy(out=kaT[:, 0:128], in_=kaT[:, nt * 128:(nt + 1) * 128]), reads=[BkaT], writes=[BkaT])
        k.op("dve", lambda e: e.tensor_copy(out=kaT2[:, 0:128], in_=kaT2[:, nt * 128:(nt + 1) * 128]), reads=[BkaT2], writes=[BkaT2])
        k.op("dve", lambda e: e.tensor_copy(out=vtok[:, 0, :], in_=vtok[:, nt, :]), reads=[Bvtok], writes=[Bvtok])

    def odd_mixer(N, last, seg=False):
        wvs = {}
        for c in range(8):
            chunk_aps = []
            for q in range(3):
                idx = c * 3 + q
                bj, bq = idx // 4, idx % 4
                if bj not in wvs:
                    wvs[bj] = w_get_ahead("in1_%d" % bj, len(wvs))
                chunk_aps.append((wvs[bj][0], wvs[bj][1], bq * 128, bj))
            i = rot("cu", 2)
            wv, wb, c0, bj = chunk_aps[0]
            pcg = palloc()
            for kc in range(8):
                mm(PS[pcg][:, 0:N], wv[:, kc, c0:c0 + 128], hT[:, kc, 0:N], kc == 0, kc == 7, [wb, BhT[kc]], [BPS[pcg]])
            wv, wb, c0, bj = chunk_aps[1]
            pu = palloc()
            for kc in range(8):
                mm(PS[pu][:, 0:N], wv[:, kc, c0:c0 + 128], hT[:, kc, 0:N], kc == 0, kc == 7, [wb, BhT[kc]], [BPS[pu]])
            k.op("act", lambda e: e.copy(out=ucp[i][:, 0:N], in_=PS[pu][:, 0:N]), reads=[BPS[pu]], writes=[Bucp[i]])
            pfree_(pu)
            k.op("dve", lambda e: e.tensor_tensor(out=cu[i][:, 0:N], in0=PS[pcg][:, 0:N], in1=ucp[i][:, 0:N], op=ALU.mult),
                 reads=[BPS[pcg], Bucp[i]], writes=[Bcu[i]])
            pfree_(pcg)
            w0 = PT[:, PC_CW + 0 * 8 + c:PC_CW + 0 * 8 + c + 1]
            w1 = PT[:, PC_CW + 1 * 8 + c:PC_CW + 1 * 8 + c + 1]
            w2 = PT[:, PC_CW + 2 * 8 + c:PC_CW + 2 * 8 + c + 1]
            conv_taps(zt[i], Bzt[i], cu[i], Bcu[i], (ccs[:, c, :, :] if seg else ccar[:, c, :]), (Bccs if seg else Bccar), w0, w1, w2, None, N, seg)
            wv, wb, c0, bj = chunk_aps[2]
            pbg = palloc()
            for kc in range(8):
                mm(PS[pbg][:, 0:N], wv[:, kc, c0:c0 + 128], hT[:, kc, 0:N], kc == 0, kc == 7, [wb, BhT[kc]], [BPS[pbg]])
            k.op("dve", lambda e: e.tensor_tensor(out=mixT[:, c, 0:N], in0=PS[pbg][:, 0:N], in1=zt[i][:, 0:N], op=ALU.mult),
                 reads=[BPS[pbg], Bzt[i]], writes=[BmixT])
            pfree_(pbg)
            done_upto = (c * 3 + 3) // 4
            for bj in sorted(list(wvs.keys())):
                if bj < done_upto:
                    del wvs[bj]
                    w_done()
        assert not wvs

    def ffn(l, N, seg=False):
        pend = []

        def flush():
            while pend:
                i_, c_ = pend.pop(0)
                k.op("act", lambda e: e.activation(out=tg[i_][:, 0:N], in_=tg[i_][:, 0:N], func=AF.Gelu_apprx_tanh),
                     reads=[Btg[i_]], writes=[Btg[i_]])
                k.op("pool", lambda e: e.tensor_tensor(out=gated[:, c_, 0:N], in0=tg[i_][:, 0:N], in1=tv[i_][:, 0:N], op=ALU.mult),
                     reads=[Btg[i_], Btv[i_]], writes=[Bgated])

        for j in range(11):
            wv, wb = w_get("up%d_%d" % (l, j))
            for jj in range(2):
                c = 2 * j + jj
                i = rot("tg", 2)
                pgt = palloc()
                for kc in range(8):
                    mm(PS[pgt][:, 0:N], wv[:, kc, jj * 128:(jj + 1) * 128], hT[:, kc, 0:N], kc == 0, kc == 7, [wb, BhT[kc]], [BPS[pgt]])
                pvl = palloc()
                for kc in range(8):
                    mm(PS[pvl][:, 0:N], wv[:, kc, 256 + jj * 128:256 + (jj + 1) * 128], hT[:, kc, 0:N], kc == 0, kc == 7, [wb, BhT[kc]], [BPS[pvl]])
                first = True
                for (pp, tt, tB, cc) in ((pgt, tg[i], Btg[i], c), (pvl, tv[i], Btv[i], 22 + c)):
                    w0 = PT[:, PC_FW + (l * 3 + 0) * 44 + cc:PC_FW + (l * 3 + 0) * 44 + cc + 1]
                    w1 = PT[:, PC_FW + (l * 3 + 1) * 44 + cc:PC_FW + (l * 3 + 1) * 44 + cc + 1]
                    w2 = PT[:, PC_FW + (l * 3 + 2) * 44 + cc:PC_FW + (l * 3 + 2) * 44 + cc + 1]
                    bb = PT[:, PC_FB + l * 44 + cc:PC_FB + l * 44 + cc + 1]
                    conv_taps(tt, tB, PS[pp], BPS[pp], (fcs[:, cc, :, :] if seg else fcar[:, l, cc, :]), (Bfcs if seg else Bfcar), w0, w1, w2, bb, N, seg,
                              after_act=(flush if first else None))
                    first = False
                    pfree_(pp)
                pend.append((i, c))
            w_done()
        flush()
        proj_out(["dn%d_%d" % (l, m) for m in range(8)], gated, Bgated, 22, N, 1)

    def state_rows_out(src_cols, ncol, dst_rows_ap, stage_off):
        done = 0
        while done < ncol:
            n = min(4, ncol - done)
            pb = palloc()
            for q in range(n):
                ap_, bf_ = src_cols[done + q]
                k.op("pe", lambda e: e.transpose(out=PS[pb][0:2, q * 128:(q + 1) * 128], in_=ap_, identity=ident[:, :]),
                     reads=[bf_, Bc], writes=[BPS[pb]])
            k.op("act", lambda e: e.copy(out=rowst[:, stage_off + done * 128:stage_off + (done + n) * 128], in_=PS[pb][0:2, 0:n * 128]),
                 reads=[BPS[pb]], writes=[BmT])
            pfree_(pb)
            done += n


    def sample_unit():
        N = 64
        KcF = biasT[:, 0:2, :].rearrange("p a (s c) -> p (a s) c", c=128)
        biasS = biasT[:, 2:4, :]
        KcT = ptT[:, :, :].rearrange("p a (s c) -> p (a s) c", c=128)
        KcT2 = stg[1].bitcast(BF16)[:, :].rearrange("p (s c) -> p s c", c=128)
        Vc = gated[:, 16:20, :].rearrange("p a (s c) -> p (a s) c", c=128)
        S0bz = gated[:, 0:16, :].rearrange("p s (h v) -> p s h v", h=4)
        S0f = mT[:, :, :].rearrange("p a (s q v) -> p (a s) q v", s=2, q=2)
        khm = hT[0:64, :, :].rearrange("p a (s c) -> p (a s) c", c=256)
        Pc, BPc = osq, Bosq
        Pn, BPn = sq[0], Bsq[0]
        scC, BscC = sc[1], Bsc[1]
        scN, BscN = tmpn[0], Btmpn[0]
        BK2 = Bstg[1]
        k.dma(stg[0][0:64, :], xs[:, :], writes=[Bstg[0]], sem="d_stg0")
        k.dma(stg[0][64:96, :], scs[:, :], writes=[Bstg[0]], sem="d_stg0")
        k.dma(biasS, c_bias_s[:, :, :], writes=[Bc], sem="d_s1")
        for q in range(4):
            k.dma(Vc[:, q * 4:(q + 1) * 4, :], scv[q * 4:(q + 1) * 4].rearrange("s t c -> t s c"), writes=[Bgated], sem="d_s2", q="pool")
        k.op("pool", lambda e: e.memset(gated[:, 0:16, :], 0.0), writes=[Bgated])
        for sq_ in range(16):
            k.dma(S0f[:, sq_, :, :], sgl[sq_].rearrange("(p h) d v -> (h d) p v", h=2), writes=[BmT], sem="d_s3")
        for half in range(2):
            pb = palloc()
            for c4 in range(4):
                c = half * 4 + c4
                k.op("pe", lambda e: e.transpose(out=PS[pb][:, c4 * 64:(c4 + 1) * 64], in_=stg[0][0:64, c * 128:(c + 1) * 128],
                                                 identity=ident[0:64, 0:64]), reads=[Bstg[0], Bc], writes=[BPS[pb]])
            k.op("act", lambda e: e.copy(out=xT[:, half * 4:(half + 1) * 4, 0:64], in_=PS[pb][:, 0:256].rearrange("p (a b) -> p a b", a=4)),
                 reads=[BPS[pb]], writes=[BxT])
            pfree_(pb)
        pb = palloc()
        for c in range(8):
            k.op("pe", lambda e: e.transpose(out=PS[pb][:, c * 32:(c + 1) * 32], in_=stg[0][64:96, c * 128:(c + 1) * 128],
                                             identity=ident[64:96, 64:96]), reads=[Bstg[0], Bc], writes=[BPS[pb]])
        k.op("act", lambda e: e.copy(out=ccs[:, :, :, :].rearrange("p c s r -> p c (s r)"), in_=PS[pb][:, 0:256].rearrange("p (c x) -> p c x", c=8)),
             reads=[BPS[pb]], writes=[Bccs])
        pfree_(pb)
        for hf in range(2):
            k.dma(KcF, sck[hf * 8:(hf + 1) * 8].rearrange("s t c -> t s c"), writes=[Bc], sem="d_s4")
            for s8 in range(8):
                sq_ = hf * 8 + s8
                pb = palloc()
                k.op("pe", lambda e: e.transpose(out=PS[pb][:, 0:128], in_=KcF[:, s8, :], identity=ident[:, :]), reads=[Bc], writes=[BPS[pb]])
                mm(PS[pb][64:128, 128:256], KcF[:, s8, 0:64], ident[:, :], True, True, [Bc], [BPS[pb]])
                mm(PS[pb][0:64, 128:256], KcF[:, s8, 64:128], ident[:, :], True, True, [Bc], [BPS[pb]])
                k.op("act", lambda e: e.copy(out=KcT[:, sq_, :], in_=PS[pb][:, 0:128]), reads=[BPS[pb]], writes=[Bpt[0]])
                k.op("act", lambda e: e.copy(out=KcT2[:, sq_, :], in_=PS[pb][:, 128:256]), reads=[BPS[pb]], writes=[BK2])
                pfree_(pb)
        k.dma(ks_out[:, 0:124, :], sck[:, 4:128, :], sem="d_so")
        k.dma(vs_out[:, 0:124, :], scv[:, 4:128, :], sem="d_so")

        pre_norm(0, 0, N)
        wv, wb = w_get("in0_0")
        for c in range(4):
            fm_chunk(wv, wb, c * 128, 128, N, lambda pb: k.op(
                "act", lambda e: e.copy(out=qaT[:, c, 0:N], in_=PS[pb][:, 0:N]), reads=[BPS[pb]], writes=[BqaT]))
        w_done()
        wv, wb = w_get("in0_1")

        def ka_evac(pb):
            k.op("act", lambda e: e.copy(out=kaT[:, 0:N], in_=PS[pb][:, 0:N]), reads=[BPS[pb]], writes=[BkaT])
            k.op("act", lambda e: e.copy(out=kf32[:, 0:N], in_=PS[pb][:, 0:N]), reads=[BPS[pb]], writes=[Bkf32])
        fm_chunk(wv, wb, 0, 128, N, ka_evac)
        pb = palloc()
        for kc in range(8):
            mm(PS[pb][64:128, 0:N], wv[:, kc, 0:64], hT[:, kc, 0:N], kc == 0, kc == 7, [wb, BhT[kc]], [BPS[pb]], signal=False)
        for kc in range(8):
            mm(PS[pb][0:64, 0:N], wv[:, kc, 64:128], hT[:, kc, 0:N], kc == 0, kc == 7, [wb, BhT[kc]], [BPS[pb]])
        k.op("act", lambda e: e.copy(out=kaT2[:, 0:N], in_=PS[pb][:, 0:N]), reads=[BPS[pb]], writes=[BkaT2])
        pfree_(pb)
        pb = palloc()
        for kc in range(8):
            mm(PS[pb][0:64, 0:128], hT[:, kc, 0:64], wv[:, kc, 128:256], kc == 0, kc == 7, [wb, BhT[kc]], [BPS[pb]])
        k.op("act", lambda e: e.copy(out=vtok[0:64, 1, :], in_=PS[pb][0:64, 0:128]), reads=[BPS[pb]], writes=[Bvtok])
        k.op("act", lambda e: e.copy(out=osm[0:64, 128:256], in_=PS[pb][0:64, 0:128]), reads=[BPS[pb]], writes=[Bosm])
        pfree_(pb)
        for p in range(2):
            fm_chunk(wv, wb, 256 + p * 128, 128, N, lambda pb: k.op(
                "act", lambda e: e.copy(out=qbT[:, p, 0:N], in_=PS[pb][:, 0:N]), reads=[BPS[pb]], writes=[BqbT]))
        w_done()
        wv2, wb2 = w_get("in0_2")
        for p in range(2):
            fm_chunk(wv2, wb2, p * 128, 128, N, lambda pb: k.op(
                "act", lambda e: e.copy(out=kbT[:, p, 0:N], in_=PS[pb][:, 0:N]), reads=[BPS[pb]], writes=[BkbT]))
        wv3, wb3 = w_get_ahead("in0_3", 1)
        wv4, wb4 = w_get_ahead("in0_4", 2)
        for h in range(4):
            src_v, src_b, c0 = (wv3, wb3, 256 + h * 128) if h < 2 else (wv4, wb4, (h - 2) * 128)
            fm_chunk(src_v, src_b, c0, 128, N, lambda pb: k.op(
                "act", lambda e: e.activation(out=srbT[:, h, 0:N], in_=PS[pb][:, 0:N], func=AF.Silu), reads=[BPS[pb]], writes=[BsrbT]))
        fm_chunk(wv4, wb4, 256, 16, N, lambda pb: k.op(
            "act", lambda e: e.copy(out=glrT[0:16, 0:N], in_=PS[pb][0:16, 0:N]), reads=[BPS[pb]], writes=[BglrT]))
        pb = palloc()
        k.op("pe", lambda e: e.transpose(out=PS[pb][0:64, 0:128], in_=kf32[:, 0:64], identity=ident[:, :]), reads=[Bkf32, Bc], writes=[BPS[pb]])
        k.op("act", lambda e: e.copy(out=osm[0:64, 0:128], in_=PS[pb][0:64, 0:128]), reads=[BPS[pb]], writes=[Bosm])
        pfree_(pb)
        for i4 in range(4):
            k.dma(ks_out[:, 124 + i4, :], osm[i4:64:4, 0:128], reads=[Bosm], sem="d_so")
            k.dma(vs_out[:, 124 + i4, :], osm[i4:64:4, 128:256], reads=[Bosm], sem="d_so")

        pg = palloc()
        mm(PS[pg][0:64, 0:256], glrT[0:32, 0:64], wg[0:32, :], True, True, [BglrT, Bwg], [BPS[pg]])
        k.op("act", lambda e: e.activation(out=l_sb[0:64, :], in_=PS[pg][0:64, 0:256], func=AF.Exp, scale=-1.0), reads=[BPS[pg]], writes=[Bl])
        pfree_(pg)
        k.op("act", lambda e: e.activation(out=l_sb[0:64, :], in_=l_sb[0:64, :], func=AF.Ln, bias=epsD[0:64, 2:3]), reads=[Bl, Bc], writes=[Bl])
        pr = palloc()
        mm(PS[pr][0:64, 0:256], urev_s[:, :], l_sb[0:64, :], True, True, [Bc, Bl], [BPS[pr]])
        k.op("act", lambda e: e.activation(out=er_sb[0:64, :], in_=PS[pr][0:64, 0:256], func=AF.Exp), reads=[BPS[pr]], writes=[Ber])
        pfree_(pr)
        pk = palloc()
        for kc in range(8):
            mm(PS[pk][0:64, 0:256], hT[:, kc, 0:64], wv2[:, kc, 0:256], kc == 0, kc == 7, [wb2, BhT[kc]], [BPS[pk]])
        k.op("dve", lambda e: e.tensor_tensor(out=khat[0:64, :], in0=PS[pk][0:64, 0:256], in1=er_sb[0:64, :], op=ALU.mult),
             reads=[BPS[pk], Ber], writes=[Bkhat])
        pfree_(pk)
        pv = palloc()
        for kc in range(8):
            mm(PS[pv][0:64, 0:256], hT[:, kc, 0:64], wv2[:, kc, 256:512], kc == 0, kc == 7, [wb2, BhT[kc]], [BPS[pv]], signal=False)
        for kc in range(8):
            mm(PS[pv][0:64, 256:512], hT[:, kc, 0:64], wv3[:, kc, 0:256], kc == 0, kc == 7, [wb3, BhT[kc]], [BPS[pv]])
        k.op("act", lambda e: e.copy(out=vbtok[0:64, :], in_=PS[pv][0:64, :]), reads=[BPS[pv]], writes=[Bvbtok])
        pfree_(pv)
        w_done(); w_done(); w_done()
        pbk = palloc()
        for p in range(2):
            mm(PS[pbk][:, p * 64:(p + 1) * 64], l_sb[0:64, p * 128:(p + 1) * 128], ucum_s[:, :], True, True, [Bl, Bc], [BPS[pbk]])
        k.op("act", lambda e: e.activation(out=e1_sb[:, 0:128], in_=PS[pbk][:, 0:128], func=AF.Exp), reads=[BPS[pbk]], writes=[Be1])
        k.op("act", lambda e: e.activation(out=e2_sb[:, 0:128], in_=PS[pbk][:, 0:128], func=AF.Exp, scale=-1.0), reads=[BPS[pbk]], writes=[Be2])
        pfree_(pbk)
        k.op("dve", lambda e: e.scalar_tensor_tensor(out=qtl[:, :, 0:64], in0=qbT[:, :, 0:64], scalar=0.125,
                                                     in1=e1_sb[:, 0:128].rearrange("p (a b) -> p a b", a=2), op0=ALU.mult, op1=ALU.mult),
             reads=[BqbT, Be1], writes=[Bqtl])
        k.op("dve", lambda e: e.tensor_tensor(out=ktl[:, :, 0:64], in0=kbT[:, :, 0:64], in1=e2_sb[:, 0:128].rearrange("p (a b) -> p a b", a=2),
                                              op=ALU.mult), reads=[BkbT, Be2], writes=[Bktl])
        for p in range(2):
            for half in range(2):
                hp = half * 64
                k.op("act", lambda e: e.copy(out=S0bz[hp:hp + 64, :, p * 2 + half, :], in_=S0f[hp:hp + 64, :, p, :]), reads=[BmT], writes=[Bgated])
        k.op("dve", lambda e: e.tensor_tensor(out=khm, in0=khat[0:64, :].unsqueeze(1).to_broadcast([64, 16, 256]),
                                              in1=rowmask[:, :].unsqueeze(2).to_broadcast([64, 16, 256]), op=ALU.mult),
             reads=[Bkhat, Bc], writes=[BhT])
        po = palloc()
        for h in range(4):
            p, hp = h // 2, (h % 2) * 64
            pa = palloc()
            mm(PS[pa][0:64, 0:64], ktl[hp:hp + 64, p, 0:64], qtl[hp:hp + 64, p, 0:64], True, True, [Bktl, Bqtl], [BPS[pa]])
            ia = rot("atm", 2)
            k.op("dve", lambda e: e.tensor_tensor(out=atm[ia][0:64, 0:64], in0=PS[pa][0:64, 0:64], in1=maska_s[:, :], op=ALU.mult),
                 reads=[BPS[pa], Bc], writes=[Batm[ia]])
            pfree_(pa)
            mm(PS[po][:, h * 64:(h + 1) * 64], vbtok[0:64, h * 128:(h + 1) * 128], atm[ia][0:64, 0:64], h == 0, False,
               [Bvbtok, Batm[ia]], [BPS[po]], signal=False, skip=True)
        for h in range(4):
            p = h // 2
            for sq_ in range(16):
                mm(PS[po][:, h * 64 + sq_ * 4:h * 64 + sq_ * 4 + 4], S0bz[:, sq_, h, :], qtl[:, p, sq_ * 4:sq_ * 4 + 4], False,
                   (h == 3 and sq_ == 15), [Bgated, Bqtl], [BPS[po]], signal=(sq_ == 15), skip=True)
        for r in range(8):
            pd = palloc()
            for s2 in range(2):
                sq_ = r * 2 + s2
                for h in range(4):
                    p, hp = h // 2, (h % 2) * 64
                    mm(PS[pd][hp:hp + 64, (s2 * 2 + p) * 128:(s2 * 2 + p + 1) * 128], khm[:, sq_, p * 128 + hp:p * 128 + hp + 64],
                       vbtok[0:64, h * 128:(h + 1) * 128], True, True, [BhT, Bvbtok], [BPS[pd]])
            for s2 in range(2):
                sq_ = r * 2 + s2
                for p in range(2):
                    dcol = p * 64 + sq_ * 4 + 3
                    k.op("dve", lambda e: e.scalar_tensor_tensor(
                        out=S0f[:, sq_, p, :], in0=S0f[:, sq_, p, :], scalar=e1_sb[:, dcol:dcol + 1],
                        in1=PS[pd][:, (s2 * 2 + p) * 128:(s2 * 2 + p + 1) * 128], op0=ALU.mult, op1=ALU.add),
                        reads=[BmT, Be1, BPS[pd]], writes=[BmT])
            pfree_(pd)
        for sq_ in range(16):
            k.dma(gs_out[sq_].rearrange("(p h) d v -> (h d) p v", h=2), S0f[:, sq_, :, :], reads=[BmT], sem="d_s6")
        k.op("act", lambda e: e.activation(out=osq[:, 0:256], in_=PS[po][:, 0:256], func=AF.Square), reads=[BPS[po]], writes=[Bosq])
        ps2 = palloc()
        mm(PS[ps2][:, 0:256], ones[:, :], osq[:, 0:256], True, True, [Bc, Bosq], [BPS[ps2]])
        k.op("act", lambda e: e.activation(out=orstd[:, 0:256], in_=PS[ps2][:, 0:256], func=AF.Ln, bias=epsD[:, 1:2]), reads=[BPS[ps2], Bc], writes=[Borstd])
        k.op("act", lambda e: e.activation(out=orstd[:, 0:256], in_=orstd[:, 0:256], func=AF.Exp, scale=-0.5), reads=[Borstd], writes=[Borstd])
        pfree_(ps2)
        for h in range(4):
            col = PC_GN + h
            k.op("dve", lambda e: e.scalar_tensor_tensor(out=otmp[:, h * 64:(h + 1) * 64], in0=PS[po][:, h * 64:(h + 1) * 64], scalar=PT[:, col:col + 1],
                                                         in1=orstd[:, h * 64:(h + 1) * 64], op0=ALU.mult, op1=ALU.mult),
                 reads=[BPS[po], BPT, Borstd], writes=[Botmp])
        pfree_(po)
        k.op("pool", lambda e: e.tensor_tensor(out=mixT[:, 4:8, 0:64], in0=otmp[:, 0:256].rearrange("p (a b) -> p a b", a=4),
                                               in1=srbT[:, :, 0:64], op=ALU.mult), reads=[Botmp, BsrbT], writes=[BmixT])

        scb = [[palloc(), palloc()], [palloc(), palloc()]]
        for par in range(2):
            for kv in range(2):
                ksrc, kB = (KcT, Bpt[0]) if kv == par else (KcT2, BK2)
                nsrc, nB = (kaT, BkaT) if kv == par else (kaT2, BkaT2)
                for gi in range(2):
                    for sq_ in range(16):
                        cc0 = kv * 128 + gi * 64 + sq_ * 4
                        mm(PS[scb[0][par]][:, cc0:cc0 + 4], ksrc[par * 64:(par + 1) * 64, sq_, :],
                           qaT[par * 64:(par + 1) * 64, 2 * kv + gi, sq_ * 4:sq_ * 4 + 4], True, True, [kB, BqaT], [BPS[scb[0][par]]],
                           signal=(sq_ == 15))
                mm(PS[scb[1][par]][0:64, kv * 128:(kv + 1) * 128], nsrc[par * 64:(par + 1) * 64, 0:64],
                   qaT[par * 64:(par + 1) * 64, 2 * kv:2 * kv + 2, 0:64], True, True, [nB, BqaT], [BPS[scb[1][par]]])
        for par in range(2):
            k.op("dve", lambda e: e.tensor_tensor(out=scC[:, par * 256:(par + 1) * 256], in0=PS[scb[0][par]][:, 0:256],
                                                  in1=biasS[:, 0, par * 256:(par + 1) * 256], op=ALU.add), reads=[BPS[scb[0][par]], Bc], writes=[BscC])
            k.op("dve", lambda e: e.tensor_tensor(out=scN[0:64, par * 256:(par + 1) * 256], in0=PS[scb[1][par]][0:64, 0:256],
                                                  in1=biasS[0:64, 1, par * 256:(par + 1) * 256], op=ALU.add), reads=[BPS[scb[1][par]], Bc], writes=[BscN])
            pfree_(scb[0][par]); pfree_(scb[1][par])
        k.op("act", lambda e: e.activation(out=Pc[:, :], in_=scC[:, :], func=AF.Exp, scale=0.125), reads=[BscC], writes=[BPc])
        k.op("act", lambda e: e.activation(out=Pn[0:64, :], in_=scN[0:64, :], func=AF.Exp, scale=0.125), reads=[BscN], writes=[BPn])
        poa = palloc()
        pdn = palloc()
        for par in range(2):
            first = True
            for kv in range(2):
                for gi in range(2):
                    for sq_ in range(16):
                        cc0 = kv * 128 + gi * 64 + sq_ * 4
                        oc0 = (2 * kv + gi) * 64 + sq_ * 4
                        rhs = Pc[:, par * 256 + cc0:par * 256 + cc0 + 4]
                        mm(PS[poa][par * 64:(par + 1) * 64, oc0:oc0 + 4], Vc[:, sq_, kv * 64:(kv + 1) * 64], rhs, first, False,
                           [Bgated, BPc], [BPS[poa]], signal=False, skip=True)
                        mm(PS[pdn][par * 64:(par + 1) * 64, oc0:oc0 + 4], ones[:, 0:64], rhs, first, False,
                           [Bc, BPc], [BPS[pdn]], signal=False, skip=True)
                        first = False
                rhs = Pn[0:64, par * 256 + kv * 128:par * 256 + (kv + 1) * 128]
                mm(PS[poa][par * 64:(par + 1) * 64, 2 * kv * 64:(2 * kv + 2) * 64], vtok[0:64, 1, kv * 64:(kv + 1) * 64], rhs, False, True,
                   [Bvtok, BPn], [BPS[poa]], signal=True, skip=True)
                mm(PS[pdn][par * 64:(par + 1) * 64, 2 * kv * 64:(2 * kv + 2) * 64], ones[0:64, 0:64], rhs, False, True,
                   [Bc, BPn], [BPS[pdn]], signal=True, skip=True)
        k.op("dve", lambda e: e.tensor_tensor(out=rec[:, 0:256].rearrange("p (c q) -> p c q", c=4),
                                              in0=PS[pdn][:, 0:256].rearrange("p (c q) -> p c q", c=4),
                                              in1=sinkE[:, :].unsqueeze(2).to_broadcast([128, 4, 64]), op=ALU.add),
             reads=[BPS[pdn], BsinkE], writes=[Brec])
        pfree_(pdn)
        k.op("dve", lambda e: e.reciprocal(out=rec[:, 0:256], in_=rec[:, 0:256]), reads=[Brec], writes=[Brec])
        k.op("dve", lambda e: e.tensor_tensor(out=mixT[:, 0:4, 0:64], in0=PS[poa][:, 0:256].rearrange("p (c q) -> p c q", c=4),
                                              in1=rec[:, 0:256].rearrange("p (c q) -> p c q", c=4), op=ALU.mult),
             reads=[BPS[poa], Brec], writes=[BmixT])
        pfree_(poa)

        proj_out(["out0_0", "out0_1"], mixT, BmixT, 8, N, 4)
        post_norm_add(1, 0, N)

        def ffn_state_in(l):
            for (c0_, w_) in ((0, 4096), (4096, 1536)):
                stage = mT[0:32, :, :].rearrange("p a b -> p (a b)")
                k.dma(stage[:, 0:w_], sfs[l, :, c0_:c0_ + w_], writes=[BmT], sem="d_s5")
                for q0 in range(0, w_ // 128, 16):
                    n = min(16, w_ // 128 - q0)
                    pb = palloc()
                    for q in range(n):
                        k.op("pe", lambda e: e.transpose(out=PS[pb][:, q * 32:(q + 1) * 32], in_=stage[:, (q0 + q) * 128:(q0 + q + 1) * 128],
                                                         identity=ident[0:32, 0:32]), reads=[BmT, Bc], writes=[BPS[pb]])
                    cb = c0_ // 128 + q0
                    k.op("act", lambda e: e.copy(out=fcs[:, cb:cb + n, :, :].rearrange("p c s r -> p c (s r)"),
                                                 in_=PS[pb][:, 0:n * 32].rearrange("p (c x) -> p c x", c=n)), reads=[BPS[pb]], writes=[Bfcs])
                    pfree_(pb)

        def rows_out(src, srcB, nchunk, dst):
            stage = mT[0:32, :, :].rearrange("p a b -> p (a b)")
            done = 0
            while done < nchunk:
                nn = min(32, nchunk - done)
                for q0 in range(0, nn, 4):
                    n = min(4, nn - q0)
                    pb = palloc()
                    for q in range(n):
                        cidx = done + q0 + q
                        k.op("pe", lambda e: e.transpose(out=PS[pb][0:32, q * 128:(q + 1) * 128],
                                                         in_=src[:, cidx, :, :].rearrange("p s r -> p (s r)"), identity=ident[:, :]),
                             reads=[srcB, Bc], writes=[BPS[pb]])
                    k.op("act", lambda e: e.copy(out=stage[:, q0 * 128:(q0 + n) * 128], in_=PS[pb][0:32, 0:n * 128]), reads=[BPS[pb]], writes=[BmT])
                    pfree_(pb)
                k.dma(dst[:, done * 128:(done + nn) * 128], stage[:, 0:nn * 128], reads=[BmT], sem="d_rows")
                done += nn

        pre_norm(2, 0, N)
        ffn_state_in(0)
        ffn(0, N, seg=True)
        post_norm_add(3, 0, N)
        rows_out(fcs, Bfcs, 44, fs_out[0])
        pre_norm(0, 1, N)
        odd_mixer(N, False, seg=True)
        proj_out(["out1_0", "out1_1"], mixT, BmixT, 8, N, 4)
        post_norm_add(1, 1, N)
        rows_out(ccs, Bccs, 8, cs_out)
        pre_norm(2, 1, N)
        ffn_state_in(1)
        ffn(1, N, seg=True)
        post_norm_add(3, 1, N)
        rows_out(fcs, Bfcs, 44, fs_out[1])
        for half in range(2):
            pb = palloc()
            for c4 in range(4):
                c = half * 4 + c4
                k.op("pe", lambda e: e.transpose(out=PS[pb][0:64, c4 * 128:(c4 + 1) * 128], in_=xT[:, c, 0:64], identity=ident[:, :]),
                     reads=[BxT, Bc], writes=[BPS[pb]])
            k.op("act", lambda e: e.copy(out=stg[0][0:64, half * 512:(half + 1) * 512], in_=PS[pb][0:64, :]), reads=[BPS[pb]], writes=[Bstg[0]])
            pfree_(pb)
        k.dma(ys_out[:, :], stg[0][0:64, :], reads=[Bstg[0]], sem="d_stg0")

    def finish():
        for s_, v in k.cnt.items():
            if v and s_ != "sp":
                k.E["sp"].wait_ge(k.sems[s_], v)

    import os
    CUT = int(os.environ.get("KCUT", "99"))
    KSUB = int(os.environ.get("KSUB", "0"))

    class _Stop(Exception):
        pass

    cur = dict(ui=-1)

    def ck(n):
        if KSUB == n and cur["ui"] == CUT - 1:
            raise _Stop()
    load_resident()
    for ui, (mode, ts) in enumerate(units):
        if ui >= CUT:
            break
        cur["ui"] = ui
        nt = len(ts)
        N = nt * 128
        if mode == "pre":
            try:
                load_x(ts, N)
                ck(1)
                pre_norm(0, 0, N)
                ck(2)
                even_mixer("pre", ts, N, [-1] * nt)
            except _Stop:
                break
            continue
        try:
            load_x([NPRE + t for t in ts], N)
            pre_norm(0, 0, N)
            even_mixer("main", ts, N, ts)
            ck(13)
            proj_out(["out0_0", "out0_1"], mixT, BmixT, 8, N, 4)
            ck(141)
            post_norm_add(1, 0, N)
            ck(14)
            pre_norm(2, 0, N)
            ffn(0, N)
            ck(15)
            post_norm_add(3, 0, N)
            pre_norm(0, 1, N)
            ck(16)
            odd_mixer(N, ts[-1] == NMAIN - 1)
            ck(17)
            proj_out(["out1_0", "out1_1"], mixT, BmixT, 8, N, 4)
            post_norm_add(1, 1, N)
            pre_norm(2, 1, N)
            ffn(1, N)
            post_norm_add(3, 1, N)
            ck(18)
            store_y([(ti, t - 1) for ti, t in enumerate(ts) if t >= 1], N)
        except _Stop:
            break
        if ts == [0]:
            k.op("dve", lambda e: e.tensor_scalar(out=fcar[:, :, :, :], in0=fcar[:, :, :, :], scalar1=flg[:, 0:1], scalar2=None,
                                                  op0=ALU.mult), reads=[Bfcar, Bc], writes=[Bfcar])
            k.op("dve", lambda e: e.tensor_scalar(out=ccar[:, :, :], in0=ccar[:, :, :], scalar1=flg[:, 0:1], scalar2=None,
                                                  op0=ALU.mult), reads=[Bccar, Bc], writes=[Bccar])
        if ts[-1] == NMAIN - 1:
            state_rows_out([(ccar[:, c, :], Bccar) for c in range(8)], 8, None, 0)
            k.dma(c_out[:, :], rowst[:, 0:1024], reads=[BmT], sem="d_rows")
            for l in range(2):
                for part in range(4):
                    state_rows_out([(fcar[:, l, part * 11 + c, :], Bfcar) for c in range(11)], 11, None, 0)
                    k.dma(f_out[l, :, part * 1408:(part + 1) * 1408], rowst[:, 0:1408], reads=[BmT], sem="d_rows")

    if with_sample and CUT > len(units):
        sample_unit()
    finish()
    return nc, k


def host_consts():
    c = {}
    c["c_ident"] = np.eye(128, dtype=np.float32)
    c["c_ones"] = np.ones((128, 128), dtype=ml_dtypes.bfloat16)
    j = np.arange(128)[:, None]
    i = np.arange(128)[None, :]
    same = (j // 64) == (i // 64)
    c["c_ucum"] = np.where(same & (j <= i), -1.0 / 16.0, 0.0).astype(np.float32)
    c["c_urev"] = np.where(same & (j > i), -1.0 / 16.0, 0.0).astype(np.float32)
    c["c_maska"] = np.where(same & (j <= i), 1.0, 0.0).astype(np.float32)
    slopes = 2.0 ** (-8.0 * np.arange(1, 9) / 8.0)
    bias = np.zeros((128, 4, 4, 128), np.float32)
    s = np.arange(128)[:, None]
    q = np.arange(128)[None, :]
    for kb_ in range(2):
        dist = (128 + q - s) if kb_ == 0 else (q - s)
        valid = (dist >= 0) & (dist <= 128)
        for kv in range(2):
            for par in range(2):
                for gi in range(2):
                    b = -slopes[kv * 4 + par + 2 * gi] * dist * 8.0
                    bias[:, kv * 2 + par, kb_ * 2 + gi, :] = np.where(valid, b, NEG)
    c["c_bias"] = bias.reshape(128, 4, 512)
    j = np.arange(64)[:, None]
    i = np.arange(64)[None, :]
    same = (j // 4) == (i // 4)
    c["c_ucum_s"] = np.where(same & (j <= i), -1.0 / 16.0, 0.0).astype(np.float32)
    c["c_urev_s"] = np.where(same & (j > i), -1.0 / 16.0, 0.0).astype(np.float32)
    c["c_maska_s"] = np.where(same & (j <= i), 1.0, 0.0).astype(np.float32)
    c["c_rowmask"] = ((np.arange(64)[:, None] // 4) == np.arange(16)[None, :]).astype(np.float32)
    bs = np.full((128, 2, 2, 2, 2, 64), NEG, np.float32)
    tok = np.arange(64)
    ti = tok % 4
    srow = np.arange(128)[:, None]
    for par in range(2):
        for kv in range(2):
            for gi in range(2):
                sl = slopes[kv * 4 + par + 2 * gi]
                dist = 128 + ti[None, :] - srow
                bs[:, 0, par, kv, gi, :] = np.where(srow >= ti[None, :], -sl * dist * 8.0, NEG)
                jj = np.arange(64)[:, None]
                d2 = ti[None, :] - (jj % 4)
                ok = ((jj // 4) == (tok[None, :] // 4)) & (d2 >= 0)
                bs[0:64, 1, par, kv, gi, :] = np.where(ok, -sl * d2 * 8.0, NEG)
    c["c_bias_s"] = bs.reshape(128, 2, 512)
    return c


def host_pvec(inp):
    rows = np.zeros((512, 128), np.float32)
    for q, name in enumerate(("norm_mix_pre", "norm_mix_post", "norm_ffn_pre", "norm_ffn_post")):
        a = np.asarray(inp[name], np.float32)
        for l in range(2):
            rows[(q * 2 + l) * 8:(q * 2 + l) * 8 + 8] = a[l].reshape(8, 128)
    fw = np.asarray(inp["ffn_conv_w"], np.float32)
    for l in range(2):
        for i in range(3):
            rows[PC_FW + (l * 3 + i) * 44:PC_FW + (l * 3 + i) * 44 + 44] = fw[l, i].reshape(44, 128)
    fb = np.asarray(inp["ffn_conv_b"], np.float32)
    for l in range(2):
        rows[PC_FB + l * 44:PC_FB + l * 44 + 44] = fb[l].reshape(44, 128)
    cw = np.asarray(inp["conv_w_odd"], np.float32)
    for i in range(3):
        rows[PC_CW + i * 8:PC_CW + i * 8 + 8] = cw[0, i].reshape(8, 128)
    rows[PC_GN:PC_GN + 4] = np.asarray(inp["gla_norm"], np.float32)[0].reshape(4, 128)
    return rows


_CACHE = {}


def make_in_maps(inp):
    consts = host_consts()
    pvec = host_pvec(inp)
    xp = np.asarray(inp["x_prompt"], np.float32)
    shared = dict(consts)
    shared["pvec"] = pvec
    shared["w_in_even"] = np.ascontiguousarray(np.asarray(inp["w_in_even"], np.float32)[0])
    shared["w_out_even"] = np.ascontiguousarray(np.asarray(inp["w_out_even"], np.float32)[0])
    shared["w_in_odd"] = np.ascontiguousarray(np.asarray(inp["w_in_odd"], np.float32)[0])
    shared["w_out_odd"] = np.ascontiguousarray(np.asarray(inp["w_out_odd"], np.float32)[0])
    for l in range(2):
        shared["ffn_up%d" % l] = np.ascontiguousarray(np.asarray(inp["ffn_up"], np.float32)[l])
        shared["ffn_down%d" % l] = np.ascontiguousarray(np.asarray(inp["ffn_down"], np.float32)[l])
    shared["w_gate_up"] = np.ascontiguousarray(np.asarray(inp["w_gate_up"], np.float32)[0])
    shared["b_gate"] = np.ascontiguousarray(np.asarray(inp["b_gate"], np.float32))
    shared["attn_sinks"] = np.ascontiguousarray(np.asarray(inp["attn_sinks"], np.float32))
    in_maps = []
    for c in range(8):
        b, half = c // 2, c % 2
        xin = np.zeros(((NPRE + NMAIN) * 128, D), np.float32)
        if half == 1:
            xin[:] = xp[b]
        else:
            xin[(NPRE + 1) * 128:] = xp[b, 0:2048]
        m = dict(shared)
        m["xin"] = xin
        fl = np.zeros((128, 2), np.float32)
        fl[:, 0] = float(half)
        fl[:, 1] = (float(half) - 1.0) * (-NEG)
        m["flagv"] = fl
        sl = slice(c * NSEQ, (c + 1) * NSEQ)
        m["xs"] = np.ascontiguousarray(np.asarray(inp["x_sample"], np.float32)[sl].reshape(64, D))
        m["sck"] = np.ascontiguousarray(np.asarray(inp["cache_swa_k"], np.float32)[0, sl].reshape(16, 128, 128))
        m["scv"] = np.ascontiguousarray(np.asarray(inp["cache_swa_v"], np.float32)[0, sl].reshape(16, 128, 128))
        m["sgl"] = np.ascontiguousarray(np.asarray(inp["state_gla"], np.float32)[0, sl])
        m["scs"] = np.ascontiguousarray(np.asarray(inp["state_conv"], np.float32)[0, sl].reshape(32, D))
        m["sfs"] = np.ascontiguousarray(np.asarray(inp["state_ffn"], np.float32)[:, sl].reshape(2, 32, F2))
        in_maps.append(m)
    return in_maps


def kernel(**inp):
    if "nc" not in _CACHE:
        _CACHE["nc"] = build_program()[0]
    nc = _CACHE["nc"]
    in_maps = make_in_maps(inp)
    import os
    if os.environ.get("KCORES"):
        lc = [int(x) for x in os.environ["KCORES"].split(",")]
        res = run_bass_kernel_spmd(nc, [in_maps[c] for c in lc], core_ids=list(range(len(lc))))
        R = [None] * 8
        for i, c in enumerate(lc):
            R[c] = res.results[i]
        for c in range(8):
            if R[c] is None:
                R[c] = {kk: np.zeros_like(vv) for kk, vv in res.results[0].items()}
    else:
        res = run_bass_kernel_spmd(nc, in_maps, core_ids=list(range(8)))
        R = res.results
    y_prompt = np.zeros((4, 4096, D), np.float32)
    swa_k = np.zeros((1, 4, 128, 2, 64), np.float32)
    swa_v = np.zeros((1, 4, 128, 2, 64), np.float32)
    gla = np.zeros((1, 4, 4, 64, 128), np.float32)
    conv = np.zeros((1, 4, 2, D), np.float32)
    ffn_s = np.zeros((2, 4, 2, F2), np.float32)
    for c in range(8):
        b, half = c // 2, c % 2
        y_prompt[b, half * 2048:(half + 1) * 2048] = R[c]["y_out"]
        if half == 1:
            swa_k[0, b] = R[c]["k_out"].reshape(128, 2, 64)
            swa_v[0, b] = R[c]["v_out"].reshape(128, 2, 64)
            gla[0, b] = R[c]["g_out"]
            conv[0, b] = R[c]["c_out"]
            ffn_s[:, b] = R[c]["f_out"]
    y_s = np.zeros((128, 4, D), np.float32)
    ks_s = np.zeros((1, 128, 128, 2, 64), np.float32)
    vs_s = np.zeros((1, 128, 128, 2, 64), np.float32)
    gs_s = np.zeros((1, 128, 4, 64, 128), np.float32)
    cs_s = np.zeros((1, 128, 2, D), np.float32)
    fs_s = np.zeros((2, 128, 2, F2), np.float32)
    for c in range(8):
        sl = slice(c * NSEQ, (c + 1) * NSEQ)
        y_s[sl] = R[c]["ys_out"].reshape(16, 4, D)
        ks_s[0, sl] = R[c]["ks_out"].reshape(16, 128, 2, 64)
        vs_s[0, sl] = R[c]["vs_out"].reshape(16, 128, 2, 64)
        gs_s[0, sl] = R[c]["gs_out"]
        cs_s[0, sl] = R[c]["cs_out"].reshape(16, 2, D)
        fs_s[:, sl] = R[c]["fs_out"].reshape(2, 16, 2, F2)
    return (y_prompt, y_s, swa_k, swa_v, gla, conv, ffn_s, ks_s, vs_s, gs_s, cs_s, fs_s)
```
